# Optimizing a Trainium2 kernel written in Bass

```python
import jax, jax.numpy as jnp
from jax import lax
import numpy as np

D_MODEL = 1024
BATCH = 32
SEQ = 2048
DEPTH = 4
DEC_BATCH = 4
DEC_SEQ = 8192
PAST_LEN = 128

N_MIXERS = 2
N_GLA = (DEPTH + 1) // 2
N_SG = DEPTH // 2
GLA_HEADS = 4
GLA_DK = D_MODEL // 2
GLA_DV = D_MODEL
GLA_HK = GLA_DK // GLA_HEADS
GLA_HV = GLA_DV // GLA_HEADS
GLA_RANK = 16
GLA_TAU = 16.0
GLA_CHUNK = 64
SG_CHUNK = 128
SG_GROUPS = 4
SG_WIDTH = D_MODEL
SG_GD = SG_WIDTH // SG_GROUPS
FFN_HIDDEN = -(-8 * D_MODEL // 768) * 256
EPS = 1e-6

kernel_name = 'hybrid_gla_sgu_adaln_encoder'


def rms_norm(x, g):
    xf = x.astype(jnp.float32)
    y = xf * lax.rsqrt(jnp.mean(xf * xf, axis=-1, keepdims=True) + EPS)
    return (y * g.astype(jnp.float32)).astype(x.dtype)


def layer_norm(x, g, b):
    xf = x.astype(jnp.float32)
    mu = jnp.mean(xf, axis=-1, keepdims=True)
    xc = xf - mu
    y = xc * lax.rsqrt(jnp.mean(xc * xc, axis=-1, keepdims=True) + EPS)
    return (y * g.astype(jnp.float32) + b.astype(jnp.float32)).astype(x.dtype)


def gla_direction(q, k, v, log_a, include_diag):
    B, H, L, dk = q.shape
    dv = v.shape[-1]
    C = GLA_CHUNK
    n = L // C
    q = q.reshape(B, H, n, C, dk)
    k = k.reshape(B, H, n, C, dk)
    v = v.reshape(B, H, n, C, dv)
    b = jnp.cumsum(log_a.reshape(B, H, n, C, dk), axis=3)
    b_last = b[:, :, :, -1:, :]
    q_d = q * jnp.exp(b)
    k_d = k * jnp.exp(-b)
    k_end = k * jnp.exp(b_last - b)
    scores = jnp.einsum('bhncd,bhnsd->bhncs', q_d, k_d)
    mask = jnp.tril(jnp.ones((C, C), dtype=bool), 0 if include_diag else -1)
    scores = jnp.where(mask, scores, 0.0)
    o_intra = jnp.einsum('bhncs,bhnsv->bhncv', scores, v)
    decay = jnp.exp(b_last[:, :, :, 0, :])

    def step(S, inp):
        qd, ke, vc, dec = inp
        o = jnp.einsum('bhcd,bhdv->bhcv', qd, S)
        S = S * dec[..., None] + jnp.einsum('bhcd,bhcv->bhdv', ke, vc)
        return S, o

    xs = (jnp.moveaxis(q_d, 2, 0), jnp.moveaxis(k_end, 2, 0),
          jnp.moveaxis(v, 2, 0), jnp.moveaxis(decay, 2, 0))
    S0 = jnp.zeros((B, H, dk, dv), jnp.float32)
    _, o_inter = lax.scan(step, S0, xs)
    o = o_intra + jnp.moveaxis(o_inter, 0, 2)
    return o.reshape(B, H, L, dv)


def gla_mixer(h, w_in, w_gk1, w_gk2, b_gk, g_head, w_out):
    B, L, _ = h.shape
    proj = h @ w_in
    q, k, v, r = jnp.split(proj, [GLA_DK, 2 * GLA_DK, 2 * GLA_DK + GLA_DV], axis=-1)

    def heads(t, hd):
        return t.reshape(B, L, GLA_HEADS, hd).transpose(0, 2, 1, 3).astype(jnp.float32)

    q = heads(q, GLA_HK) * (GLA_HK ** -0.5)
    k = heads(k, GLA_HK)
    v = heads(v, GLA_HV)
    gate = jnp.einsum('bld,edr->eblr', h, w_gk1)
    gate = jnp.einsum('eblr,erk->eblk', gate, w_gk2) + b_gk[:, None, None, :]
    log_a = jax.nn.log_sigmoid(gate.astype(jnp.float32)) / GLA_TAU
    la_f = heads(log_a[0], GLA_HK)
    la_b = heads(log_a[1], GLA_HK)
    o_f = gla_direction(q, k, v, la_f, True)
    flip = lambda t: jnp.flip(t, axis=2)
    o_b = flip(gla_direction(flip(q), flip(k), flip(v), flip(la_b), False))
    o = rms_norm(o_f + o_b, g_head)
    o = o.transpose(0, 2, 1, 3).reshape(B, L, GLA_DV).astype(h.dtype)
    return (o * jax.nn.silu(r)) @ w_out


def sgu_mixer(h, w_in, b_in, ln_g, ln_b, w_s, b_s, w_out):
    B, L, _ = h.shape
    z = jax.nn.gelu(h @ w_in + b_in)
    u, v = jnp.split(z, 2, axis=-1)
    v = layer_norm(v, ln_g, ln_b)
    n = L // SG_CHUNK
    v = v.reshape(B, n, SG_CHUNK, SG_GROUPS, SG_GD)
    v = jnp.einsum('gts,bnsgd->bntgd', w_s, v) + b_s.T[None, None, :, :, None]
    v = v.reshape(B, L, SG_WIDTH)
    return (u * v) @ w_out


def trunk(x, c, norm_g, w_ada, b_ada,
          gla_w_in, gla_w_gk1, gla_w_gk2, gla_b_gk, gla_g_head, gla_w_out,
          sg_w_in, sg_b_in, sg_ln_g, sg_ln_b, sg_w_s, sg_b_s, sg_w_out,
          ffn_w_in, ffn_w_out):
    for i in range(DEPTH):
        mod = jax.nn.silu(c) @ w_ada[i] + b_ada[i]
        sh1, sc1, g1, sh2, sc2, g2 = jnp.split(mod[:, None, :], 6, axis=-1)
        h = rms_norm(x, norm_g[i, 0]) * (1 + sc1) + sh1
        j = i // N_MIXERS
        if i % N_MIXERS == 0:
            y = gla_mixer(h, gla_w_in[j], gla_w_gk1[j], gla_w_gk2[j], gla_b_gk[j],
                          gla_g_head[j], gla_w_out[j])
        else:
            y = sgu_mixer(h, sg_w_in[j], sg_b_in[j], sg_ln_g[j], sg_ln_b[j],
                          sg_w_s[j], sg_b_s[j], sg_w_out[j])
        x = x + g1 * rms_norm(y, norm_g[i, 1])
        h = rms_norm(x, norm_g[i, 2]) * (1 + sc2) + sh2
        a, bb = jnp.split(h @ ffn_w_in[i], 2, axis=-1)
        y = (jax.nn.silu(a) * bb) @ ffn_w_out[i]
        x = x + g2 * rms_norm(y, norm_g[i, 3])
    return x


def setup_inputs(seed: int = 0) -> dict:
    key = jax.random.key(seed)
    ks = jax.random.split(key, 24)
    D = D_MODEL
    nrm = lambda k, shape, s: jax.random.normal(k, shape, jnp.float32) * s
    return {
        'x_prompt': nrm(ks[0], (BATCH, SEQ, D), 1.0),
        'x_sample': nrm(ks[1], (DEC_BATCH, DEC_SEQ, D), 1.0),
        'c_prompt': nrm(ks[2], (BATCH, D), 1.0),
        'c_sample': nrm(ks[3], (DEC_BATCH, D), 1.0),
        'norm_g': 1.0 + nrm(ks[4], (DEPTH, 4, D), 0.05),
        'w_ada': nrm(ks[5], (DEPTH, D, 6 * D), 0.5 * D ** -0.5),
        'b_ada': nrm(ks[6], (DEPTH, 6 * D), 0.02),
        'gla_w_in': nrm(ks[7], (N_GLA, D, 2 * GLA_DK + 2 * GLA_DV), D ** -0.5),
        'gla_w_gk1': nrm(ks[8], (N_GLA, 2, D, GLA_RANK), D ** -0.5),
        'gla_w_gk2': nrm(ks[9], (N_GLA, 2, GLA_RANK, GLA_DK), GLA_RANK ** -0.5),
        'gla_b_gk': nrm(ks[10], (N_GLA, 2, GLA_DK), 0.1),
        'gla_g_head': 1.0 + nrm(ks[11], (N_GLA, GLA_HV), 0.05),
        'gla_w_out': nrm(ks[12], (N_GLA, GLA_DV, D), GLA_DV ** -0.5),
        'sg_w_in': nrm(ks[13], (N_SG, D, 2 * SG_WIDTH), D ** -0.5),
        'sg_b_in': nrm(ks[14], (N_SG, 2 * SG_WIDTH), 0.02),
        'sg_ln_g': 1.0 + nrm(ks[15], (N_SG, SG_WIDTH), 0.05),
        'sg_ln_b': nrm(ks[16], (N_SG, SG_WIDTH), 0.02),
        'sg_w_s': nrm(ks[17], (N_SG, SG_GROUPS, SG_CHUNK, SG_CHUNK), SG_CHUNK ** -0.5),
        'sg_b_s': 1.0 + nrm(ks[18], (N_SG, SG_GROUPS, SG_CHUNK), 0.1),
        'sg_w_out': nrm(ks[19], (N_SG, SG_WIDTH, D), SG_WIDTH ** -0.5),
        'ffn_w_in': nrm(ks[20], (DEPTH, D, 2 * FFN_HIDDEN), D ** -0.5),
        'ffn_w_out': nrm(ks[21], (DEPTH, FFN_HIDDEN, D), FFN_HIDDEN ** -0.5),
    }


def reference(x_prompt, x_sample, c_prompt, c_sample, norm_g, w_ada, b_ada,
              gla_w_in, gla_w_gk1, gla_w_gk2, gla_b_gk, gla_g_head, gla_w_out,
              sg_w_in, sg_b_in, sg_ln_g, sg_ln_b, sg_w_s, sg_b_s, sg_w_out,
              ffn_w_in, ffn_w_out):
    y_prompt = trunk(x_prompt, c_prompt, norm_g, w_ada, b_ada,
                     gla_w_in, gla_w_gk1, gla_w_gk2, gla_b_gk, gla_g_head, gla_w_out,
                     sg_w_in, sg_b_in, sg_ln_g, sg_ln_b, sg_w_s, sg_b_s, sg_w_out,
                     ffn_w_in, ffn_w_out)
    y_sample = trunk(x_sample, c_sample, norm_g, w_ada, b_ada,
                     gla_w_in, gla_w_gk1, gla_w_gk2, gla_b_gk, gla_g_head, gla_w_out,
                     sg_w_in, sg_b_in, sg_ln_g, sg_ln_b, sg_w_s, sg_b_s, sg_w_out,
                     ffn_w_in, ffn_w_out)
    return (y_prompt, y_sample)
```

```python
import os
import contextlib
import numpy as np
import concourse.bass as bass
import concourse.mybir as mybir
from concourse.bass_utils import run_bass_kernel_spmd

F32 = mybir.dt.float32
BF16 = mybir.dt.bfloat16
AF = mybir.ActivationFunctionType
ALU = mybir.AluOpType

D = 1024
NCORES = 8
TOK = 12288
TT = 256
NT = TOK // TT
TPG = 8
NG = NT // TPG
DEPTH = 4
FH = 2816
EPS = 1e-6
DEPTH_RUN = int(os.environ.get("MK_DEPTH", "4"))
PH_LIMIT = int(os.environ.get("MK_PHASES", "99"))


def _fsz(ap):
    n = 1
    for d in list(ap.shape)[1:]:
        n *= int(d)
    return n


SCHED_WINDOW = int(os.environ.get("MK_WINDOW", "96"))
VERB = bool(os.environ.get("MK_VERBOSE"))
STALL = {}


class Prog:
    ENGS = ("pe", "act", "dve", "pool", "sp")

    def __init__(self, nc, stack):
        self.nc = nc
        self.stack = stack
        self.esem = {e: stack.enter_context(nc.semaphore(f"es_{e}"))
                     for e in ("pe", "act", "dve", "pool")}
        self.ecnt = {e: 0 for e in self.esem}
        self.dsem = {}
        self.allsems = list(self.esem.values())
        self.waited = {e: {} for e in self.ENGS}
        self.semobj = {}
        for e, s in self.esem.items():
            self.semobj[id(s)] = s
        self.frozen = False
        self._reset()

    def _reset(self):
        self.ops = []
        self.lastw = {}
        self.readers = {}
        self.last_stream_op = {}

    def stream(self, name):
        if name not in self.dsem:
            assert not self.frozen, name
            s = self.stack.enter_context(self.nc.semaphore(f"ds_{name}"))
            self.dsem[name] = [s, 0]
            self.allsems.append(s)
            self.semobj[id(s)] = s
        return self.dsem[name]

    def _record(self, e, fn, reads, writes, kind, est, stream=None, ndma=0, comp=None):
        idx = len(self.ops)
        deps = {}
        for k in reads:
            w = self.lastw.get(k)
            if w is not None:
                deps[w] = True
            if isinstance(k, str) and k.startswith(("pg", "pab", "pz", "ps")):
                for r in self.readers.get(k, ()):
                    if self.ops[r]["e"] != e:
                        deps.setdefault(r, False)
        for k in writes:
            w = self.lastw.get(k)
            if w is not None:
                deps.setdefault(w, False)
            for r in self.readers.get(k, ()):
                deps.setdefault(r, False)
        order = []
        if stream is not None:
            p = self.last_stream_op.get(stream)
            if p is not None:
                order.append(p)
            self.last_stream_op[stream] = idx
        self.ops.append(dict(e=e, fn=fn, deps=deps, order=order, kind=kind, est=est,
                             stream=stream, ndma=ndma, comp=comp if comp is not None else est,
                             tag="%s>%s" % (",".join(str(k) for k in reads)[:40], ",".join(str(k) for k in writes)[:30])))
        for k in reads:
            self.readers.setdefault(k, set()).add(idx)
        for k in writes:
            self.lastw[k] = idx
            self.readers[k] = set()
        return idx

    def op(self, e, fn, reads=(), writes=(), est=300.0):
        return self._record(e, fn, reads, writes, "c", est)

    def dma(self, qe, stream, pairs, reads=(), writes=(), **kw):
        self.stream(stream)
        nbytes = 0
        for o, i in pairs:
            try:
                nbytes += int(o.nbytes)
            except Exception:
                nbytes += 4 * _fsz(o) * int(o.shape[0])

        def fn(eng, pairs=pairs, kw=kw):
            return [eng.dma_start(out=o, in_=i, **kw) for (o, i) in pairs]
        return self._record(qe, fn, reads, writes, "d", 60.0 * len(pairs), stream=stream, ndma=len(pairs),
                            comp=2500.0 + nbytes / 120.0)

    def fence(self, stream, keys):
        p = self.last_stream_op[stream]
        for k in keys:
            self.lastw[k] = p
            self.readers[k] = set()

    def _schedule(self):
        ops = self.ops
        n = len(ops)
        succ = [[] for _ in range(n)]
        nun = [0] * n
        ready = [0.0] * n
        start = [0.0] * n
        finish = [0.0] * n
        for i, o in enumerate(ops):
            ds = set(o["deps"].keys()) | set(o["order"])
            nun[i] = len(ds)
            for d in ds:
                succ[d].append(i)
        pend = {e: [i for i in range(n) if ops[i]["e"] == e] for e in self.ENGS}
        free = {e: 0.0 for e in self.ENGS}
        sched = {e: [] for e in self.ENGS}
        left = n
        W = SCHED_WINDOW
        while left:
            best = None
            for e in self.ENGS:
                pl = pend[e]
                fe = free[e]
                for pos in range(min(W, len(pl))):
                    i = pl[pos]
                    if nun[i]:
                        continue
                    st = ready[i] if ready[i] > fe else fe
                    if best is None or st < best[0] or (st == best[0] and i < best[1]):
                        best = (st, i, e, pos)
                    if st <= fe:
                        break
            assert best is not None, "scheduler deadlock"
            st, i, e, pos = best
            pend[e].pop(pos)
            o = ops[i]
            if VERB and st > free[e] + 1.0 and "blk" in o:
                key = (e, "", ops[o["blk"]]["e"], ops[o["blk"]].get("tag", "?"))
                STALL[key] = STALL.get(key, 0.0) + (st - free[e])
            start[i] = st
            free[e] = st + o["est"]
            finish[i] = st + o["comp"]
            sched[e].append(i)
            left -= 1
            for s in succ[i]:
                nun[s] -= 1
                so = ops[s]
                raw = so["deps"].get(i)
                if raw is None:
                    t = start[i]
                elif o["kind"] == "c" and so["kind"] == "c" and so["e"] == e and (e == "pe" or not raw):
                    t = free[e]
                else:
                    t = finish[i] + 60.0
                if t > ready[s]:
                    ready[s] = t
                    so["blk"] = i
        if VERB:
            top = sorted([kv for kv in STALL.items() if kv[0][0] == "pe"], key=lambda kv: -kv[1])[:12]
            for k, v in top:
                print("   stall %.0f us: %s" % (v / 1e3, k))
            STALL.clear()
            busy = {e: sum(ops[i]["est"] for i in sched[e]) / 1e3 for e in self.ENGS}
            print("block: n=%d makespan=%.1f us busy(us)=%s" % (n, max(finish) / 1e3 if n else 0.0,
                  {e: round(v) for e, v in busy.items()}), flush=True)
        return sched

    def flush(self):
        nc = self.nc
        ops = self.ops
        sched = self._schedule()
        tok = {}
        for e in self.ENGS:
            for i in sched[e]:
                o = ops[i]
                if o["kind"] == "x":
                    o["inc"] = None
                elif o["kind"] == "c":
                    self.ecnt[e] += 1
                    tok[i] = (id(self.esem[e]), self.ecnt[e])
                    o["inc"] = (self.esem[e], 1)
                else:
                    st = self.dsem[o["stream"]]
                    st[1] += 16 * o["ndma"]
                    tok[i] = (id(st[0]), st[1])
                    o["inc"] = (st[0], 16)
        qs = {e: [] for e in self.ENGS}
        for e in self.ENGS:
            w = self.waited[e]
            for i in sched[e]:
                o = ops[i]
                need = {}
                for d, raw in o["deps"].items():
                    od = ops[d]
                    if o["kind"] == "c" and od["kind"] == "c" and od["e"] == e and (e == "pe" or not raw):
                        continue
                    sid, val = tok[d]
                    if need.get(sid, 0) < val:
                        need[sid] = val
                waits = []
                for sid, val in need.items():
                    if w.get(sid, 0) < val:
                        w[sid] = val
                        waits.append((self.semobj[sid], val))
                qs[e].append((waits, o["fn"], o["inc"]))
        fin = []
        for name, (s, cnt) in self.dsem.items():
            if cnt > 0 and self.waited["sp"].get(id(s), 0) < cnt:
                self.waited["sp"][id(s)] = cnt
                fin.append((s, cnt))
        if fin:
            qs["sp"].append((fin, (lambda eng: None), None))
        with nc.Block() as block:
            names = {"pe": "tensor", "act": "scalar", "dve": "vector", "pool": "gpsimd", "sp": "sync"}
            for e in self.ENGS:
                if not qs[e]:
                    continue

                def body(eng, lst=qs[e]):
                    for waits, fn, inc in lst:
                        for s, v in waits:
                            eng.wait_ge(s, v)
                        r = fn(eng)
                        if inc is not None and r is not None:
                            if isinstance(r, list):
                                for ins in r:
                                    ins.then_inc(inc[0], inc[1])
                            else:
                                r.then_inc(inc[0], inc[1])
                getattr(block, names[e])(body)
        self._reset()
        for e in self.ENGS:
            w = self.waited[e]
            for ee, s in self.esem.items():
                w[id(s)] = self.ecnt[ee]
            for name, (s, cnt) in self.dsem.items():
                w[id(s)] = cnt

    def raw_sp(self, fn):
        self.ops.append(dict(e="sp", fn=fn, deps={}, order=[], kind="x", est=50.0, stream=None, ndma=0, comp=50.0))

    PE_NS = 0.513

    def mm(self, out, pairs, reads, writes):
        return self.mmg([(out, pairs)], reads, writes)

    def mmg(self, groups, reads, writes):
        est = 0.0
        for out, pairs in groups:
            for l, rh in pairs:
                est += max(_fsz(rh), 64) * self.PE_NS + 2.0

        def fn(eng, groups=groups):
            r = None
            for out, pairs in groups:
                n = len(pairs)
                for i, (l, rh) in enumerate(pairs):
                    r = eng.matmul(out, lhsT=l, rhs=rh, start=(i == 0), stop=(i == n - 1))
            return r
        return self.op("pe", fn, reads, writes, est=est)

    def tr(self, groups, reads, writes):
        def fn(eng, groups=groups):
            r = None
            for out, in_, ident in groups:
                r = eng.transpose(out, in_, ident)
            return r
        return self.op("pe", fn, reads, writes, est=70.0 * len(groups))

    def act(self, out, in_, func, reads, writes, scale=1.0, bias=0.0, accum=None):
        def fn(eng):
            if accum is not None:
                return eng.activation(out=out, in_=in_, func=func, bias=bias, scale=scale, accum_out=accum)
            return eng.activation(out=out, in_=in_, func=func, bias=bias, scale=scale)
        est = 190.0 + _fsz(in_) * 0.84 + (120.0 if accum is not None else 0.0)
        return self.op("act", fn, reads, writes, est=est)

    def tt(self, e, out, in0, in1, op, reads, writes, est=None):
        def fn(eng):
            return eng.tensor_tensor(out=out, in0=in0, in1=in1, op=op)
        n = _fsz(out)
        if est is None:
            if e == "dve":
                ps = (str(in0.space) == "PSUM") or (str(in1.space) == "PSUM")
                est = 80.0 + n * (1.05 if ps else 2.1)
            else:
                est = 300.0 + n * 1.8
        return self.op(e, fn, reads, writes, est=est)

    def stt(self, out, in0, scalar, in1, op0, op1, reads, writes):
        def fn(eng):
            return eng.scalar_tensor_tensor(out=out, in0=in0, scalar=scalar, in1=in1, op0=op0, op1=op1)
        ps = (str(in0.space) == "PSUM") or (str(in1.space) == "PSUM")
        return self.op("dve", fn, reads, writes, est=80.0 + _fsz(out) * (1.05 if ps else 2.1))

    def ts(self, e, out, in0, s1, s2, op0, op1, reads, writes):
        def fn(eng):
            return eng.tensor_scalar(out=out, in0=in0, scalar1=s1, scalar2=s2, op0=op0, op1=op1)
        return self.op(e, fn, reads, writes, est=80.0 + _fsz(out) * 1.05)

    def recip(self, out, in_, reads, writes):
        def fn(eng):
            return eng.reciprocal(out=out, in_=in_)
        return self.op("dve", fn, reads, writes, est=80.0 + _fsz(out) * 1.05)

    def copy(self, e, out, in_, reads, writes):
        if e == "act":
            return self.act(out, in_, AF.Copy, reads, writes)

        def fn(eng):
            return eng.tensor_copy(out=out, in_=in_)
        return self.op(e, fn, reads, writes, est=80.0 + _fsz(out) * 1.05)

    def memset(self, e, ap, val, writes):
        def fn(eng):
            return eng.memset(ap, val)
        return self.op(e, fn, (), writes, est=100.0 + _fsz(ap) * 1.0)


class Banks:
    def __init__(self, items):
        self.items = items
        self.i = 0

    def next(self):
        it = self.items[self.i % len(self.items)]
        self.i += 1
        return it


def build_program():
    nc = bass.Bass("TRN2", target_bir_lowering=False)

    uid = [0]

    def SB(name, shape, dt):
        uid[0] += 1
        return nc.sbuf_tensor(f"{name}_s{uid[0]}", shape, dt)

    def PS(name, shape, dt):
        uid[0] += 1
        return nc.psum_tensor(f"{name}_p{uid[0]}", shape, dt)

    def din(name, shape):
        return nc.dram_tensor(name, list(shape), F32, kind="ExternalInput").ap()

    x_in = din("x", [TOK, D])
    cg_in = din("cg", [NG, D])
    mk_in = din("mk", [128, 2 * NT])
    cst_in = din("cst", [128, 128 * 6 + 1024])
    norm_g = din("norm_g", [DEPTH, 4, D])
    w_ada = din("w_ada", [DEPTH, D, 6 * D])
    b_ada = din("b_ada", [DEPTH, 6 * D])
    gla_w_in = din("gla_w_in", [2, D, 3072])
    gla_w_gk1 = din("gla_w_gk1", [2, 2, D, 16])
    gla_w_gk2 = din("gla_w_gk2", [2, 2, 16, 512])
    gla_b_gk = din("gla_b_gk", [2, 2, 512])
    gla_g_head = din("gla_g_head", [2, 256])
    gla_w_out = din("gla_w_out", [2, D, D])
    sg_w_in = din("sg_w_in", [2, D, 2048])
    sg_b_in = din("sg_b_in", [2, 2048])
    sg_ln_g = din("sg_ln_g", [2, D])
    sg_ln_b = din("sg_ln_b", [2, D])
    sg_w_s = din("sg_w_s", [2, 4, 128, 128])
    sg_b_s = din("sg_b_s", [2, 4, 128])
    sg_w_out = din("sg_w_out", [2, D, D])
    ffn_w_in = din("ffn_w_in", [DEPTH, D, 2 * FH])
    ffn_w_out = din("ffn_w_out", [DEPTH, FH, D])
    y_out = nc.dram_tensor("y", [TOK, D], F32, kind="ExternalOutput").ap()
    xscr = nc.dram_tensor("xscr", [TOK, D], F32, kind="Internal").ap()
    modscr = nc.dram_tensor("modscr", [DEPTH, 6, NG, D], F32, kind="Internal").ap()
    sfscr = nc.dram_tensor("sfscr", [NT, 128, D], F32, kind="Internal").ap()
    kscr = nc.dram_tensor("kscr", [NT, 2, 128, 512], F32, kind="Internal").ap()
    vscr = nc.dram_tensor("vscr", [NT, 2, 128, D], BF16, kind="Internal").ap()
    pscr = nc.dram_tensor("pscr", [NT, 2, 128, D], BF16, kind="Internal").ap()

    with contextlib.ExitStack() as gstack:
        P = Prog(nc, gstack)

        stream_names = ["xa00", "xa01", "xa10", "xa11", "xr0", "xr1", "st0", "st1", "bcAS", "bcG", "w", "cb", "c0", "bada", "wa0", "wa1",
                        "sf", "sfl", "mod", "sgm", "gh", "wst0", "wst1", "w2", "w3", "w4", "w5",
                        "k0", "k1", "v0", "v1", "v2", "v3", "p0", "p1", "p2", "p3"]
        for n in stream_names:
            P.stream(n)
        P.frozen = True

        def clr(eng):
            r = None
            for s in P.allsems:
                r = eng.sem_clear(s)
            return None
        P.raw_sp(clr)
        P.flush()

        cs = gstack.enter_context
        identb = cs(SB("identb", [128, 128], BF16))
        identf = cs(SB("identf", [128, 128], F32))
        Tf = cs(SB("Tf", [128, 128], BF16))
        Tb = cs(SB("Tb", [128, 128], BF16))
        Uf = cs(SB("Uf", [128, 128], BF16))
        Ub = cs(SB("Ub", [128, 128], BF16))
        Mf = cs(SB("Mf", [128, 4, 128], BF16))
        Mb = cs(SB("Mb", [128, 4, 128], BF16))
        onesb = cs(SB("onesb", [128, 128], BF16))
        mk = cs(SB("mk", [128, 2 * NT], F32))
        mhalf = cs(SB("mhalf", [128, 8], F32))

        with contextlib.ExitStack() as st:
            al = st.enter_context
            cg = al(SB("cg", [NG, D], F32))
            scg = al(SB("scg", [NG, D], F32))
            scT = al(SB("scT", [128, 8, 8], BF16))
            Wa = [al(SB(f"Wa{i}", [128, 8, 2048], BF16)) for i in range(2)]
            modrow = al(SB("modrow", [NG, 6 * D], F32))
            bada = al(SB("bada", [NG, 6 * D], F32))
            ng6 = al(SB("ng6", [NG, 4, D], F32))
            modo = al(SB("modo", [NG, 6, D], F32))
            psC = al(PS("psC", [128, 8, 8], F32))
            psM = [al(PS(f"psM{i}", [128, 512], F32)) for i in range(2)]

            P.dma("sp", "c0", [(identf[:], cst_in[:, 0:128]), (mk[:], mk_in[:, :]),
                                 (cg[:], cg_in[:, :])],
                  writes=["identf", "mk", "cg"])
            P.dma("pool", "cb", [(identb[:], cst_in[:, 0:128]), (Tf[:], cst_in[:, 128:256]),
                                (Tb[:], cst_in[:, 256:384]), (Uf[:], cst_in[:, 384:512]),
                                (Ub[:], cst_in[:, 512:640]), (onesb[:], cst_in[:, 640:768]),
                                (Mf[:], cst_in[:, 768:1280].rearrange("p (h c) -> p h c", h=4)),
                                (Mb[:], cst_in[:, 1280:1792].rearrange("p (h c) -> p h c", h=4))],
                  writes=["cb", "Mf", "Mb"])
            P.memset("pool", mhalf[:], -0.5, ["mhalf"])
            P.act(scg[:], cg[:], AF.Silu, ["cg"], ["scg"])
            P.tr([(psC[:, kc, 0:NG], scg[0:NG, kc * 128:(kc + 1) * 128], identf[0:NG, 0:NG]) for kc in range(8)],
                 ["scg", "identf"], ["psC"])
            P.copy("act", scT[:, :, 0:NG], psC[:, :, 0:NG], ["psC"], ["scT"])
            pi = 0
            for i in range(DEPTH_RUN):
                P.dma("sp", "bada", [(bada[:], b_ada[i:i + 1, :].partition_broadcast(NG)),
                                     (ng6[:], norm_g[i:i + 1, :, :].partition_broadcast(NG))],
                      writes=["bada", "ng6"])
                for q in range(3):
                    wa = Wa[(i * 3 + q) % 2]
                    wk = f"Wa{(i * 3 + q) % 2}"
                    P.dma("pool", f"wa{(i * 3 + q) % 2}", [(wa[:, kc, :], w_ada[i, kc * 128:(kc + 1) * 128, q * 2048:(q + 1) * 2048])
                                        for kc in range(8)], writes=[wk])
                    for n in range(4):
                        pm = psM[pi % 2]
                        pk = f"psM{pi % 2}"
                        pi += 1
                        P.mm(pm[0:NG, :], [(scT[:, kc, 0:NG], wa[:, kc, n * 512:(n + 1) * 512]) for kc in range(8)],
                             ["scT", wk], [pk])
                        c0 = q * 2048 + n * 512
                        P.tt("dve", modrow[:, c0:c0 + 512], pm[0:NG, :], bada[:, c0:c0 + 512], ALU.add,
                             [pk, "bada"], ["modrow"])
                P.stt(modo[:, 0, :], modrow[:, 1024:2048], 1.0, ng6[:, 0, :], ALU.add, ALU.mult,
                      ["modrow", "ng6"], ["modo"])
                P.copy("dve", modo[:, 1, :], modrow[:, 0:1024], ["modrow"], ["modo"])
                P.tt("dve", modo[:, 2, :], modrow[:, 2048:3072], ng6[:, 1, :], ALU.mult, ["modrow", "ng6"], ["modo"])
                P.stt(modo[:, 3, :], modrow[:, 4096:5120], 1.0, ng6[:, 2, :], ALU.add, ALU.mult,
                      ["modrow", "ng6"], ["modo"])
                P.copy("dve", modo[:, 4, :], modrow[:, 3072:4096], ["modrow"], ["modo"])
                P.tt("dve", modo[:, 5, :], modrow[:, 5120:6144], ng6[:, 3, :], ALU.mult, ["modrow", "ng6"], ["modo"])
                P.dma("sp", "mod", [(modscr[i].rearrange("k g d -> g k d"), modo[:])], reads=["modo"],
                      writes=[("modscr", i)])
            P.flush()

        def rows(t, j=None):
            if j is None:
                return slice(t * TT, (t + 1) * TT)
            return slice(t * TT + j * 128, t * TT + (j + 1) * 128)

        class Ctx:
            pass

        def rsqrt(out, acc, n, scale, keys_in, key_out):
            P.ts("dve", acc, acc, scale, EPS, ALU.mult, ALU.add, keys_in, keys_in)
            P.tt("pool", out, acc, mhalf[:, 0:n], ALU.pow, keys_in + ["mhalf"], [key_out], est=1700.0)

        def alloc_common(al, C, nY=2, inplace=True, nxa=1):
            C.inplace = inplace
            C.xas = [al(SB(f"xa{i}", [128, 2, D], F32)) for i in range(nxa)]
            C.xai = 0
            C.xa = C.xas[0]
            C.xak = "xa0"
            C.xr = [al(SB(f"xr{j}", [128, D], F32)) for j in range(2)]
            C.bcA = al(SB("bcA", [128, D], F32))
            C.bcS = al(SB("bcS", [128, D], F32))
            C.bcG = al(SB("bcG", [128, D], F32))
            C.junk = al(SB("junk", [128, D], BF16))
            C.t1 = [al(SB(f"t1_{j}", [128, D], F32)) for j in range(2)]
            C.h = [al(SB(f"h{j}", [128, D], BF16)) for j in range(2)]
            C.hTs = [al(SB(f"hT{i}", [128, 8, TT], BF16)) for i in range(2)]
            C.hTi = 0
            C.hT = C.hTs[0]
            C.hTk = "hT0"
            C.ssq = al(SB("ssq", [128, 2], F32))
            C.rstd = al(SB("rstd", [128, 2], F32))
            C.ssqy = al(SB("ssqy", [128, 2], F32))
            C.rstdy = al(SB("rstdy", [128, 2], F32))
            C.ssq2 = [al(SB(f"ssq2_{j}", [128, 2], F32)) for j in range(2)]
            C.psT = al(PS("psT", [128, 8, 128], BF16))
            C.psY = [al(PS(f"psY{j}", [128, D], F32)) for j in range(nY)]

        def load_bcAS(C, layer, sub, g):
            k0 = 3 * sub
            P.dma("sp", "bcAS", [(C.bcA[:], modscr[layer, k0, g:g + 1, :].partition_broadcast(128)),
                                 (C.bcS[:], modscr[layer, k0 + 1, g:g + 1, :].partition_broadcast(128))],
                  reads=[("modscr", layer)], writes=["bcA", "bcS"])

        def load_bcG(C, layer, sub, g):
            k0 = 3 * sub
            P.dma("sp", "bcG", [(C.bcG[:], modscr[layer, k0 + 2, g:g + 1, :].partition_broadcast(128))],
                  reads=[("modscr", layer)], writes=["bcG"])

        def front_a(C, t, xsrc):
            C.xai += 1
            bi = C.xai % len(C.xas)
            C.xa = C.xas[bi]
            for j in range(2):
                xk = f"xa{bi}_{j}"
                P.dma("sp", f"xa{bi}{j}", [(C.xa[:, j, :], xsrc[rows(t, j), :])], reads=[("xd", t, j)], writes=[xk])
                P.act(C.junk[:], C.xa[:, j, :], AF.Square, [xk], [f"ssq{j}"], accum=C.ssq[:, j:j + 1])
                rsqrt(C.rstd[:, j:j + 1], C.ssq[:, j:j + 1], 1, 1.0 / D, [f"ssq{j}"], f"rstd{j}")
                if C.inplace:
                    P.stt(C.xa[:, j, :], C.xa[:, j, :], C.rstd[:, j:j + 1], C.bcA[:], ALU.mult, ALU.mult,
                          [xk, f"rstd{j}", "bcA"], [xk])
                    P.tt("pool", C.h[j][:], C.xa[:, j, :], C.bcS[:], ALU.add, [xk, "bcS"], [f"h{j}"])
                else:
                    P.stt(C.t1[j][:], C.xa[:, j, :], C.rstd[:, j:j + 1], C.bcA[:], ALU.mult, ALU.mult,
                          [xk, f"rstd{j}", "bcA"], [f"t1_{j}"])
                    P.tt("pool", C.h[j][:], C.t1[j][:], C.bcS[:], ALU.add, [f"t1_{j}", "bcS"], [f"h{j}"])

        def front_b(C, t):
            C.hTi += 1
            C.hT = C.hTs[C.hTi % 2]
            C.hTk = f"hT{C.hTi % 2}"
            for j in range(2):
                P.tr([(C.psT[:, kc, :], C.h[j][:, kc * 128:(kc + 1) * 128], identb[:]) for kc in range(8)],
                     [f"h{j}", "cb"], ["psT"])
                P.copy("act", C.hT[:, :, j * 128:(j + 1) * 128], C.psT[:], ["psT"], [C.hTk])

        def load_xr(C, t, xsrc):
            for j in range(2):
                P.dma("sp", f"xr{j}", [(C.xr[j][:], xsrc[rows(t, j), :])], reads=[("xd", t, j)], writes=[f"xr{j}"])

        def post(C, t, j, xdst, pieces=None):
            if pieces is None:
                pieces = [(C.psY[j % len(C.psY)][:], f"psY{j % len(C.psY)}", 0, D)]
            if len(pieces) == 1:
                ap, pk, c0, ncol = pieces[0]
                P.act(C.junk[:], ap, AF.Square, [pk], ["ssqy"], accum=C.ssqy[:, j:j + 1])
            else:
                for i, (ap, pk, c0, ncol) in enumerate(pieces):
                    P.act(C.junk[:, 0:ncol], ap, AF.Square, [pk], [f"ssq2_{j}"], accum=C.ssq2[j][:, i:i + 1])
                P.tt("dve", C.ssqy[:, j:j + 1], C.ssq2[j][:, 0:1], C.ssq2[j][:, 1:2], ALU.add, [f"ssq2_{j}"], ["ssqy"])
            for ap, pk, c0, ncol in pieces:
                P.tt("dve", ap, ap, C.bcG[:, c0:c0 + ncol], ALU.mult, [pk, "bcG"], [pk])
            rsqrt(C.rstdy[:, j:j + 1], C.ssqy[:, j:j + 1], 1, 1.0 / D, ["ssqy"], "rstdy")
            for ap, pk, c0, ncol in pieces:
                P.stt(C.xr[j][:, c0:c0 + ncol], ap, C.rstdy[:, j:j + 1], C.xr[j][:, c0:c0 + ncol], ALU.mult, ALU.add,
                      [pk, "rstdy", f"xr{j}"], [f"xr{j}"])
            P.dma("sp", f"st{j}", [(xdst[rows(t, j), :], C.xr[j][:])], reads=[f"xr{j}"], writes=[("xd", t, j)])

        wkeys = {}

        def load_w(dst, src2d, nk, ncols, key, stream="w", ranges=None):
            if ranges is None:
                ranges = [(0, ncols)]
            pairs = []
            for (r0, r1) in ranges:
                for kc in range(nk):
                    c0 = r0
                    while c0 < r1:
                        c1 = min(r1, c0 + 2048)
                        pairs.append((dst[:, kc, c0:c1], src2d[kc * 128:(kc + 1) * 128, c0:c1]))
                        c0 = c1
            for i in range(0, len(pairs), 8):
                P.dma("pool", stream, pairs[i:i + 8], writes=[key])
            wkeys.setdefault(stream, []).append(key)

        def wfence():
            for stream, keys in wkeys.items():
                P.fence(stream, keys)
            wkeys.clear()

        phase_no = [0]

        def phase_enabled():
            phase_no[0] += 1
            return phase_no[0] <= PH_LIMIT

        def run_tiles(C, layer, sub, order, xsrc, xdst, mainA, mainB, with_post=True):
            g0 = order[0] // TPG
            load_bcAS(C, layer, sub, g0)
            if with_post:
                load_bcG(C, layer, sub, g0)
            front_a(C, order[0], xsrc)
            front_b(C, order[0])
            for idx, t in enumerate(order):
                nxt = order[idx + 1] if idx + 1 < len(order) else None
                chg = nxt is not None and nxt // TPG != t // TPG
                if with_post:
                    load_xr(C, t, xsrc)
                if nxt is not None:
                    if chg:
                        load_bcAS(C, layer, sub, nxt // TPG)
                    front_a(C, nxt, xsrc)
                hT_cur, hTk_cur = C.hT, C.hTk
                if nxt is not None:
                    front_b(C, nxt)
                    hT_nxt, hTk_nxt = C.hT, C.hTk
                    C.hT, C.hTk = hT_cur, hTk_cur
                mainA(t)
                if nxt is not None:
                    C.hT, C.hTk = hT_nxt, hTk_nxt
                mainB(t)
                if chg and with_post:
                    load_bcG(C, layer, sub, nxt // TPG)

        def ffn_phase(layer, xsrc, xdst):
            with contextlib.ExitStack() as st:
                al = st.enter_context
                C = Ctx()
                alloc_common(al, C, nY=2, inplace=True, nxa=1)
                W1 = al(SB("W1", [128, 8, 2 * FH], BF16))
                W2 = al(SB("W2", [128, 22, D], BF16))
                gT = al(SB("gT", [128, 22, TT], BF16))
                sg = [al(SB(f"sg{i}", [128, TT], F32)) for i in range(2)]
                pab = Banks([(f"pab{i}", al(PS(f"pab{i}", [128, 2, TT], F32))) for i in range(3)])
                GC = 6
                for g, stream in enumerate(("w", "w2", "w3", "w4")):
                    c0, c1 = g * GC * 128, min((g + 1) * GC, 22) * 128
                    load_w(W1, ffn_w_in[layer], 8, 2 * FH, f"W1g{g}", stream, [(c0, c1), (FH + c0, FH + c1)])
                load_w(W2, ffn_w_out[layer], 22, D, "W2", "w5")
                wfence()

                def mainA(t):
                    for c in range(22):
                        pk, pb = pab.next()
                        P.mmg([(pb[:, 0, :], [(W1[:, kc, c * 128:(c + 1) * 128], C.hT[:, kc, :]) for kc in range(8)]),
                               (pb[:, 1, :], [(W1[:, kc, FH + c * 128:FH + (c + 1) * 128], C.hT[:, kc, :]) for kc in range(8)])],
                              [f"W1g{c // 6}", C.hTk], [pk])
                        s = sg[c % 2]
                        P.act(s[:], pb[:, 0, :], AF.Silu, [pk], [f"sg{c % 2}"])
                        P.tt("dve", gT[:, c, :], s[:], pb[:, 1, :], ALU.mult, [pk, f"sg{c % 2}"], ["gT"])

                def mainB(t):
                    for j in range(2):
                        P.mmg([(C.psY[j][:, f * 512:(f + 1) * 512],
                                [(gT[:, c, j * 128:(j + 1) * 128], W2[:, c, f * 512:(f + 1) * 512]) for c in range(22)])
                               for f in range(2)], ["gT", "W2"], [f"psY{j}"])
                        post(C, t, j, xdst)

                run_tiles(C, layer, 1, list(range(NT)), xsrc, xdst, mainA, mainB)
                P.flush()

        def sgu_phase(layer, l2, xsrc, xdst):
            with contextlib.ExitStack() as st:
                al = st.enter_context
                C = Ctx()
                alloc_common(al, C, nY=1, inplace=True, nxa=2)
                Wi = al(SB("Wi", [128, 8, 2048], BF16))
                Wo = al(SB("Wo", [128, 8, D], BF16))
                binb = al(SB("binb", [1, 2048], BF16))
                wsn = al(SB("wsn", [128, 4, 128], F32))
                WsT = al(SB("WsT", [128, 4, 128], BF16))
                bs = al(SB("bs", [128, 4], F32))
                lng = al(SB("lng", [128, D], F32))
                lnb = al(SB("lnb", [128, D], F32))
                u = [al(SB(f"u{j}", [128, D], F32)) for j in range(2)]
                vr = [al(SB(f"vr{j}", [128, D], F32)) for j in range(2)]
                vn = [al(SB(f"vn{j}", [128, D], F32)) for j in range(2)]
                vb = [al(SB(f"vb{j}", [128, D], BF16)) for j in range(2)]
                m = [al(SB(f"m{j}", [128, D], BF16)) for j in range(2)]
                mT = al(SB("mT", [128, 8, 128], BF16))
                bst = al(SB("bst", [128, 12], F32))
                mv = al(SB("mv", [128, 2], F32))
                lrs = al(SB("lrs", [128, 1], F32))
                lnm = al(SB("lnm", [128, 1], F32))
                pz = Banks([(f"pz{i}", al(PS(f"pz{i}", [128, 512], F32))) for i in range(3)])
                psV = al(PS("psV", [128, D], F32))

                P.dma("pool", "w", [(binb[:], sg_b_in[l2:l2 + 1, :])], writes=["binb"])
                wkeys.setdefault("w", []).append("binb")
                for q, stream in enumerate(("w", "w2", "w3", "w4")):
                    load_w(Wi, sg_w_in[l2], 8, 2048, f"Wi{q}", stream, [(q * 512, (q + 1) * 512)])
                load_w(Wo, sg_w_out[l2], 8, D, "Wo", "w5")
                wfence()
                P.dma("sp", "sgm", [(wsn[:], sg_w_s[l2].rearrange("g t s -> t g s")),
                                    (lng[:], sg_ln_g[l2:l2 + 1, :].partition_broadcast(128)),
                                    (lnb[:], sg_ln_b[l2:l2 + 1, :].partition_broadcast(128))]
                      + [(bs[:, g:g + 1], sg_b_s[l2, g, :].rearrange("(p o) -> p o", o=1)) for g in range(4)],
                      writes=["wsn", "lng", "lnb", "bs"])
                pk, pb = pz.next()
                P.tr([(pb[:, g * 128:(g + 1) * 128], wsn[:, g, :], identf[:]) for g in range(4)],
                     ["wsn", "identf"], [pk])
                P.copy("act", WsT[:].rearrange("p g t -> p (g t)"), pb[:], [pk], ["WsT"])

                def mainA(t):
                    for j in range(2):
                        for q in range(4):
                            pk, pb = pz.next()
                            prs = [(C.hT[:, kc, j * 128:(j + 1) * 128], Wi[:, kc, q * 512:(q + 1) * 512]) for kc in range(8)]
                            prs.append((onesb[0:1, :], binb[0:1, q * 512:(q + 1) * 512]))
                            P.mm(pb[:], prs, [C.hTk, f"Wi{q}", "binb", "cb"], [pk])
                            if q < 2:
                                P.act(u[j][:, q * 512:(q + 1) * 512], pb[:], AF.Gelu_apprx_tanh, [pk], [f"u{j}"])
                            else:
                                P.act(vr[j][:, (q - 2) * 512:(q - 1) * 512], pb[:], AF.Gelu_apprx_tanh, [pk], [f"vr{j}"])

                def mainB(t):
                    for j in range(2):
                        def bn(eng, j=j):
                            eng.bn_stats(out=bst[:, 0:6], in_=vr[j][:, 0:512])
                            return eng.bn_stats(out=bst[:, 6:12], in_=vr[j][:, 512:1024])
                        P.op("dve", bn, [f"vr{j}"], ["bst"])

                        def bna(eng):
                            return eng.bn_aggr(out=mv[:], in_=bst[:])
                        P.op("dve", bna, ["bst"], ["mv"])
                        rsqrt(lrs[:], mv[:, 1:2], 1, 1.0, ["mv"], "lrs")
                        P.stt(lnm[:], mv[:, 0:1], -1.0, lrs[:], ALU.mult, ALU.mult, ["mv", "lrs"], ["lnm"])
                        P.act(vn[j][:], vr[j][:], AF.Identity, [f"vr{j}", "lrs", "lnm"], [f"vn{j}"], scale=lrs[:, 0:1], bias=lnm[:, 0:1])
                        P.tt("dve", vn[j][:], vn[j][:], lng[:], ALU.mult, [f"vn{j}", "lng"], [f"vn{j}"])
                        P.tt("pool", vb[j][:], vn[j][:], lnb[:], ALU.add, [f"vn{j}", "lnb"], [f"vb{j}"])
                        P.mmg([(psV[:, g * 256:(g + 1) * 256], [(WsT[:, g, :], vb[j][:, g * 256:(g + 1) * 256])]) for g in range(4)],
                              ["WsT", f"vb{j}"], ["psV"])
                        for g in range(4):
                            P.stt(m[j][:, g * 256:(g + 1) * 256], psV[:, g * 256:(g + 1) * 256], bs[:, g:g + 1],
                                  u[j][:, g * 256:(g + 1) * 256], ALU.add, ALU.mult, ["psV", "bs", f"u{j}"], [f"m{j}"])
                        P.tr([(C.psT[:, kc, :], m[j][:, kc * 128:(kc + 1) * 128], identb[:]) for kc in range(8)],
                             [f"m{j}", "cb"], ["psT"])
                        P.copy("act", mT[:], C.psT[:], ["psT"], ["mT"])
                        P.mmg([(C.psY[0][:, f * 512:(f + 1) * 512],
                                [(mT[:, kc, :], Wo[:, kc, f * 512:(f + 1) * 512]) for kc in range(8)])
                               for f in range(2)], ["mT", "Wo"], ["psY0"])
                        post(C, t, j, xdst)

                run_tiles(C, layer, 0, list(range(NT)), xsrc, xdst, mainA, mainB)
                P.flush()

        def gla_phase(layer, l2, xsrc, xdst):
            with contextlib.ExitStack() as st:
                al = st.enter_context
                C = Ctx()
                alloc_common(al, C, nY=0, inplace=(os.environ.get("MK_GI", "0") == "1"), nxa=1)
                Wi = al(SB("Wi", [128, 8, 3072], BF16))
                Wo = al(SB("Wo", [128, 8, D], BF16))
                wstage = C.t1
                gh = al(SB("gh", [128, 2], F32))
                Wg1 = al(SB("Wg1", [128, 8, 32], BF16))
                Wg2 = al(SB("Wg2", [33, D], BF16))
                g1Ts = [al(SB(f"g1T{i}", [33, TT], BF16)) for i in range(2)]
                etmp = [al(SB(f"etmp{i}", [128, 512], F32)) for i in range(2)]
                Pm = [al(SB(f"Pm{j}", [128, D], BF16)) for j in range(4)]
                qTs = [al(SB(f"qT{i}", [128, 4, TT], F32)) for i in range(2)]
                kTs = [al(SB(f"kT{i}", [128, 4, TT], F32)) for i in range(2)]
                ktok = [al(SB(f"ktok{j}", [128, 512], F32)) for j in range(2)]
                vtok = [al(SB(f"vtok{j}", [128, D], BF16)) for j in range(4)]
                sr = [al(SB(f"sr{j}", [128, D], BF16)) for j in range(4)]
                decs = al(SB("decs", [128, 2, 2, 4], F32))
                qd = [[al(SB(f"qd{d}{j}", [128, 4, 128], BF16)) for j in range(2)] for d in range(2)]
                kd = [al(SB(f"kd{d}", [128, 4, 128], BF16)) for d in range(2)]
                kend = [[al(SB(f"kend{d}{j}", [128, 512], BF16)) for j in range(2)] for d in range(2)]
                scm = [[al(SB(f"scm{d}{j}", [128, 4, 128], BF16)) for j in range(2)] for d in range(2)]
                Sf = al(SB("Sf", [128, D], F32))
                Sb = al(SB("Sb", [128, D], F32))
                Sfb = [al(SB(f"Sfb{j}", [128, D], BF16)) for j in range(2)]
                Sbb = [al(SB(f"Sbb{j}", [128, D], BF16)) for j in range(2)]
                dec = al(SB("dec", [128, 4], F32))
                ssqo = al(SB("ssqo", [128, 4], F32))
                rstdo = al(SB("rstdo", [128, 4], F32))
                mm_ = al(SB("mm_", [128, D], BF16))
                mT = al(SB("mT", [128, 8, 128], BF16))
                pgbanks = [(f"pg{i}", al(PS(f"pg{i}", [128, 512], F32))) for i in range(7)]
                pools = {1: (Banks(pgbanks[:5]), Banks(pgbanks[5:])), 2: (Banks(pgbanks[:2]), Banks(pgbanks[2:]))}
                pgA, pgB = pools[1]
                pgsel = [pgA]

                class _PG:
                    def next(self):
                        return pgsel[0].next()
                pg = _PG()

                P.dma("pool", "w", [(Wg1[:, :, e * 16:(e + 1) * 16], gla_w_gk1[l2, e].rearrange("(kc p) r -> p kc r", p=128))
                                    for e in range(2)], writes=["Wg1"])
                P.memset("pool", Wg2[:], 0.0, ["Wg2"])
                P.dma("pool", "w", [(Wg2[0:16, 0:512], gla_w_gk2[l2, 0]), (Wg2[16:32, 512:1024], gla_w_gk2[l2, 1]),
                                    (Wg2[32:33, :], gla_b_gk[l2:l2 + 1].rearrange("o e k -> o (e k)"))],
                      reads=["Wg2"], writes=["Wg2"])
                wkeys.setdefault("w", []).extend(["Wg1", "Wg2"])
                load_w(Wi, gla_w_in[l2], 8, 3072, "Wik", "w2", [(512, 1024)])
                load_w(Wi, gla_w_in[l2], 8, 3072, "Wiv", "w3", [(1024, 2048)])
                load_w(Wi, gla_w_in[l2], 8, 3072, "Wiq", "w4", [(0, 512)])
                load_w(Wi, gla_w_in[l2], 8, 3072, "Wib", "w5", [(2048, 3072)])
                wfence()
                P.dma("sp", "gh", [(gh[:, b:b + 1], gla_g_head[l2, b * 128:(b + 1) * 128].rearrange("(p o) -> p o", o=1))
                                   for b in range(2)], writes=["gh"])
                for kc in range(8):
                    ws = wstage[kc % 2]
                    P.dma("sp", f"wst{kc % 2}", [(ws[:], gla_w_out[l2, kc * 128:(kc + 1) * 128, :])], writes=[f"t1_{kc % 2}"])
                    P.act(Wo[:, kc, :], ws[:], AF.Copy, [f"t1_{kc % 2}", "gh"], ["Wo"], scale=gh[:, (kc % 2):(kc % 2) + 1])
                for i in range(2):
                    P.memset("pool", g1Ts[i][:], 1.0, [f"g1T{i}"])

                def VJ(t, j):
                    return j + 2 * (t % 2)

                def gates(t, ndir):
                    pk, pb = pg.next()
                    g1T = g1Ts[t % 2]
                    gk = f"g1T{t % 2}"
                    P.mm(pb[0:32, 0:TT], [(Wg1[:, kc, :], C.hT[:, kc, :]) for kc in range(8)], ["Wg1", C.hTk], [pk])
                    P.copy("act", g1T[0:32, :], pb[0:32, 0:TT], [pk], [gk])
                    for j in range(2):
                        for d in range(ndir):
                            pk, pb = pg.next()
                            P.mm(pb[:], [(g1T[0:33, j * 128:(j + 1) * 128], Wg2[0:33, d * 512:(d + 1) * 512])],
                                 [gk, "Wg2"], [pk])
                            P.act(pb[:], pb[:], AF.Exp, [pk], [pk], scale=-1.0)
                            P.act(Pm[VJ(t, j)][:, d * 512:(d + 1) * 512], pb[:], AF.Ln, [pk], [f"Pm{VJ(t, j)}"], bias=1.0)

                eti = [0]

                def proj_tok(t, j, c0, ncol, dst, dkey, func=AF.Copy, scale=1.0):
                    for n in range(ncol // 512):
                        pk, pb = pg.next()
                        P.mm(pb[:], [(C.hT[:, kc, j * 128:(j + 1) * 128], Wi[:, kc, c0 + n * 512:c0 + (n + 1) * 512]) for kc in range(8)],
                             [C.hTk, {0: "Wiq", 512: "Wik", 1024: "Wiv", 2048: "Wib"}[c0]], [pk])
                        if func == AF.Silu:
                            eti[0] += 1
                            et = etmp[eti[0] % 2]
                            ek = f"etmp{eti[0] % 2}"
                            if os.environ.get("MK_SILU", "dve") == "act":
                                P.act(et[:], pb[:], AF.Exp, [pk], [ek], scale=-1.0)
                                P.act(et[:], et[:], AF.Ln, [ek], [ek], bias=1.0)
                                P.act(et[:], et[:], AF.Exp, [ek], [ek], scale=-1.0)
                            else:
                                P.act(et[:], pb[:], AF.Exp, [pk], [ek], scale=-1.0)
                                P.ts("dve", et[:], et[:], 1.0, None, ALU.add, ALU.bypass, [ek], [ek])
                                P.recip(et[:], et[:], [ek], [ek])
                            P.tt("dve", dst[:, n * 512:(n + 1) * 512], pb[:], et[:], ALU.mult, [pk, ek], [dkey])
                        else:
                            P.act(dst[:, n * 512:(n + 1) * 512], pb[:], func, [pk], [dkey], scale=scale)

                def kend_dir(d, j, U, t):
                    pk, pb = pg.next()
                    P.mm(pb[:], [(U[:], Pm[VJ(t, j)][:, d * 512:(d + 1) * 512])], ["cb", f"Pm{VJ(t, j)}"], [pk])
                    P.act(pb[:], pb[:], AF.Exp, [pk], [pk], scale=-1.0 / 16)
                    P.tt("dve", kend[d][j][:], ktok[j][:], pb[:], ALU.mult, [f"ktok{j}", pk], [f"kend{d}{j}"])

                def state_update(S, skey, kd_, kkey, j, decap, deckeys, out_bf=None, okey=None):
                    for hh in range(2):
                        pk, pb = pg.next()
                        P.mmg([(pb[:, i * 256:(i + 1) * 256],
                                [(kd_[:, (2 * hh + i) * 128:(2 * hh + i + 1) * 128], vtok[j][:, (2 * hh + i) * 256:(2 * hh + i + 1) * 256])])
                               for i in range(2)], [kkey, f"vtok{j}"], [pk])
                        for i in range(2):
                            hd = 2 * hh + i
                            dst = S if out_bf is None else out_bf
                            dk = skey if out_bf is None else okey
                            P.stt(dst[:, hd * 256:(hd + 1) * 256], S[:, hd * 256:(hd + 1) * 256], decap(hd),
                                  pb[:, i * 256:(i + 1) * 256], ALU.mult, ALU.add, [skey, pk] + deckeys, [dk])

                if phase_enabled():
                    P.memset("dve", Sf[:], 0.0, ["Sf"])

                    def p1A(t):
                        pgsel[0] = pools[1][0]
                        if t % TPG == 0:
                            P.ts("dve", Sf[:], Sf[:], mk[:, t:t + 1], None, ALU.mult, ALU.bypass, ["Sf", "mk"], ["Sf"])
                        P.dma("sp", "sf", [(sfscr[t], Sf[:])], reads=["Sf"], writes=[("sfs", t)])
                        gates(t, 2)
                        for j in range(2):
                            vj = VJ(t, j)
                            proj_tok(t, j, 512, 512, ktok[j], f"ktok{j}")
                            proj_tok(t, j, 1024, 1024, vtok[vj], f"vtok{vj}")
                            P.dma("sp", f"k{j}", [(kscr[t, j], ktok[j][:])], reads=[f"ktok{j}"], writes=[("ks", t, j)])
                            P.dma("sp", f"v{vj}", [(vscr[t, j], vtok[vj][:])], reads=[f"vtok{vj}"], writes=[("vs", t, j)])
                            P.dma("sp", f"p{vj}", [(pscr[t, j], Pm[vj][:])], reads=[f"Pm{vj}"], writes=[("pss", t, j)])

                    def p1B(t):
                        pgsel[0] = pools[1][1]
                        for j in range(2):
                            kend_dir(0, j, Uf, t)
                            pk, pb = pg.next()
                            P.mmg([(pb[:, 2 * hd:2 * hd + 2], [(Pm[VJ(t, j)][:, hd * 128:(hd + 1) * 128], onesb[:, 0:2])]) for hd in range(4)],
                                  [f"Pm{VJ(t, j)}", "cb"], [pk])
                            P.act(dec[:], pb[:, 0:8].rearrange("p (h two) -> p h two", two=2)[:, :, 0], AF.Exp, [pk], ["dec"], scale=-1.0 / 16)
                            state_update(Sf, "Sf", kend[0][j], f"kend0{j}", VJ(t, j), lambda hd: dec[:, hd:hd + 1], ["dec"])

                    run_tiles(C, layer, 0, list(range(NT)), xsrc, xdst, p1A, p1B, with_post=False)
                    P.flush()

                if phase_enabled():
                    P.memset("dve", Sb[:], 0.0, ["Sb"])
                    P.memset("pool", Sbb[0][:], 0.0, ["Sbb0"])
                    sbi = [0]

                    def p2A(t):
                        pgsel[0] = pools[2][0]
                        for j in range(2):
                            vj = VJ(t, j)
                            P.dma("sp", f"v{vj}", [(vtok[vj][:], vscr[t, j])], reads=[("vs", t, j)], writes=[f"vtok{vj}"])
                            P.dma("sp", f"p{vj}", [(Pm[vj][:], pscr[t, j])], reads=[("pss", t, j)], writes=[f"Pm{vj}"])
                            P.dma("sp", f"k{j}", [(ktok[j][:], kscr[t, j])], reads=[("ks", t, j)], writes=[f"ktok{j}"])
                        for j in range(2):
                            proj_tok(t, j, 2048, 1024, sr[VJ(t, j)], f"sr{VJ(t, j)}", func=AF.Silu)
                        for dst, dkey, c0, sc in ((qTs[t % 2], f"qT{t % 2}", 0, 128.0 ** -0.5), (kTs[t % 2], f"kT{t % 2}", 512, 1.0)):
                            for hh in range(2):
                                pk, pb = pg.next()
                                P.mmg([(pb[:, i * TT:(i + 1) * TT],
                                        [(Wi[:, kc, c0 + (2 * hh + i) * 128:c0 + (2 * hh + i + 1) * 128], C.hT[:, kc, :]) for kc in range(8)])
                                       for i in range(2)], ["Wiq" if c0 == 0 else "Wik", C.hTk], [pk])
                                P.act(dst[:, 2 * hh:2 * hh + 2, :], pb[:].rearrange("p (i c) -> p i c", i=2), AF.Copy, [pk], [dkey], scale=sc)

                    def p2B(t):
                        pgsel[0] = pools[2][1]
                        P.dma("sp", "sfl", [(Sf[:], sfscr[t])], reads=[("sfs", t)], writes=["Sf"])
                        if t % TPG == TPG - 1:
                            P.ts("dve", Sb[:], Sb[:], mk[:, NT + t:NT + t + 1], None, ALU.mult, ALU.bypass, ["Sb", "mk"], ["Sb"])
                            sbi[0] += 1
                            P.copy("act", Sbb[sbi[0] % 2][:], Sb[:], ["Sb"], [f"Sbb{sbi[0] % 2}"])
                        P.copy("act", Sfb[0][:], Sf[:], ["Sf"], ["Sfb0"])
                        for j in range(2):
                            for d, (Tm, Um, Mm) in enumerate(((Tf, Uf, Mf), (Tb, Ub, Mb))):
                                pk, pb = pg.next()
                                P.mmg([(pb[:, hd * 128:(hd + 1) * 128], [(Pm[VJ(t, j)][:, d * 512 + hd * 128:d * 512 + (hd + 1) * 128], Tm[:])])
                                       for hd in range(4)], [f"Pm{VJ(t, j)}", "cb"], [pk])
                                pbv = pb[:].rearrange("p (h c) -> p h c", h=4)
                                pk2, pb2 = pg.next()
                                pbv2 = pb2[:].rearrange("p (h c) -> p h c", h=4)
                                P.act(pbv2, pbv, AF.Exp, [pk], [pk2], scale=1.0 / 16)
                                P.act(pbv, pbv, AF.Exp, [pk], [pk], scale=-1.0 / 16)
                                col = 127 if d == 0 else 0
                                P.act(decs[:, d, j, :], pbv[:, :, col], AF.Copy, [pk], [f"decs{d}{j}"])
                                P.tt("dve", qd[d][j][:], qTs[t % 2][:, :, j * 128:(j + 1) * 128], pbv, ALU.mult, [f"qT{t % 2}", pk], [f"qd{d}{j}"])
                                P.tt("dve", kd[d][:], kTs[t % 2][:, :, j * 128:(j + 1) * 128], pbv2, ALU.mult, [f"kT{t % 2}", pk2], [f"kd{d}"])
                                kend_dir(d, j, Um, t)
                                pk, pb = pg.next()
                                P.mmg([(pb[:, hd * 128:(hd + 1) * 128], [(kd[d][:, hd, :], qd[d][j][:, hd, :])]) for hd in range(4)],
                                      [f"kd{d}", f"qd{d}{j}"], [pk])
                                P.tt("dve", scm[d][j][:], pb[:].rearrange("p (h c) -> p h c", h=4), Mm[:], ALU.mult, [pk, "Mf", "Mb"], [f"scm{d}{j}"])
                        state_update(Sf, "Sf", kend[0][0], "kend00", VJ(t, 0), lambda hd: decs[:, 0, 0, hd:hd + 1], ["decs00"],
                                     out_bf=Sfb[1], okey="Sfb1")
                        for j in (1, 0):
                            cur = Sbb[sbi[0] % 2]
                            ck = f"Sbb{sbi[0] % 2}"
                            vv = vtok[VJ(t, j)]
                            obanks = []
                            for hh in range(2):
                                pk, pb = pg.next()
                                obanks.append((pk, pb))
                                P.mmg([(pb[:, i * 256:(i + 1) * 256],
                                        [(scm[0][j][:, 2 * hh + i, :], vv[:, (2 * hh + i) * 256:(2 * hh + i + 1) * 256]),
                                         (scm[1][j][:, 2 * hh + i, :], vv[:, (2 * hh + i) * 256:(2 * hh + i + 1) * 256]),
                                         (qd[0][j][:, 2 * hh + i, :], Sfb[j][:, (2 * hh + i) * 256:(2 * hh + i + 1) * 256]),
                                         (qd[1][j][:, 2 * hh + i, :], cur[:, (2 * hh + i) * 256:(2 * hh + i + 1) * 256])]) for i in range(2)],
                                      [f"scm0{j}", f"scm1{j}", f"vtok{VJ(t, j)}", f"qd0{j}", f"qd1{j}", f"Sfb{j}", ck], [pk])
                            state_update(Sb, "Sb", kend[1][j], f"kend1{j}", VJ(t, j), lambda hd, j=j: decs[:, 1, j, hd:hd + 1], [f"decs1{j}"])
                            sbi[0] += 1
                            P.copy("act", Sbb[sbi[0] % 2][:], Sb[:], ["Sb"], [f"Sbb{sbi[0] % 2}"])
                            for hd in range(4):
                                pk, pb = obanks[hd // 2]
                                P.act(C.junk[:, 0:256], pb[:, (hd % 2) * 256:(hd % 2 + 1) * 256], AF.Square, [pk], ["ssqo"], accum=ssqo[:, hd:hd + 1])
                            rsqrt(rstdo[:], ssqo[:], 4, 1.0 / 256, ["ssqo"], "rstdo")
                            for hd in range(4):
                                pk, pb = obanks[hd // 2]
                                P.stt(mm_[:, hd * 256:(hd + 1) * 256], pb[:, (hd % 2) * 256:(hd % 2 + 1) * 256], rstdo[:, hd:hd + 1],
                                      sr[VJ(t, j)][:, hd * 256:(hd + 1) * 256], ALU.mult, ALU.mult, [pk, "rstdo", f"sr{VJ(t, j)}"], ["mm_"])
                            P.tr([(C.psT[:, kc, :], mm_[:, kc * 128:(kc + 1) * 128], identb[:]) for kc in range(8)],
                                 ["mm_", "cb"], ["psT"])
                            P.copy("act", mT[:], C.psT[:], ["psT"], ["mT"])
                            pieces = []
                            for f in range(2):
                                pk, pb = pg.next()
                                P.mm(pb[:], [(mT[:, kc, :], Wo[:, kc, f * 512:(f + 1) * 512]) for kc in range(8)], ["mT", "Wo"], [pk])
                                pieces.append((pb[:], pk, f * 512, 512))
                            post(C, t, j, xdst, pieces)

                    run_tiles(C, layer, 0, list(range(NT - 1, -1, -1)), xsrc, xdst, p2A, p2B)
                    P.flush()

        cur = x_in
        for layer in range(DEPTH_RUN):
            last = layer == DEPTH_RUN - 1
            if layer % 2 == 0:
                gla_phase(layer, layer // 2, cur, xscr)
            else:
                if phase_enabled():
                    sgu_phase(layer, layer // 2, cur, xscr)
            cur = xscr
            if phase_enabled():
                ffn_phase(layer, cur, y_out if last else xscr)
    return nc


_NC_CACHE = {}


def _consts():
    s = np.arange(128)[:, None]
    c = np.arange(128)[None, :]
    ident = np.eye(128, dtype=np.float32)
    tf = (s <= c).astype(np.float32)
    tb = (s >= c).astype(np.float32)
    uf = (s > c).astype(np.float32)
    ub = (s < c).astype(np.float32)
    ones = np.ones((128, 128), np.float32)
    mf = np.tile(tf, (1, 4))
    mb = np.tile(uf, (1, 4))
    return np.ascontiguousarray(np.concatenate([ident, tf, tb, uf, ub, ones, mf, mb], axis=1))


def kernel(x_prompt, x_sample, c_prompt, c_sample, norm_g, w_ada, b_ada,
           gla_w_in, gla_w_gk1, gla_w_gk2, gla_b_gk, gla_g_head, gla_w_out,
           sg_w_in, sg_b_in, sg_ln_g, sg_ln_b, sg_w_s, sg_b_s, sg_w_out,
           ffn_w_in, ffn_w_out):
    f = lambda a: np.ascontiguousarray(np.asarray(a, dtype=np.float32))
    x_prompt, x_sample, c_prompt, c_sample = f(x_prompt), f(x_sample), f(c_prompt), f(c_sample)
    shared = dict(norm_g=f(norm_g), w_ada=f(w_ada), b_ada=f(b_ada), gla_w_in=f(gla_w_in),
                  gla_w_gk1=f(gla_w_gk1), gla_w_gk2=f(gla_w_gk2), gla_b_gk=f(gla_b_gk),
                  gla_g_head=f(gla_g_head), gla_w_out=f(gla_w_out), sg_w_in=f(sg_w_in),
                  sg_b_in=f(sg_b_in), sg_ln_g=f(sg_ln_g), sg_ln_b=f(sg_ln_b), sg_w_s=f(sg_w_s),
                  sg_b_s=f(sg_b_s), sg_w_out=f(sg_w_out), ffn_w_in=f(ffn_w_in), ffn_w_out=f(ffn_w_out),
                  cst=_consts())
    plan = []
    for r in range(4):
        plan.append([("s", r), ("p", 2 * r), ("p", 2 * r + 1)])
    for r in range(4, 8):
        plan.append([("p", 8 + (r - 4) * 6 + k) for k in range(6)])
    in_maps = []
    for r in range(NCORES):
        xs, cgs = [], []
        mkf = np.ones(NT, np.float32)
        mkb = np.ones(NT, np.float32)
        tpos = 0
        for kind, idx in plan[r]:
            if kind == "s":
                xs.append(x_sample[idx]); ng = 4; cv = c_sample[idx]
            else:
                xs.append(x_prompt[idx]); ng = 1; cv = c_prompt[idx]
            for _ in range(ng):
                cgs.append(cv)
            mkf[tpos] = 0.0
            tpos += ng * TPG
            mkb[tpos - 1] = 0.0
        xc = np.ascontiguousarray(np.concatenate(xs, axis=0))
        cg = np.ascontiguousarray(np.stack(cgs, axis=0))
        mk = np.ascontiguousarray(np.tile(np.concatenate([mkf, mkb])[None, :], (128, 1)).astype(np.float32))
        m = dict(shared)
        m.update(x=xc, cg=cg, mk=mk)
        in_maps.append(m)
    if "nc" not in _NC_CACHE:
        _NC_CACHE["nc"] = build_program()
    nc = _NC_CACHE["nc"]
    res = run_bass_kernel_spmd(nc, in_maps, core_ids=list(range(NCORES)))
    y_prompt = np.empty_like(x_prompt)
    y_sample = np.empty_like(x_sample)
    for r in range(NCORES):
        y = np.asarray(res.results[r]["y"], dtype=np.float32)
        pos = 0
        for kind, idx in plan[r]:
            if kind == "s":
                y_sample[idx] = y[pos:pos + 8192]; pos += 8192
            else:
                y_prompt[idx] = y[pos:pos + 2048]; pos += 2048
    return (y_prompt, y_sample)
```

```python
import os
import contextlib
import numpy as np
import concourse.bass as bass
import concourse.mybir as mybir
from concourse.bass_utils import run_bass_kernel_spmd

F32 = mybir.dt.float32
BF16 = mybir.dt.bfloat16
AF = mybir.ActivationFunctionType
ALU = mybir.AluOpType

D = 1024
NCORES = 8
TOK = 12288
TT = 256
NT = TOK // TT
TPG = 8
NG = NT // TPG
DEPTH = 4
FH = 2816
EPS = 1e-6
DEPTH_RUN = int(os.environ.get("MK_DEPTH", "4"))
PH_LIMIT = int(os.environ.get("MK_PHASES", "99"))


def _fsz(ap):
    n = 1
    for d in list(ap.shape)[1:]:
        n *= int(d)
    return n


SCHED_WINDOW = int(os.environ.get("MK_WINDOW", "96"))
VERB = bool(os.environ.get("MK_VERBOSE"))
STALL = {}


class Prog:
    ENGS = ("pe", "act", "dve", "pool", "sp")

    def __init__(self, nc, stack):
        self.nc = nc
        self.stack = stack
        self.esem = {e: stack.enter_context(nc.semaphore(f"es_{e}"))
                     for e in ("pe", "act", "dve", "pool")}
        self.ecnt = {e: 0 for e in self.esem}
        self.dsem = {}
        self.allsems = list(self.esem.values())
        self.waited = {e: {} for e in self.ENGS}
        self.semobj = {}
        for e, s in self.esem.items():
            self.semobj[id(s)] = s
        self.frozen = False
        self._reset()

    def _reset(self):
        self.ops = []
        self.lastw = {}
        self.readers = {}
        self.last_stream_op = {}

    def stream(self, name):
        if name not in self.dsem:
            assert not self.frozen, name
            s = self.stack.enter_context(self.nc.semaphore(f"ds_{name}"))
            self.dsem[name] = [s, 0]
            self.allsems.append(s)
            self.semobj[id(s)] = s
        return self.dsem[name]

    def _record(self, e, fn, reads, writes, kind, est, stream=None, ndma=0, comp=None):
        idx = len(self.ops)
        deps = {}
        for k in reads:
            w = self.lastw.get(k)
            if w is not None:
                deps[w] = True
            if isinstance(k, str) and k.startswith(("pg", "pab", "pz", "ps")):
                for r in self.readers.get(k, ()):
                    if self.ops[r]["e"] != e:
                        deps.setdefault(r, False)
        for k in writes:
            w = self.lastw.get(k)
            if w is not None:
                deps.setdefault(w, False)
            for r in self.readers.get(k, ()):
                deps.setdefault(r, False)
        order = []
        if stream is not None:
            p = self.last_stream_op.get(stream)
            if p is not None:
                order.append(p)
            self.last_stream_op[stream] = idx
        self.ops.append(dict(e=e, fn=fn, deps=deps, order=order, kind=kind, est=est,
                             stream=stream, ndma=ndma, comp=comp if comp is not None else est,
                             tag="%s>%s" % (",".join(str(k) for k in reads)[:40], ",".join(str(k) for k in writes)[:30])))
        for k in reads:
            self.readers.setdefault(k, set()).add(idx)
        for k in writes:
            self.lastw[k] = idx
            self.readers[k] = set()
        return idx

    def op(self, e, fn, reads=(), writes=(), est=300.0):
        return self._record(e, fn, reads, writes, "c", est)

    def dma(self, qe, stream, pairs, reads=(), writes=(), **kw):
        self.stream(stream)
        nbytes = 0
        for o, i in pairs:
            try:
                nbytes += int(o.nbytes)
            except Exception:
                nbytes += 4 * _fsz(o) * int(o.shape[0])

        def fn(eng, pairs=pairs, kw=kw):
            return [eng.dma_start(out=o, in_=i, **kw) for (o, i) in pairs]
        return self._record(qe, fn, reads, writes, "d", 60.0 * len(pairs), stream=stream, ndma=len(pairs),
                            comp=2500.0 + nbytes / 120.0)

    def fence(self, stream, keys):
        p = self.last_stream_op[stream]
        for k in keys:
            self.lastw[k] = p
            self.readers[k] = set()

    def _schedule(self):
        ops = self.ops
        n = len(ops)
        succ = [[] for _ in range(n)]
        nun = [0] * n
        ready = [0.0] * n
        start = [0.0] * n
        finish = [0.0] * n
        for i, o in enumerate(ops):
            ds = set(o["deps"].keys()) | set(o["order"])
            nun[i] = len(ds)
            for d in ds:
                succ[d].append(i)
        pend = {e: [i for i in range(n) if ops[i]["e"] == e] for e in self.ENGS}
        free = {e: 0.0 for e in self.ENGS}
        sched = {e: [] for e in self.ENGS}
        left = n
        W = SCHED_WINDOW
        while left:
            best = None
            for e in self.ENGS:
                pl = pend[e]
                fe = free[e]
                for pos in range(min(W, len(pl))):
                    i = pl[pos]
                    if nun[i]:
                        continue
                    st = ready[i] if ready[i] > fe else fe
                    if best is None or st < best[0] or (st == best[0] and i < best[1]):
                        best = (st, i, e, pos)
                    if st <= fe:
                        break
            assert best is not None, "scheduler deadlock"
            st, i, e, pos = best
            pend[e].pop(pos)
            o = ops[i]
            if VERB and st > free[e] + 1.0 and "blk" in o:
                key = (e, "", ops[o["blk"]]["e"], ops[o["blk"]].get("tag", "?"))
                STALL[key] = STALL.get(key, 0.0) + (st - free[e])
            start[i] = st
            free[e] = st + o["est"]
            finish[i] = st + o["comp"]
            sched[e].append(i)
            left -= 1
            for s in succ[i]:
                nun[s] -= 1
                so = ops[s]
                raw = so["deps"].get(i)
                if raw is None:
                    t = start[i]
                elif o["kind"] == "c" and so["kind"] == "c" and so["e"] == e and (e == "pe" or not raw):
                    t = free[e]
                else:
                    t = finish[i] + 60.0
                if t > ready[s]:
                    ready[s] = t
                    so["blk"] = i
        if VERB:
            top = sorted([kv for kv in STALL.items() if kv[0][0] == "pe"], key=lambda kv: -kv[1])[:12]
            for k, v in top:
                print("   stall %.0f us: %s" % (v / 1e3, k))
            STALL.clear()
            busy = {e: sum(ops[i]["est"] for i in sched[e]) / 1e3 for e in self.ENGS}
            print("block: n=%d makespan=%.1f us busy(us)=%s" % (n, max(finish) / 1e3 if n else 0.0,
                  {e: round(v) for e, v in busy.items()}), flush=True)
        return sched

    def flush(self):
        nc = self.nc
        ops = self.ops
        sched = self._schedule()
        tok = {}
        for e in self.ENGS:
            for i in sched[e]:
                o = ops[i]
                if o["kind"] == "x":
                    o["inc"] = None
                elif o["kind"] == "c":
                    self.ecnt[e] += 1
                    tok[i] = (id(self.esem[e]), self.ecnt[e])
                    o["inc"] = (self.esem[e], 1)
                else:
                    st = self.dsem[o["stream"]]
                    st[1] += 16 * o["ndma"]
                    tok[i] = (id(st[0]), st[1])
                    o["inc"] = (st[0], 16)
        qs = {e: [] for e in self.ENGS}
        for e in self.ENGS:
            w = self.waited[e]
            for i in sched[e]:
                o = ops[i]
                need = {}
                for d, raw in o["deps"].items():
                    od = ops[d]
                    if o["kind"] == "c" and od["kind"] == "c" and od["e"] == e and (e == "pe" or not raw):
                        continue
                    sid, val = tok[d]
                    if need.get(sid, 0) < val:
                        need[sid] = val
                waits = []
                for sid, val in need.items():
                    if w.get(sid, 0) < val:
                        w[sid] = val
                        waits.append((self.semobj[sid], val))
                qs[e].append((waits, o["fn"], o["inc"]))
        fin = []
        for name, (s, cnt) in self.dsem.items():
            if cnt > 0 and self.waited["sp"].get(id(s), 0) < cnt:
                self.waited["sp"][id(s)] = cnt
                fin.append((s, cnt))
        if fin:
            qs["sp"].append((fin, (lambda eng: None), None))
        with nc.Block() as block:
            names = {"pe": "tensor", "act": "scalar", "dve": "vector", "pool": "gpsimd", "sp": "sync"}
            for e in self.ENGS:
                if not qs[e]:
                    continue

                def body(eng, lst=qs[e]):
                    for waits, fn, inc in lst:
                        for s, v in waits:
                            eng.wait_ge(s, v)
                        r = fn(eng)
                        if inc is not None and r is not None:
                            if isinstance(r, list):
                                for ins in r:
                                    ins.then_inc(inc[0], inc[1])
                            else:
                                r.then_inc(inc[0], inc[1])
                getattr(block, names[e])(body)
        self._reset()
        for e in self.ENGS:
            w = self.waited[e]
            for ee, s in self.esem.items():
                w[id(s)] = self.ecnt[ee]
            for name, (s, cnt) in self.dsem.items():
                w[id(s)] = cnt

    def raw_sp(self, fn):
        self.ops.append(dict(e="sp", fn=fn, deps={}, order=[], kind="x", est=50.0, stream=None, ndma=0, comp=50.0))

    PE_NS = 0.513

    def mm(self, out, pairs, reads, writes):
        return self.mmg([(out, pairs)], reads, writes)

    def mmg(self, groups, reads, writes):
        est = 0.0
        for out, pairs in groups:
            for l, rh in pairs:
                est += max(_fsz(rh), 64) * self.PE_NS + 2.0

        def fn(eng, groups=groups):
            r = None
            for out, pairs in groups:
                n = len(pairs)
                for i, (l, rh) in enumerate(pairs):
                    r = eng.matmul(out, lhsT=l, rhs=rh, start=(i == 0), stop=(i == n - 1))
            return r
        return self.op("pe", fn, reads, writes, est=est)

    def tr(self, groups, reads, writes):
        def fn(eng, groups=groups):
            r = None
            for out, in_, ident in groups:
                r = eng.transpose(out, in_, ident)
            return r
        return self.op("pe", fn, reads, writes, est=70.0 * len(groups))

    def act(self, out, in_, func, reads, writes, scale=1.0, bias=0.0, accum=None):
        def fn(eng):
            if accum is not None:
                return eng.activation(out=out, in_=in_, func=func, bias=bias, scale=scale, accum_out=accum)
            return eng.activation(out=out, in_=in_, func=func, bias=bias, scale=scale)
        est = 190.0 + _fsz(in_) * 0.84 + (120.0 if accum is not None else 0.0)
        return self.op("act", fn, reads, writes, est=est)

    def tt(self, e, out, in0, in1, op, reads, writes, est=None):
        def fn(eng):
            return eng.tensor_tensor(out=out, in0=in0, in1=in1, op=op)
        n = _fsz(out)
        if est is None:
            if e == "dve":
                ps = (str(in0.space) == "PSUM") or (str(in1.space) == "PSUM")
                est = 80.0 + n * (1.05 if ps else 2.1)
            else:
                est = 300.0 + n * 1.8
        return self.op(e, fn, reads, writes, est=est)

    def stt(self, out, in0, scalar, in1, op0, op1, reads, writes):
        def fn(eng):
            return eng.scalar_tensor_tensor(out=out, in0=in0, scalar=scalar, in1=in1, op0=op0, op1=op1)
        ps = (str(in0.space) == "PSUM") or (str(in1.space) == "PSUM")
        return self.op("dve", fn, reads, writes, est=80.0 + _fsz(out) * (1.05 if ps else 2.1))

    def ts(self, e, out, in0, s1, s2, op0, op1, reads, writes):
        def fn(eng):
            return eng.tensor_scalar(out=out, in0=in0, scalar1=s1, scalar2=s2, op0=op0, op1=op1)
        return self.op(e, fn, reads, writes, est=80.0 + _fsz(out) * 1.05)

    def recip(self, out, in_, reads, writes):
        def fn(eng):
            return eng.reciprocal(out=out, in_=in_)
        return self.op("dve", fn, reads, writes, est=80.0 + _fsz(out) * 1.05)

    def copy(self, e, out, in_, reads, writes):
        if e == "act":
            return self.act(out, in_, AF.Copy, reads, writes)

        def fn(eng):
            return eng.tensor_copy(out=out, in_=in_)
        return self.op(e, fn, reads, writes, est=80.0 + _fsz(out) * 1.05)

    def memset(self, e, ap, val, writes):
        def fn(eng):
            return eng.memset(ap, val)
        return self.op(e, fn, (), writes, est=100.0 + _fsz(ap) * 1.0)


class Banks:
    def __init__(self, items):
        self.items = items
        self.i = 0

    def next(self):
        it = self.items[self.i % len(self.items)]
        self.i += 1
        return it


def build_program():
    nc = bass.Bass("TRN2", target_bir_lowering=False)

    uid = [0]

    def SB(name, shape, dt):
        uid[0] += 1
        return nc.sbuf_tensor(f"{name}_s{uid[0]}", shape, dt)

    def PS(name, shape, dt):
        uid[0] += 1
        return nc.psum_tensor(f"{name}_p{uid[0]}", shape, dt)

    def din(name, shape):
        return nc.dram_tensor(name, list(shape), F32, kind="ExternalInput").ap()

    x_in = din("x", [TOK, D])
    cg_in = din("cg", [NG, D])
    mk_in = din("mk", [128, 2 * NT])
    cst_in = din("cst", [128, 128 * 6 + 1024])
    norm_g = din("norm_g", [DEPTH, 4, D])
    w_ada = din("w_ada", [DEPTH, D, 6 * D])
    b_ada = din("b_ada", [DEPTH, 6 * D])
    gla_w_in = din("gla_w_in", [2, D, 3072])
    gla_w_gk1 = din("gla_w_gk1", [2, 2, D, 16])
    gla_w_gk2 = din("gla_w_gk2", [2, 2, 16, 512])
    gla_b_gk = din("gla_b_gk", [2, 2, 512])
    gla_g_head = din("gla_g_head", [2, 256])
    gla_w_out = din("gla_w_out", [2, D, D])
    sg_w_in = din("sg_w_in", [2, D, 2048])
    sg_b_in = din("sg_b_in", [2, 2048])
    sg_ln_g = din("sg_ln_g", [2, D])
    sg_ln_b = din("sg_ln_b", [2, D])
    sg_w_s = din("sg_w_s", [2, 4, 128, 128])
    sg_b_s = din("sg_b_s", [2, 4, 128])
    sg_w_out = din("sg_w_out", [2, D, D])
    ffn_w_in = din("ffn_w_in", [DEPTH, D, 2 * FH])
    ffn_w_out = din("ffn_w_out", [DEPTH, FH, D])
    y_out = nc.dram_tensor("y", [TOK, D], F32, kind="ExternalOutput").ap()
    xscr = nc.dram_tensor("xscr", [TOK, D], F32, kind="Internal").ap()
    modscr = nc.dram_tensor("modscr", [DEPTH, 6, NG, D], F32, kind="Internal").ap()
    sfscr = nc.dram_tensor("sfscr", [NT, 128, D], F32, kind="Internal").ap()
    kscr = nc.dram_tensor("kscr", [NT, 2, 128, 512], F32, kind="Internal").ap()
    vscr = nc.dram_tensor("vscr", [NT, 2, 128, D], BF16, kind="Internal").ap()
    pscr = nc.dram_tensor("pscr", [NT, 2, 128, D], BF16, kind="Internal").ap()

    with contextlib.ExitStack() as gstack:
        P = Prog(nc, gstack)

        stream_names = ["xa00", "xa01", "xa10", "xa11", "xr0", "xr1", "st0", "st1", "bcAS", "bcG", "w", "cb", "c0", "bada", "wa0", "wa1",
                        "sf", "sfl", "mod", "sgm", "gh", "wst0", "wst1", "w2", "w3", "w4",
                        "k0", "k1", "v0", "v1", "v2", "v3", "p0", "p1", "p2", "p3"]
        for n in stream_names:
            P.stream(n)
        P.frozen = True

        def clr(eng):
            r = None
            for s in P.allsems:
                r = eng.sem_clear(s)
            return None
        P.raw_sp(clr)
        P.flush()

        cs = gstack.enter_context
        identb = cs(SB("identb", [128, 128], BF16))
        identf = cs(SB("identf", [128, 128], F32))
        Tf = cs(SB("Tf", [128, 128], BF16))
        Tb = cs(SB("Tb", [128, 128], BF16))
        Uf = cs(SB("Uf", [128, 128], BF16))
        Ub = cs(SB("Ub", [128, 128], BF16))
        Mf = cs(SB("Mf", [128, 4, 128], BF16))
        Mb = cs(SB("Mb", [128, 4, 128], BF16))
        onesb = cs(SB("onesb", [128, 128], BF16))
        mk = cs(SB("mk", [128, 2 * NT], F32))
        mhalf = cs(SB("mhalf", [128, 8], F32))

        with contextlib.ExitStack() as st:
            al = st.enter_context
            cg = al(SB("cg", [NG, D], F32))
            scg = al(SB("scg", [NG, D], F32))
            scT = al(SB("scT", [128, 8, 8], BF16))
            Wa = [al(SB(f"Wa{i}", [128, 8, 2048], BF16)) for i in range(2)]
            modrow = al(SB("modrow", [NG, 6 * D], F32))
            bada = al(SB("bada", [NG, 6 * D], F32))
            ng6 = al(SB("ng6", [NG, 4, D], F32))
            modo = al(SB("modo", [NG, 6, D], F32))
            psC = al(PS("psC", [128, 8, 8], F32))
            psM = [al(PS(f"psM{i}", [128, 512], F32)) for i in range(2)]

            P.dma("sp", "c0", [(identf[:], cst_in[:, 0:128]), (mk[:], mk_in[:, :]),
                                 (cg[:], cg_in[:, :])],
                  writes=["identf", "mk", "cg"])
            P.dma("pool", "cb", [(identb[:], cst_in[:, 0:128]), (Tf[:], cst_in[:, 128:256]),
                                (Tb[:], cst_in[:, 256:384]), (Uf[:], cst_in[:, 384:512]),
                                (Ub[:], cst_in[:, 512:640]), (onesb[:], cst_in[:, 640:768]),
                                (Mf[:], cst_in[:, 768:1280].rearrange("p (h c) -> p h c", h=4)),
                                (Mb[:], cst_in[:, 1280:1792].rearrange("p (h c) -> p h c", h=4))],
                  writes=["cb", "Mf", "Mb"])
            P.memset("pool", mhalf[:], -0.5, ["mhalf"])
            P.act(scg[:], cg[:], AF.Silu, ["cg"], ["scg"])
            P.tr([(psC[:, kc, 0:NG], scg[0:NG, kc * 128:(kc + 1) * 128], identf[0:NG, 0:NG]) for kc in range(8)],
                 ["scg", "identf"], ["psC"])
            P.copy("act", scT[:, :, 0:NG], psC[:, :, 0:NG], ["psC"], ["scT"])
            pi = 0
            for i in range(DEPTH_RUN):
                P.dma("sp", "bada", [(bada[:], b_ada[i:i + 1, :].partition_broadcast(NG)),
                                     (ng6[:], norm_g[i:i + 1, :, :].partition_broadcast(NG))],
                      writes=["bada", "ng6"])
                for q in range(3):
                    wa = Wa[(i * 3 + q) % 2]
                    wk = f"Wa{(i * 3 + q) % 2}"
                    P.dma("pool", f"wa{(i * 3 + q) % 2}",
                          [(wa[:, :, :], w_ada[i, :, q * 2048:(q + 1) * 2048].rearrange("(kc p) n -> p kc n", p=128))], writes=[wk])
                    for n in range(4):
                        pm = psM[pi % 2]
                        pk = f"psM{pi % 2}"
                        pi += 1
                        P.mm(pm[0:NG, :], [(scT[:, kc, 0:NG], wa[:, kc, n * 512:(n + 1) * 512]) for kc in range(8)],
                             ["scT", wk], [pk])
                        c0 = q * 2048 + n * 512
                        P.tt("dve", modrow[:, c0:c0 + 512], pm[0:NG, :], bada[:, c0:c0 + 512], ALU.add,
                             [pk, "bada"], ["modrow"])
                P.stt(modo[:, 0, :], modrow[:, 1024:2048], 1.0, ng6[:, 0, :], ALU.add, ALU.mult,
                      ["modrow", "ng6"], ["modo"])
                P.copy("dve", modo[:, 1, :], modrow[:, 0:1024], ["modrow"], ["modo"])
                P.tt("dve", modo[:, 2, :], modrow[:, 2048:3072], ng6[:, 1, :], ALU.mult, ["modrow", "ng6"], ["modo"])
                P.stt(modo[:, 3, :], modrow[:, 4096:5120], 1.0, ng6[:, 2, :], ALU.add, ALU.mult,
                      ["modrow", "ng6"], ["modo"])
                P.copy("dve", modo[:, 4, :], modrow[:, 3072:4096], ["modrow"], ["modo"])
                P.tt("dve", modo[:, 5, :], modrow[:, 5120:6144], ng6[:, 3, :], ALU.mult, ["modrow", "ng6"], ["modo"])
                P.dma("sp", "mod", [(modscr[i].rearrange("k g d -> g k d"), modo[:])], reads=["modo"],
                      writes=[("modscr", i)])
            P.flush()

        def rows(t, j=None):
            if j is None:
                return slice(t * TT, (t + 1) * TT)
            return slice(t * TT + j * 128, t * TT + (j + 1) * 128)

        class Ctx:
            pass

        def rsqrt(out, acc, n, scale, keys_in, key_out):
            P.ts("dve", acc, acc, scale, EPS, ALU.mult, ALU.add, keys_in, keys_in)
            P.tt("pool", out, acc, mhalf[:, 0:n], ALU.pow, keys_in + ["mhalf"], [key_out], est=1700.0)

        def alloc_common(al, C, nY=2, inplace=True, nxa=1):
            C.inplace = inplace
            C.xas = [al(SB(f"xa{i}", [128, 2, D], F32)) for i in range(nxa)]
            C.xai = 0
            C.xa = C.xas[0]
            C.xak = "xa0"
            C.xr = [al(SB(f"xr{j}", [128, D], F32)) for j in range(2)]
            C.bcA = al(SB("bcA", [128, D], F32))
            C.bcS = al(SB("bcS", [128, D], F32))
            C.bcG = al(SB("bcG", [128, D], F32))
            C.junk = al(SB("junk", [128, D], BF16))
            C.t1 = [al(SB(f"t1_{j}", [128, D], F32)) for j in range(2)]
            C.h = [al(SB(f"h{j}", [128, D], BF16)) for j in range(2)]
            C.hTs = [al(SB(f"hT{i}", [128, 8, TT], BF16)) for i in range(2)]
            C.hTi = 0
            C.hT = C.hTs[0]
            C.hTk = "hT0"
            C.ssq = al(SB("ssq", [128, 2], F32))
            C.rstd = al(SB("rstd", [128, 2], F32))
            C.ssqy = al(SB("ssqy", [128, 2], F32))
            C.rstdy = al(SB("rstdy", [128, 2], F32))
            C.ssq2 = [al(SB(f"ssq2_{j}", [128, 2], F32)) for j in range(2)]
            C.psT = al(PS("psT", [128, 8, 128], BF16))
            C.psY = [al(PS(f"psY{j}", [128, D], F32)) for j in range(nY)]

        def load_bcAS(C, layer, sub, g):
            k0 = 3 * sub
            P.dma("sp", "bcAS", [(C.bcA[:], modscr[layer, k0, g:g + 1, :].partition_broadcast(128)),
                                 (C.bcS[:], modscr[layer, k0 + 1, g:g + 1, :].partition_broadcast(128))],
                  reads=[("modscr", layer)], writes=["bcA", "bcS"])

        def load_bcG(C, layer, sub, g):
            k0 = 3 * sub
            P.dma("sp", "bcG", [(C.bcG[:], modscr[layer, k0 + 2, g:g + 1, :].partition_broadcast(128))],
                  reads=[("modscr", layer)], writes=["bcG"])

        def front_a(C, t, xsrc):
            C.xai += 1
            bi = C.xai % len(C.xas)
            C.xa = C.xas[bi]
            for j in range(2):
                xk = f"xa{bi}_{j}"
                P.dma("sp", f"xa{bi}{j}", [(C.xa[:, j, :], xsrc[rows(t, j), :])], reads=[("xd", t, j)], writes=[xk])
                P.act(C.junk[:], C.xa[:, j, :], AF.Square, [xk], [f"ssq{j}"], accum=C.ssq[:, j:j + 1])
                rsqrt(C.rstd[:, j:j + 1], C.ssq[:, j:j + 1], 1, 1.0 / D, [f"ssq{j}"], f"rstd{j}")
                if C.inplace:
                    P.stt(C.xa[:, j, :], C.xa[:, j, :], C.rstd[:, j:j + 1], C.bcA[:], ALU.mult, ALU.mult,
                          [xk, f"rstd{j}", "bcA"], [xk])
                    P.tt("pool", C.h[j][:], C.xa[:, j, :], C.bcS[:], ALU.add, [xk, "bcS"], [f"h{j}"])
                else:
                    P.stt(C.t1[j][:], C.xa[:, j, :], C.rstd[:, j:j + 1], C.bcA[:], ALU.mult, ALU.mult,
                          [xk, f"rstd{j}", "bcA"], [f"t1_{j}"])
                    P.tt("pool", C.h[j][:], C.t1[j][:], C.bcS[:], ALU.add, [f"t1_{j}", "bcS"], [f"h{j}"])

        def front_b(C, t):
            C.hTi += 1
            C.hT = C.hTs[C.hTi % 2]
            C.hTk = f"hT{C.hTi % 2}"
            for j in range(2):
                P.tr([(C.psT[:, kc, :], C.h[j][:, kc * 128:(kc + 1) * 128], identb[:]) for kc in range(8)],
                     [f"h{j}", "cb"], ["psT"])
                P.copy("act", C.hT[:, :, j * 128:(j + 1) * 128], C.psT[:], ["psT"], [C.hTk])

        def load_xr(C, t, xsrc):
            for j in range(2):
                P.dma("sp", f"xr{j}", [(C.xr[j][:], xsrc[rows(t, j), :])], reads=[("xd", t, j)], writes=[f"xr{j}"])

        def post(C, t, j, xdst, pieces=None):
            if pieces is None:
                pieces = [(C.psY[j % len(C.psY)][:], f"psY{j % len(C.psY)}", 0, D)]
            if len(pieces) == 1:
                ap, pk, c0, ncol = pieces[0]
                P.act(C.junk[:], ap, AF.Square, [pk], ["ssqy"], accum=C.ssqy[:, j:j + 1])
            else:
                for i, (ap, pk, c0, ncol) in enumerate(pieces):
                    P.act(C.junk[:, 0:ncol], ap, AF.Square, [pk], [f"ssq2_{j}"], accum=C.ssq2[j][:, i:i + 1])
                P.tt("dve", C.ssqy[:, j:j + 1], C.ssq2[j][:, 0:1], C.ssq2[j][:, 1:2], ALU.add, [f"ssq2_{j}"], ["ssqy"])
            for ap, pk, c0, ncol in pieces:
                P.tt("dve", ap, ap, C.bcG[:, c0:c0 + ncol], ALU.mult, [pk, "bcG"], [pk])
            rsqrt(C.rstdy[:, j:j + 1], C.ssqy[:, j:j + 1], 1, 1.0 / D, ["ssqy"], "rstdy")
            for ap, pk, c0, ncol in pieces:
                P.stt(C.xr[j][:, c0:c0 + ncol], ap, C.rstdy[:, j:j + 1], C.xr[j][:, c0:c0 + ncol], ALU.mult, ALU.add,
                      [pk, "rstdy", f"xr{j}"], [f"xr{j}"])
            P.dma("sp", f"st{j}", [(xdst[rows(t, j), :], C.xr[j][:])], reads=[f"xr{j}"], writes=[("xd", t, j)])

        wkeys = {}

        def load_w(dst, src2d, nk, ncols, key, stream="w", ranges=None):
            if ranges is None:
                ranges = [(0, ncols)]
            pairs = []
            for (r0, r1) in ranges:
                c0 = r0
                while c0 < r1:
                    c1 = min(r1, c0 + 2048)
                    for k0 in range(0, nk, 8):
                        k1 = min(nk, k0 + 8)
                        pairs.append((dst[:, k0:k1, c0:c1],
                                      src2d[k0 * 128:k1 * 128, c0:c1].rearrange("(kc p) n -> p kc n", p=128)))
                    c0 = c1
            for i in range(0, len(pairs), 4):
                P.dma("pool", stream, pairs[i:i + 4], writes=[key])
            wkeys.setdefault(stream, []).append(key)

        def wfence():
            for stream, keys in wkeys.items():
                P.fence(stream, keys)
            wkeys.clear()

        phase_no = [0]

        def phase_enabled():
            phase_no[0] += 1
            return phase_no[0] <= PH_LIMIT

        def run_tiles(C, layer, sub, order, xsrc, xdst, mainA, mainB, with_post=True):
            g0 = order[0] // TPG
            load_bcAS(C, layer, sub, g0)
            if with_post:
                load_bcG(C, layer, sub, g0)
            front_a(C, order[0], xsrc)
            front_b(C, order[0])
            for idx, t in enumerate(order):
                nxt = order[idx + 1] if idx + 1 < len(order) else None
                chg = nxt is not None and nxt // TPG != t // TPG
                if with_post:
                    load_xr(C, t, xsrc)
                if nxt is not None:
                    if chg:
                        load_bcAS(C, layer, sub, nxt // TPG)
                    front_a(C, nxt, xsrc)
                hT_cur, hTk_cur = C.hT, C.hTk
                if nxt is not None:
                    front_b(C, nxt)
                    hT_nxt, hTk_nxt = C.hT, C.hTk
                    C.hT, C.hTk = hT_cur, hTk_cur
                mainA(t)
                if nxt is not None:
                    C.hT, C.hTk = hT_nxt, hTk_nxt
                mainB(t)
                if chg and with_post:
                    load_bcG(C, layer, sub, nxt // TPG)

        def ffn_phase(layer, xsrc, xdst):
            with contextlib.ExitStack() as st:
                al = st.enter_context
                C = Ctx()
                alloc_common(al, C, nY=2, inplace=True, nxa=1)
                W1 = al(SB("W1", [128, 8, 2 * FH], BF16))
                W2 = al(SB("W2", [128, 22, D], BF16))
                gT = al(SB("gT", [128, 22, TT], BF16))
                sg = [al(SB(f"sg{i}", [128, TT], F32)) for i in range(2)]
                pab = Banks([(f"pab{i}", al(PS(f"pab{i}", [128, 2, TT], F32))) for i in range(3)])
                CG = (0, 2, 11, 22)
                for g, stream in enumerate(("w", "w2", "w3")):
                    a0, a1 = CG[g] * 128, CG[g + 1] * 128
                    load_w(W1, ffn_w_in[layer], 8, 2 * FH, f"W1g{g}", stream, [(a0, a1), (FH + a0, FH + a1)])
                load_w(W2, ffn_w_out[layer], 22, D, "W2", "w4")
                wfence()

                def mainA(t):
                    for c in range(22):
                        pk, pb = pab.next()
                        P.mmg([(pb[:, 0, :], [(W1[:, kc, c * 128:(c + 1) * 128], C.hT[:, kc, :]) for kc in range(8)]),
                               (pb[:, 1, :], [(W1[:, kc, FH + c * 128:FH + (c + 1) * 128], C.hT[:, kc, :]) for kc in range(8)])],
                              ["W1g0" if c < 2 else ("W1g1" if c < 11 else "W1g2"), C.hTk], [pk])
                        s = sg[c % 2]
                        P.act(s[:], pb[:, 0, :], AF.Silu, [pk], [f"sg{c % 2}"])
                        P.tt("dve", gT[:, c, :], s[:], pb[:, 1, :], ALU.mult, [pk, f"sg{c % 2}"], ["gT"])

                def mainB(t):
                    for j in range(2):
                        P.mmg([(C.psY[j][:, f * 512:(f + 1) * 512],
                                [(gT[:, c, j * 128:(j + 1) * 128], W2[:, c, f * 512:(f + 1) * 512]) for c in range(22)])
                               for f in range(2)], ["gT", "W2"], [f"psY{j}"])
                        post(C, t, j, xdst)

                run_tiles(C, layer, 1, list(range(NT)), xsrc, xdst, mainA, mainB)
                P.flush()

        def sgu_phase(layer, l2, xsrc, xdst):
            with contextlib.ExitStack() as st:
                al = st.enter_context
                C = Ctx()
                alloc_common(al, C, nY=1, inplace=True, nxa=2)
                Wi = al(SB("Wi", [128, 8, 2048], BF16))
                Wo = al(SB("Wo", [128, 8, D], BF16))
                binb = al(SB("binb", [1, 2048], BF16))
                wsn = al(SB("wsn", [128, 4, 128], F32))
                WsT = al(SB("WsT", [128, 4, 128], BF16))
                bs = al(SB("bs", [128, 4], F32))
                lng = al(SB("lng", [128, D], F32))
                lnb = al(SB("lnb", [128, D], F32))
                u = [al(SB(f"u{j}", [128, D], F32)) for j in range(2)]
                vr = [al(SB(f"vr{j}", [128, D], F32)) for j in range(2)]
                vn = [al(SB(f"vn{j}", [128, D], F32)) for j in range(2)]
                vb = [al(SB(f"vb{j}", [128, D], BF16)) for j in range(2)]
                m = [al(SB(f"m{j}", [128, D], BF16)) for j in range(2)]
                mT = al(SB("mT", [128, 8, 128], BF16))
                bst = al(SB("bst", [128, 12], F32))
                mv = al(SB("mv", [128, 2], F32))
                lrs = al(SB("lrs", [128, 1], F32))
                lnm = al(SB("lnm", [128, 1], F32))
                pz = Banks([(f"pz{i}", al(PS(f"pz{i}", [128, 512], F32))) for i in range(3)])
                psV = al(PS("psV", [128, D], F32))

                P.dma("pool", "w", [(binb[:], sg_b_in[l2:l2 + 1, :])], writes=["binb"])
                wkeys.setdefault("w", []).append("binb")
                load_w(Wi, sg_w_in[l2], 8, 2048, "Wi", "w")
                load_w(Wo, sg_w_out[l2], 8, D, "Wo", "w2")
                wfence()
                P.dma("sp", "sgm", [(wsn[:], sg_w_s[l2].rearrange("g t s -> t g s")),
                                    (lng[:], sg_ln_g[l2:l2 + 1, :].partition_broadcast(128)),
                                    (lnb[:], sg_ln_b[l2:l2 + 1, :].partition_broadcast(128))]
                      + [(bs[:, g:g + 1], sg_b_s[l2, g, :].rearrange("(p o) -> p o", o=1)) for g in range(4)],
                      writes=["wsn", "lng", "lnb", "bs"])
                pk, pb = pz.next()
                P.tr([(pb[:, g * 128:(g + 1) * 128], wsn[:, g, :], identf[:]) for g in range(4)],
                     ["wsn", "identf"], [pk])
                P.copy("act", WsT[:].rearrange("p g t -> p (g t)"), pb[:], [pk], ["WsT"])

                def mainA(t):
                    for j in range(2):
                        for q in range(4):
                            pk, pb = pz.next()
                            prs = [(C.hT[:, kc, j * 128:(j + 1) * 128], Wi[:, kc, q * 512:(q + 1) * 512]) for kc in range(8)]
                            prs.append((onesb[0:1, :], binb[0:1, q * 512:(q + 1) * 512]))
                            P.mm(pb[:], prs, [C.hTk, "Wi", "binb", "cb"], [pk])
                            if q < 2:
                                P.act(u[j][:, q * 512:(q + 1) * 512], pb[:], AF.Gelu_apprx_tanh, [pk], [f"u{j}"])
                            else:
                                P.act(vr[j][:, (q - 2) * 512:(q - 1) * 512], pb[:], AF.Gelu_apprx_tanh, [pk], [f"vr{j}"])

                def mainB(t):
                    for j in range(2):
                        def bn(eng, j=j):
                            eng.bn_stats(out=bst[:, 0:6], in_=vr[j][:, 0:512])
                            return eng.bn_stats(out=bst[:, 6:12], in_=vr[j][:, 512:1024])
                        P.op("dve", bn, [f"vr{j}"], ["bst"])

                        def bna(eng):
                            return eng.bn_aggr(out=mv[:], in_=bst[:])
                        P.op("dve", bna, ["bst"], ["mv"])
                        rsqrt(lrs[:], mv[:, 1:2], 1, 1.0, ["mv"], "lrs")
                        P.stt(lnm[:], mv[:, 0:1], -1.0, lrs[:], ALU.mult, ALU.mult, ["mv", "lrs"], ["lnm"])
                        P.act(vn[j][:], vr[j][:], AF.Identity, [f"vr{j}", "lrs", "lnm"], [f"vn{j}"], scale=lrs[:, 0:1], bias=lnm[:, 0:1])
                        P.tt("dve", vn[j][:], vn[j][:], lng[:], ALU.mult, [f"vn{j}", "lng"], [f"vn{j}"])
                        P.tt("pool", vb[j][:], vn[j][:], lnb[:], ALU.add, [f"vn{j}", "lnb"], [f"vb{j}"])
                        P.mmg([(psV[:, g * 256:(g + 1) * 256], [(WsT[:, g, :], vb[j][:, g * 256:(g + 1) * 256])]) for g in range(4)],
                              ["WsT", f"vb{j}"], ["psV"])
                        for g in range(4):
                            P.stt(m[j][:, g * 256:(g + 1) * 256], psV[:, g * 256:(g + 1) * 256], bs[:, g:g + 1],
                                  u[j][:, g * 256:(g + 1) * 256], ALU.add, ALU.mult, ["psV", "bs", f"u{j}"], [f"m{j}"])
                        P.tr([(C.psT[:, kc, :], m[j][:, kc * 128:(kc + 1) * 128], identb[:]) for kc in range(8)],
                             [f"m{j}", "cb"], ["psT"])
                        P.copy("act", mT[:], C.psT[:], ["psT"], ["mT"])
                        P.mmg([(C.psY[0][:, f * 512:(f + 1) * 512],
                                [(mT[:, kc, :], Wo[:, kc, f * 512:(f + 1) * 512]) for kc in range(8)])
                               for f in range(2)], ["mT", "Wo"], ["psY0"])
                        post(C, t, j, xdst)

                run_tiles(C, layer, 0, list(range(NT)), xsrc, xdst, mainA, mainB)
                P.flush()

        def gla_phase(layer, l2, xsrc, xdst):
            with contextlib.ExitStack() as st:
                al = st.enter_context
                C = Ctx()
                alloc_common(al, C, nY=0, inplace=(os.environ.get("MK_GI", "0") == "1"), nxa=1)
                Wi = al(SB("Wi", [128, 8, 3072], BF16))
                Wo = al(SB("Wo", [128, 8, D], BF16))
                wstage = C.t1
                gh = al(SB("gh", [128, 2], F32))
                Wg1 = al(SB("Wg1", [128, 8, 32], BF16))
                Wg2 = al(SB("Wg2", [33, D], BF16))
                g1Ts = [al(SB(f"g1T{i}", [33, TT], BF16)) for i in range(2)]
                etmp = [al(SB(f"etmp{i}", [128, 512], F32)) for i in range(2)]
                Pm = [al(SB(f"Pm{j}", [128, D], BF16)) for j in range(4)]
                qTs = [al(SB(f"qT{i}", [128, 4, TT], F32)) for i in range(2)]
                kTs = [al(SB(f"kT{i}", [128, 4, TT], F32)) for i in range(2)]
                ktok = [al(SB(f"ktok{j}", [128, 512], F32)) for j in range(2)]
                vtok = [al(SB(f"vtok{j}", [128, D], BF16)) for j in range(4)]
                sr = [al(SB(f"sr{j}", [128, D], BF16)) for j in range(4)]
                decs = al(SB("decs", [128, 2, 2, 4], F32))
                qd = [[al(SB(f"qd{d}{j}", [128, 4, 128], BF16)) for j in range(2)] for d in range(2)]
                kd = [al(SB(f"kd{d}", [128, 4, 128], BF16)) for d in range(2)]
                kend = [[al(SB(f"kend{d}{j}", [128, 512], BF16)) for j in range(2)] for d in range(2)]
                scm = [[al(SB(f"scm{d}{j}", [128, 4, 128], BF16)) for j in range(2)] for d in range(2)]
                Sf = al(SB("Sf", [128, D], F32))
                Sb = al(SB("Sb", [128, D], F32))
                Sfb = [al(SB(f"Sfb{j}", [128, D], BF16)) for j in range(2)]
                Sbb = [al(SB(f"Sbb{j}", [128, D], BF16)) for j in range(2)]
                dec = al(SB("dec", [128, 4], F32))
                ssqo = al(SB("ssqo", [128, 4], F32))
                rstdo = al(SB("rstdo", [128, 4], F32))
                mm_ = al(SB("mm_", [128, D], BF16))
                mT = al(SB("mT", [128, 8, 128], BF16))
                pgbanks = [(f"pg{i}", al(PS(f"pg{i}", [128, 512], F32))) for i in range(7)]
                pools = {1: (Banks(pgbanks[:5]), Banks(pgbanks[5:])), 2: (Banks(pgbanks[:2]), Banks(pgbanks[2:]))}
                pgA, pgB = pools[1]
                pgsel = [pgA]

                class _PG:
                    def next(self):
                        return pgsel[0].next()
                pg = _PG()

                P.dma("pool", "w", [(Wg1[:, :, e * 16:(e + 1) * 16], gla_w_gk1[l2, e].rearrange("(kc p) r -> p kc r", p=128))
                                    for e in range(2)], writes=["Wg1"])
                P.memset("pool", Wg2[:], 0.0, ["Wg2"])
                P.dma("pool", "w", [(Wg2[0:16, 0:512], gla_w_gk2[l2, 0]), (Wg2[16:32, 512:1024], gla_w_gk2[l2, 1]),
                                    (Wg2[32:33, :], gla_b_gk[l2:l2 + 1].rearrange("o e k -> o (e k)"))],
                      reads=["Wg2"], writes=["Wg2"])
                wkeys.setdefault("w", []).extend(["Wg1", "Wg2"])
                load_w(Wi, gla_w_in[l2], 8, 3072, "Wia", "w2", [(0, 2048)])
                load_w(Wi, gla_w_in[l2], 8, 3072, "Wib", "w3", [(2048, 3072)])
                wfence()
                P.dma("sp", "gh", [(gh[:, b:b + 1], gla_g_head[l2, b * 128:(b + 1) * 128].rearrange("(p o) -> p o", o=1))
                                   for b in range(2)], writes=["gh"])
                for kc in range(8):
                    ws = wstage[kc % 2]
                    P.dma("sp", f"wst{kc % 2}", [(ws[:], gla_w_out[l2, kc * 128:(kc + 1) * 128, :])], writes=[f"t1_{kc % 2}"])
                    P.act(Wo[:, kc, :], ws[:], AF.Copy, [f"t1_{kc % 2}", "gh"], ["Wo"], scale=gh[:, (kc % 2):(kc % 2) + 1])
                for i in range(2):
                    P.memset("pool", g1Ts[i][:], 1.0, [f"g1T{i}"])

                def VJ(t, j):
                    return j + 2 * (t % 2)

                def gates(t, ndir):
                    pk, pb = pg.next()
                    g1T = g1Ts[t % 2]
                    gk = f"g1T{t % 2}"
                    P.mm(pb[0:32, 0:TT], [(Wg1[:, kc, :], C.hT[:, kc, :]) for kc in range(8)], ["Wg1", C.hTk], [pk])
                    P.copy("act", g1T[0:32, :], pb[0:32, 0:TT], [pk], [gk])
                    for j in range(2):
                        for d in range(ndir):
                            pk, pb = pg.next()
                            P.mm(pb[:], [(g1T[0:33, j * 128:(j + 1) * 128], Wg2[0:33, d * 512:(d + 1) * 512])],
                                 [gk, "Wg2"], [pk])
                            P.act(pb[:], pb[:], AF.Exp, [pk], [pk], scale=-1.0)
                            P.act(Pm[VJ(t, j)][:, d * 512:(d + 1) * 512], pb[:], AF.Ln, [pk], [f"Pm{VJ(t, j)}"], bias=1.0)

                eti = [0]

                def proj_tok(t, j, c0, ncol, dst, dkey, func=AF.Copy, scale=1.0):
                    for n in range(ncol // 512):
                        pk, pb = pg.next()
                        P.mm(pb[:], [(C.hT[:, kc, j * 128:(j + 1) * 128], Wi[:, kc, c0 + n * 512:c0 + (n + 1) * 512]) for kc in range(8)],
                             [C.hTk, "Wia" if c0 < 2048 else "Wib"], [pk])
                        if func == AF.Silu:
                            eti[0] += 1
                            et = etmp[eti[0] % 2]
                            ek = f"etmp{eti[0] % 2}"
                            if os.environ.get("MK_SILU", "dve") == "act":
                                P.act(et[:], pb[:], AF.Exp, [pk], [ek], scale=-1.0)
                                P.act(et[:], et[:], AF.Ln, [ek], [ek], bias=1.0)
                                P.act(et[:], et[:], AF.Exp, [ek], [ek], scale=-1.0)
                            else:
                                P.act(et[:], pb[:], AF.Exp, [pk], [ek], scale=-1.0)
                                P.ts("dve", et[:], et[:], 1.0, None, ALU.add, ALU.bypass, [ek], [ek])
                                P.recip(et[:], et[:], [ek], [ek])
                            P.tt("dve", dst[:, n * 512:(n + 1) * 512], pb[:], et[:], ALU.mult, [pk, ek], [dkey])
                        else:
                            P.act(dst[:, n * 512:(n + 1) * 512], pb[:], func, [pk], [dkey], scale=scale)

                def kend_dir(d, j, U, t):
                    pk, pb = pg.next()
                    P.mm(pb[:], [(U[:], Pm[VJ(t, j)][:, d * 512:(d + 1) * 512])], ["cb", f"Pm{VJ(t, j)}"], [pk])
                    P.act(pb[:], pb[:], AF.Exp, [pk], [pk], scale=-1.0 / 16)
                    P.tt("dve", kend[d][j][:], ktok[j][:], pb[:], ALU.mult, [f"ktok{j}", pk], [f"kend{d}{j}"])

                def state_update(S, skey, kd_, kkey, j, decap, deckeys, out_bf=None, okey=None):
                    for hh in range(2):
                        pk, pb = pg.next()
                        P.mmg([(pb[:, i * 256:(i + 1) * 256],
                                [(kd_[:, (2 * hh + i) * 128:(2 * hh + i + 1) * 128], vtok[j][:, (2 * hh + i) * 256:(2 * hh + i + 1) * 256])])
                               for i in range(2)], [kkey, f"vtok{j}"], [pk])
                        for i in range(2):
                            hd = 2 * hh + i
                            dst = S if out_bf is None else out_bf
                            dk = skey if out_bf is None else okey
                            P.stt(dst[:, hd * 256:(hd + 1) * 256], S[:, hd * 256:(hd + 1) * 256], decap(hd),
                                  pb[:, i * 256:(i + 1) * 256], ALU.mult, ALU.add, [skey, pk] + deckeys, [dk])

                if phase_enabled():
                    P.memset("dve", Sf[:], 0.0, ["Sf"])

                    def p1A(t):
                        pgsel[0] = pools[1][0]
                        if t % TPG == 0:
                            P.ts("dve", Sf[:], Sf[:], mk[:, t:t + 1], None, ALU.mult, ALU.bypass, ["Sf", "mk"], ["Sf"])
                        P.dma("sp", "sf", [(sfscr[t], Sf[:])], reads=["Sf"], writes=[("sfs", t)])
                        gates(t, 2)
                        for j in range(2):
                            vj = VJ(t, j)
                            proj_tok(t, j, 512, 512, ktok[j], f"ktok{j}")
                            proj_tok(t, j, 1024, 1024, vtok[vj], f"vtok{vj}")
                            P.dma("sp", f"k{j}", [(kscr[t, j], ktok[j][:])], reads=[f"ktok{j}"], writes=[("ks", t, j)])
                            P.dma("sp", f"v{vj}", [(vscr[t, j], vtok[vj][:])], reads=[f"vtok{vj}"], writes=[("vs", t, j)])
                            P.dma("sp", f"p{vj}", [(pscr[t, j], Pm[vj][:])], reads=[f"Pm{vj}"], writes=[("pss", t, j)])

                    def p1B(t):
                        pgsel[0] = pools[1][1]
                        for j in range(2):
                            kend_dir(0, j, Uf, t)
                            pk, pb = pg.next()
                            P.mmg([(pb[:, 2 * hd:2 * hd + 2], [(Pm[VJ(t, j)][:, hd * 128:(hd + 1) * 128], onesb[:, 0:2])]) for hd in range(4)],
                                  [f"Pm{VJ(t, j)}", "cb"], [pk])
                            P.act(dec[:], pb[:, 0:8].rearrange("p (h two) -> p h two", two=2)[:, :, 0], AF.Exp, [pk], ["dec"], scale=-1.0 / 16)
                            state_update(Sf, "Sf", kend[0][j], f"kend0{j}", VJ(t, j), lambda hd: dec[:, hd:hd + 1], ["dec"])

                    run_tiles(C, layer, 0, list(range(NT)), xsrc, xdst, p1A, p1B, with_post=False)
                    P.flush()

                if phase_enabled():
                    P.memset("dve", Sb[:], 0.0, ["Sb"])
                    P.memset("pool", Sbb[0][:], 0.0, ["Sbb0"])
                    sbi = [0]

                    def p2A(t):
                        pgsel[0] = pools[2][0]
                        for j in range(2):
                            vj = VJ(t, j)
                            P.dma("sp", f"v{vj}", [(vtok[vj][:], vscr[t, j])], reads=[("vs", t, j)], writes=[f"vtok{vj}"])
                            P.dma("sp", f"p{vj}", [(Pm[vj][:], pscr[t, j])], reads=[("pss", t, j)], writes=[f"Pm{vj}"])
                            P.dma("sp", f"k{j}", [(ktok[j][:], kscr[t, j])], reads=[("ks", t, j)], writes=[f"ktok{j}"])
                        for j in range(2):
                            proj_tok(t, j, 2048, 1024, sr[VJ(t, j)], f"sr{VJ(t, j)}", func=AF.Silu)
                        for dst, dkey, c0, sc in ((qTs[t % 2], f"qT{t % 2}", 0, 128.0 ** -0.5), (kTs[t % 2], f"kT{t % 2}", 512, 1.0)):
                            for hh in range(2):
                                pk, pb = pg.next()
                                P.mmg([(pb[:, i * TT:(i + 1) * TT],
                                        [(Wi[:, kc, c0 + (2 * hh + i) * 128:c0 + (2 * hh + i + 1) * 128], C.hT[:, kc, :]) for kc in range(8)])
                                       for i in range(2)], ["Wia", C.hTk], [pk])
                                P.act(dst[:, 2 * hh:2 * hh + 2, :], pb[:].rearrange("p (i c) -> p i c", i=2), AF.Copy, [pk], [dkey], scale=sc)

                    def p2B(t):
                        pgsel[0] = pools[2][1]
                        P.dma("sp", "sfl", [(Sf[:], sfscr[t])], reads=[("sfs", t)], writes=["Sf"])
                        if t % TPG == TPG - 1:
                            P.ts("dve", Sb[:], Sb[:], mk[:, NT + t:NT + t + 1], None, ALU.mult, ALU.bypass, ["Sb", "mk"], ["Sb"])
                            sbi[0] += 1
                            P.copy("act", Sbb[sbi[0] % 2][:], Sb[:], ["Sb"], [f"Sbb{sbi[0] % 2}"])
                        P.copy("act", Sfb[0][:], Sf[:], ["Sf"], ["Sfb0"])
                        for j in range(2):
                            for d, (Tm, Um, Mm) in enumerate(((Tf, Uf, Mf), (Tb, Ub, Mb))):
                                pk, pb = pg.next()
                                P.mmg([(pb[:, hd * 128:(hd + 1) * 128], [(Pm[VJ(t, j)][:, d * 512 + hd * 128:d * 512 + (hd + 1) * 128], Tm[:])])
                                       for hd in range(4)], [f"Pm{VJ(t, j)}", "cb"], [pk])
                                pbv = pb[:].rearrange("p (h c) -> p h c", h=4)
                                pk2, pb2 = pg.next()
                                pbv2 = pb2[:].rearrange("p (h c) -> p h c", h=4)
                                P.act(pbv2, pbv, AF.Exp, [pk], [pk2], scale=1.0 / 16)
                                P.act(pbv, pbv, AF.Exp, [pk], [pk], scale=-1.0 / 16)
                                col = 127 if d == 0 else 0
                                P.act(decs[:, d, j, :], pbv[:, :, col], AF.Copy, [pk], [f"decs{d}{j}"])
                                P.tt("dve", qd[d][j][:], qTs[t % 2][:, :, j * 128:(j + 1) * 128], pbv, ALU.mult, [f"qT{t % 2}", pk], [f"qd{d}{j}"])
                                P.tt("dve", kd[d][:], kTs[t % 2][:, :, j * 128:(j + 1) * 128], pbv2, ALU.mult, [f"kT{t % 2}", pk2], [f"kd{d}"])
                                kend_dir(d, j, Um, t)
                                pk, pb = pg.next()
                                P.mmg([(pb[:, hd * 128:(hd + 1) * 128], [(kd[d][:, hd, :], qd[d][j][:, hd, :])]) for hd in range(4)],
                                      [f"kd{d}", f"qd{d}{j}"], [pk])
                                P.tt("dve", scm[d][j][:], pb[:].rearrange("p (h c) -> p h c", h=4), Mm[:], ALU.mult, [pk, "Mf", "Mb"], [f"scm{d}{j}"])
                        state_update(Sf, "Sf", kend[0][0], "kend00", VJ(t, 0), lambda hd: decs[:, 0, 0, hd:hd + 1], ["decs00"],
                                     out_bf=Sfb[1], okey="Sfb1")
                        for j in (1, 0):
                            cur = Sbb[sbi[0] % 2]
                            ck = f"Sbb{sbi[0] % 2}"
                            vv = vtok[VJ(t, j)]
                            obanks = []
                            for hh in range(2):
                                pk, pb = pg.next()
                                obanks.append((pk, pb))
                                P.mmg([(pb[:, i * 256:(i + 1) * 256],
                                        [(scm[0][j][:, 2 * hh + i, :], vv[:, (2 * hh + i) * 256:(2 * hh + i + 1) * 256]),
                                         (scm[1][j][:, 2 * hh + i, :], vv[:, (2 * hh + i) * 256:(2 * hh + i + 1) * 256]),
                                         (qd[0][j][:, 2 * hh + i, :], Sfb[j][:, (2 * hh + i) * 256:(2 * hh + i + 1) * 256]),
                                         (qd[1][j][:, 2 * hh + i, :], cur[:, (2 * hh + i) * 256:(2 * hh + i + 1) * 256])]) for i in range(2)],
                                      [f"scm0{j}", f"scm1{j}", f"vtok{VJ(t, j)}", f"qd0{j}", f"qd1{j}", f"Sfb{j}", ck], [pk])
                            state_update(Sb, "Sb", kend[1][j], f"kend1{j}", VJ(t, j), lambda hd, j=j: decs[:, 1, j, hd:hd + 1], [f"decs1{j}"])
                            sbi[0] += 1
                            P.copy("act", Sbb[sbi[0] % 2][:], Sb[:], ["Sb"], [f"Sbb{sbi[0] % 2}"])
                            for hd in range(4):
                                pk, pb = obanks[hd // 2]
                                P.act(C.junk[:, 0:256], pb[:, (hd % 2) * 256:(hd % 2 + 1) * 256], AF.Square, [pk], ["ssqo"], accum=ssqo[:, hd:hd + 1])
                            rsqrt(rstdo[:], ssqo[:], 4, 1.0 / 256, ["ssqo"], "rstdo")
                            for hd in range(4):
                                pk, pb = obanks[hd // 2]
                                P.stt(mm_[:, hd * 256:(hd + 1) * 256], pb[:, (hd % 2) * 256:(hd % 2 + 1) * 256], rstdo[:, hd:hd + 1],
                                      sr[VJ(t, j)][:, hd * 256:(hd + 1) * 256], ALU.mult, ALU.mult, [pk, "rstdo", f"sr{VJ(t, j)}"], ["mm_"])
                            P.tr([(C.psT[:, kc, :], mm_[:, kc * 128:(kc + 1) * 128], identb[:]) for kc in range(8)],
                                 ["mm_", "cb"], ["psT"])
                            P.copy("act", mT[:], C.psT[:], ["psT"], ["mT"])
                            pieces = []
                            for f in range(2):
                                pk, pb = pg.next()
                                P.mm(pb[:], [(mT[:, kc, :], Wo[:, kc, f * 512:(f + 1) * 512]) for kc in range(8)], ["mT", "Wo"], [pk])
                                pieces.append((pb[:], pk, f * 512, 512))
                            post(C, t, j, xdst, pieces)

                    run_tiles(C, layer, 0, list(range(NT - 1, -1, -1)), xsrc, xdst, p2A, p2B)
                    P.flush()

        cur = x_in
        for layer in range(DEPTH_RUN):
            last = layer == DEPTH_RUN - 1
            if layer % 2 == 0:
                gla_phase(layer, layer // 2, cur, xscr)
            else:
                if phase_enabled():
                    sgu_phase(layer, layer // 2, cur, xscr)
            cur = xscr
            if phase_enabled():
                ffn_phase(layer, cur, y_out if last else xscr)
    return nc


_NC_CACHE = {}


def _consts():
    s = np.arange(128)[:, None]
    c = np.arange(128)[None, :]
    ident = np.eye(128, dtype=np.float32)
    tf = (s <= c).astype(np.float32)
    tb = (s >= c).astype(np.float32)
    uf = (s > c).astype(np.float32)
    ub = (s < c).astype(np.float32)
    ones = np.ones((128, 128), np.float32)
    mf = np.tile(tf, (1, 4))
    mb = np.tile(uf, (1, 4))
    return np.ascontiguousarray(np.concatenate([ident, tf, tb, uf, ub, ones, mf, mb], axis=1))


def kernel(x_prompt, x_sample, c_prompt, c_sample, norm_g, w_ada, b_ada,
           gla_w_in, gla_w_gk1, gla_w_gk2, gla_b_gk, gla_g_head, gla_w_out,
           sg_w_in, sg_b_in, sg_ln_g, sg_ln_b, sg_w_s, sg_b_s, sg_w_out,
           ffn_w_in, ffn_w_out):
    f = lambda a: np.ascontiguousarray(np.asarray(a, dtype=np.float32))
    x_prompt, x_sample, c_prompt, c_sample = f(x_prompt), f(x_sample), f(c_prompt), f(c_sample)
    shared = dict(norm_g=f(norm_g), w_ada=f(w_ada), b_ada=f(b_ada), gla_w_in=f(gla_w_in),
                  gla_w_gk1=f(gla_w_gk1), gla_w_gk2=f(gla_w_gk2), gla_b_gk=f(gla_b_gk),
                  gla_g_head=f(gla_g_head), gla_w_out=f(gla_w_out), sg_w_in=f(sg_w_in),
                  sg_b_in=f(sg_b_in), sg_ln_g=f(sg_ln_g), sg_ln_b=f(sg_ln_b), sg_w_s=f(sg_w_s),
                  sg_b_s=f(sg_b_s), sg_w_out=f(sg_w_out), ffn_w_in=f(ffn_w_in), ffn_w_out=f(ffn_w_out),
                  cst=_consts())
    plan = []
    for r in range(4):
        plan.append([("s", r), ("p", 2 * r), ("p", 2 * r + 1)])
    for r in range(4, 8):
        plan.append([("p", 8 + (r - 4) * 6 + k) for k in range(6)])
    in_maps = []
    for r in range(NCORES):
        xs, cgs = [], []
        mkf = np.ones(NT, np.float32)
        mkb = np.ones(NT, np.float32)
        tpos = 0
        for kind, idx in plan[r]:
            if kind == "s":
                xs.append(x_sample[idx]); ng = 4; cv = c_sample[idx]
            else:
                xs.append(x_prompt[idx]); ng = 1; cv = c_prompt[idx]
            for _ in range(ng):
                cgs.append(cv)
            mkf[tpos] = 0.0
            tpos += ng * TPG
            mkb[tpos - 1] = 0.0
        xc = np.ascontiguousarray(np.concatenate(xs, axis=0))
        cg = np.ascontiguousarray(np.stack(cgs, axis=0))
        mk = np.ascontiguousarray(np.tile(np.concatenate([mkf, mkb])[None, :], (128, 1)).astype(np.float32))
        m = dict(shared)
        m.update(x=xc, cg=cg, mk=mk)
        in_maps.append(m)
    if "nc" not in _NC_CACHE:
        _NC_CACHE["nc"] = build_program()
    nc = _NC_CACHE["nc"]
    res = run_bass_kernel_spmd(nc, in_maps, core_ids=list(range(NCORES)))
    y_prompt = np.empty_like(x_prompt)
    y_sample = np.empty_like(x_sample)
    for r in range(NCORES):
        y = np.asarray(res.results[r]["y"], dtype=np.float32)
        pos = 0
        for kind, idx in plan[r]:
            if kind == "s":
                y_sample[idx] = y[pos:pos + 8192]; pos += 8192
            else:
                y_prompt[idx] = y[pos:pos + 2048]; pos += 2048
    return (y_prompt, y_sample)
```

```python
import os
import contextlib
import numpy as np
import concourse.bass as bass
import concourse.mybir as mybir
from concourse.bass_utils import run_bass_kernel_spmd

F32 = mybir.dt.float32
BF16 = mybir.dt.bfloat16
AF = mybir.ActivationFunctionType
ALU = mybir.AluOpType

D = 1024
NCORES = 8
TOK = 12288
TT = 256
NT = TOK // TT
TPG = 8
NG = NT // TPG
DEPTH = 4
FH = 2816
EPS = 1e-6
DEPTH_RUN = int(os.environ.get("MK_DEPTH", "4"))
PH_LIMIT = int(os.environ.get("MK_PHASES", "99"))


def _fsz(ap):
    n = 1
    for d in list(ap.shape)[1:]:
        n *= int(d)
    return n


SCHED_WINDOW = int(os.environ.get("MK_WINDOW", "96"))
VERB = bool(os.environ.get("MK_VERBOSE"))
STALL = {}


class Prog:
    ENGS = ("pe", "act", "dve", "pool", "sp")

    def __init__(self, nc, stack):
        self.nc = nc
        self.stack = stack
        self.esem = {e: stack.enter_context(nc.semaphore(f"es_{e}"))
                     for e in ("pe", "act", "dve", "pool")}
        self.ecnt = {e: 0 for e in self.esem}
        self.dsem = {}
        self.allsems = list(self.esem.values())
        self.waited = {e: {} for e in self.ENGS}
        self.semobj = {}
        for e, s in self.esem.items():
            self.semobj[id(s)] = s
        self.frozen = False
        self._reset()

    def _reset(self):
        self.ops = []
        self.lastw = {}
        self.readers = {}
        self.last_stream_op = {}

    def stream(self, name):
        if name not in self.dsem:
            assert not self.frozen, name
            s = self.stack.enter_context(self.nc.semaphore(f"ds_{name}"))
            self.dsem[name] = [s, 0]
            self.allsems.append(s)
            self.semobj[id(s)] = s
        return self.dsem[name]

    def _record(self, e, fn, reads, writes, kind, est, stream=None, ndma=0, comp=None):
        idx = len(self.ops)
        deps = {}
        for k in reads:
            w = self.lastw.get(k)
            if w is not None:
                deps[w] = True
            if isinstance(k, str) and k.startswith(("pg", "pab", "pz", "ps")):
                for r in self.readers.get(k, ()):
                    if self.ops[r]["e"] != e:
                        deps.setdefault(r, False)
        for k in writes:
            w = self.lastw.get(k)
            if w is not None:
                deps.setdefault(w, False)
            for r in self.readers.get(k, ()):
                deps.setdefault(r, False)
        order = []
        if stream is not None:
            p = self.last_stream_op.get(stream)
            if p is not None:
                order.append(p)
            self.last_stream_op[stream] = idx
        self.ops.append(dict(e=e, fn=fn, deps=deps, order=order, kind=kind, est=est,
                             stream=stream, ndma=ndma, comp=comp if comp is not None else est,
                             tag="%s>%s" % (",".join(str(k) for k in reads)[:40], ",".join(str(k) for k in writes)[:30])))
        for k in reads:
            self.readers.setdefault(k, set()).add(idx)
        for k in writes:
            self.lastw[k] = idx
            self.readers[k] = set()
        return idx

    def op(self, e, fn, reads=(), writes=(), est=300.0):
        return self._record(e, fn, reads, writes, "c", est)

    def dma(self, qe, stream, pairs, reads=(), writes=(), **kw):
        self.stream(stream)
        nbytes = 0
        for o, i in pairs:
            try:
                nbytes += int(o.nbytes)
            except Exception:
                nbytes += 4 * _fsz(o) * int(o.shape[0])

        def fn(eng, pairs=pairs, kw=kw):
            return [eng.dma_start(out=o, in_=i, **kw) for (o, i) in pairs]
        return self._record(qe, fn, reads, writes, "d", 60.0 * len(pairs), stream=stream, ndma=len(pairs),
                            comp=2500.0 + nbytes / 120.0)

    def fence(self, stream, keys):
        p = self.last_stream_op[stream]
        for k in keys:
            self.lastw[k] = p
            self.readers[k] = set()

    def _schedule(self):
        ops = self.ops
        n = len(ops)
        succ = [[] for _ in range(n)]
        nun = [0] * n
        ready = [0.0] * n
        start = [0.0] * n
        finish = [0.0] * n
        for i, o in enumerate(ops):
            ds = set(o["deps"].keys()) | set(o["order"])
            nun[i] = len(ds)
            for d in ds:
                succ[d].append(i)
        pend = {e: [i for i in range(n) if ops[i]["e"] == e] for e in self.ENGS}
        free = {e: 0.0 for e in self.ENGS}
        sched = {e: [] for e in self.ENGS}
        left = n
        W = SCHED_WINDOW
        while left:
            best = None
            for e in self.ENGS:
                pl = pend[e]
                fe = free[e]
                for pos in range(min(W, len(pl))):
                    i = pl[pos]
                    if nun[i]:
                        continue
                    st = ready[i] if ready[i] > fe else fe
                    if best is None or st < best[0] or (st == best[0] and i < best[1]):
                        best = (st, i, e, pos)
                    if st <= fe:
                        break
            assert best is not None, "scheduler deadlock"
            st, i, e, pos = best
            pend[e].pop(pos)
            o = ops[i]
            if VERB and st > free[e] + 1.0 and "blk" in o:
                key = (e, "", ops[o["blk"]]["e"], ops[o["blk"]].get("tag", "?"))
                STALL[key] = STALL.get(key, 0.0) + (st - free[e])
            start[i] = st
            free[e] = st + o["est"]
            finish[i] = st + o["comp"]
            sched[e].append(i)
            left -= 1
            for s in succ[i]:
                nun[s] -= 1
                so = ops[s]
                raw = so["deps"].get(i)
                if raw is None:
                    t = start[i]
                elif o["kind"] == "c" and so["kind"] == "c" and so["e"] == e and (e == "pe" or not raw):
                    t = free[e]
                else:
                    t = finish[i] + 60.0
                if t > ready[s]:
                    ready[s] = t
                    so["blk"] = i
        if VERB:
            top = sorted([kv for kv in STALL.items() if kv[0][0] == "pe"], key=lambda kv: -kv[1])[:12]
            for k, v in top:
                print("   stall %.0f us: %s" % (v / 1e3, k))
            STALL.clear()
            busy = {e: sum(ops[i]["est"] for i in sched[e]) / 1e3 for e in self.ENGS}
            print("block: n=%d makespan=%.1f us busy(us)=%s" % (n, max(finish) / 1e3 if n else 0.0,
                  {e: round(v) for e, v in busy.items()}), flush=True)
        return sched

    def flush(self):
        nc = self.nc
        ops = self.ops
        sched = self._schedule()
        tok = {}
        for e in self.ENGS:
            for i in sched[e]:
                o = ops[i]
                if o["kind"] == "x":
                    o["inc"] = None
                elif o["kind"] == "c":
                    self.ecnt[e] += 1
                    tok[i] = (id(self.esem[e]), self.ecnt[e])
                    o["inc"] = (self.esem[e], 1)
                else:
                    st = self.dsem[o["stream"]]
                    st[1] += 16 * o["ndma"]
                    tok[i] = (id(st[0]), st[1])
                    o["inc"] = (st[0], 16)
        qs = {e: [] for e in self.ENGS}
        for e in self.ENGS:
            w = self.waited[e]
            for i in sched[e]:
                o = ops[i]
                need = {}
                for d, raw in o["deps"].items():
                    od = ops[d]
                    if o["kind"] == "c" and od["kind"] == "c" and od["e"] == e and (e == "pe" or not raw):
                        continue
                    sid, val = tok[d]
                    if need.get(sid, 0) < val:
                        need[sid] = val
                waits = []
                for sid, val in need.items():
                    if w.get(sid, 0) < val:
                        w[sid] = val
                        waits.append((self.semobj[sid], val))
                qs[e].append((waits, o["fn"], o["inc"]))
        fin = []
        for name, (s, cnt) in self.dsem.items():
            if cnt > 0 and self.waited["sp"].get(id(s), 0) < cnt:
                self.waited["sp"][id(s)] = cnt
                fin.append((s, cnt))
        if fin:
            qs["sp"].append((fin, (lambda eng: None), None))
        with nc.Block() as block:
            names = {"pe": "tensor", "act": "scalar", "dve": "vector", "pool": "gpsimd", "sp": "sync"}
            for e in self.ENGS:
                if not qs[e]:
                    continue

                def body(eng, lst=qs[e]):
                    for waits, fn, inc in lst:
                        for s, v in waits:
                            eng.wait_ge(s, v)
                        r = fn(eng)
                        if inc is not None and r is not None:
                            if isinstance(r, list):
                                for ins in r:
                                    ins.then_inc(inc[0], inc[1])
                            else:
                                r.then_inc(inc[0], inc[1])
                getattr(block, names[e])(body)
        self._reset()
        for e in self.ENGS:
            w = self.waited[e]
            for ee, s in self.esem.items():
                w[id(s)] = self.ecnt[ee]
            for name, (s, cnt) in self.dsem.items():
                w[id(s)] = cnt

    def raw_sp(self, fn):
        self.ops.append(dict(e="sp", fn=fn, deps={}, order=[], kind="x", est=50.0, stream=None, ndma=0, comp=50.0))

    PE_NS = 0.513

    def mm(self, out, pairs, reads, writes):
        return self.mmg([(out, pairs)], reads, writes)

    def mmg(self, groups, reads, writes):
        est = 0.0
        for out, pairs in groups:
            for l, rh in pairs:
                est += max(_fsz(rh), 64) * self.PE_NS + 2.0

        def fn(eng, groups=groups):
            r = None
            for out, pairs in groups:
                n = len(pairs)
                for i, (l, rh) in enumerate(pairs):
                    r = eng.matmul(out, lhsT=l, rhs=rh, start=(i == 0), stop=(i == n - 1))
            return r
        return self.op("pe", fn, reads, writes, est=est)

    def tr(self, groups, reads, writes):
        def fn(eng, groups=groups):
            r = None
            for out, in_, ident in groups:
                r = eng.transpose(out, in_, ident)
            return r
        return self.op("pe", fn, reads, writes, est=70.0 * len(groups))

    def act(self, out, in_, func, reads, writes, scale=1.0, bias=0.0, accum=None):
        def fn(eng):
            if accum is not None:
                return eng.activation(out=out, in_=in_, func=func, bias=bias, scale=scale, accum_out=accum)
            return eng.activation(out=out, in_=in_, func=func, bias=bias, scale=scale)
        est = 190.0 + _fsz(in_) * 0.84 + (120.0 if accum is not None else 0.0)
        return self.op("act", fn, reads, writes, est=est)

    def tt(self, e, out, in0, in1, op, reads, writes, est=None):
        def fn(eng):
            return eng.tensor_tensor(out=out, in0=in0, in1=in1, op=op)
        n = _fsz(out)
        if est is None:
            if e == "dve":
                ps = (str(in0.space) == "PSUM") or (str(in1.space) == "PSUM")
                est = 80.0 + n * (1.05 if ps else 2.1)
            else:
                est = 300.0 + n * 1.8
        return self.op(e, fn, reads, writes, est=est)

    def stt(self, out, in0, scalar, in1, op0, op1, reads, writes):
        def fn(eng):
            return eng.scalar_tensor_tensor(out=out, in0=in0, scalar=scalar, in1=in1, op0=op0, op1=op1)
        ps = (str(in0.space) == "PSUM") or (str(in1.space) == "PSUM")
        return self.op("dve", fn, reads, writes, est=80.0 + _fsz(out) * (1.05 if ps else 2.1))

    def ts(self, e, out, in0, s1, s2, op0, op1, reads, writes):
        def fn(eng):
            return eng.tensor_scalar(out=out, in0=in0, scalar1=s1, scalar2=s2, op0=op0, op1=op1)
        return self.op(e, fn, reads, writes, est=80.0 + _fsz(out) * 1.05)

    def recip(self, out, in_, reads, writes):
        def fn(eng):
            return eng.reciprocal(out=out, in_=in_)
        return self.op("dve", fn, reads, writes, est=80.0 + _fsz(out) * 6.3)

    def copy(self, e, out, in_, reads, writes):
        if e == "act":
            return self.act(out, in_, AF.Copy, reads, writes)

        def fn(eng):
            return eng.tensor_copy(out=out, in_=in_)
        return self.op(e, fn, reads, writes, est=80.0 + _fsz(out) * 1.05)

    def memset(self, e, ap, val, writes):
        def fn(eng):
            return eng.memset(ap, val)
        return self.op(e, fn, (), writes, est=100.0 + _fsz(ap) * 1.0)


class Banks:
    def __init__(self, items):
        self.items = items
        self.i = 0

    def next(self):
        it = self.items[self.i % len(self.items)]
        self.i += 1
        return it


def build_program():
    nc = bass.Bass("TRN2", target_bir_lowering=False)

    uid = [0]

    def SB(name, shape, dt):
        uid[0] += 1
        return nc.sbuf_tensor(f"{name}_s{uid[0]}", shape, dt)

    def PS(name, shape, dt):
        uid[0] += 1
        return nc.psum_tensor(f"{name}_p{uid[0]}", shape, dt)

    def din(name, shape):
        return nc.dram_tensor(name, list(shape), F32, kind="ExternalInput").ap()

    x_in = din("x", [TOK, D])
    cg_in = din("cg", [NG, D])
    mk_in = din("mk", [128, 2 * NT])
    cst_in = din("cst", [128, 128 * 6 + 1024])
    norm_g = din("norm_g", [DEPTH, 4, D])
    w_ada = din("w_ada", [DEPTH, D, 6 * D])
    b_ada = din("b_ada", [DEPTH, 6 * D])
    gla_w_in = din("gla_w_in", [2, D, 3072])
    gla_w_gk1 = din("gla_w_gk1", [2, 2, D, 16])
    gla_w_gk2 = din("gla_w_gk2", [2, 2, 16, 512])
    gla_b_gk = din("gla_b_gk", [2, 2, 512])
    gla_g_head = din("gla_g_head", [2, 256])
    gla_w_out = din("gla_w_out", [2, D, D])
    sg_w_in = din("sg_w_in", [2, D, 2048])
    sg_b_in = din("sg_b_in", [2, 2048])
    sg_ln_g = din("sg_ln_g", [2, D])
    sg_ln_b = din("sg_ln_b", [2, D])
    sg_w_s = din("sg_w_s", [2, 4, 128, 128])
    sg_b_s = din("sg_b_s", [2, 4, 128])
    sg_w_out = din("sg_w_out", [2, D, D])
    ffn_w_in = din("ffn_w_in", [DEPTH, D, 2 * FH])
    ffn_w_out = din("ffn_w_out", [DEPTH, FH, D])
    y_out = nc.dram_tensor("y", [TOK, D], F32, kind="ExternalOutput").ap()
    xscr = nc.dram_tensor("xscr", [TOK, D], F32, kind="Internal").ap()
    modscr = nc.dram_tensor("modscr", [DEPTH, 6, NG, D], F32, kind="Internal").ap()
    sfscr = nc.dram_tensor("sfscr", [NT, 128, D], F32, kind="Internal").ap()
    kscr = nc.dram_tensor("kscr", [NT, 2, 128, 512], F32, kind="Internal").ap()
    vscr = nc.dram_tensor("vscr", [NT, 2, 128, D], BF16, kind="Internal").ap()
    pscr = nc.dram_tensor("pscr", [NT, 2, 128, D], BF16, kind="Internal").ap()

    with contextlib.ExitStack() as gstack:
        P = Prog(nc, gstack)

        stream_names = ["xa00", "xa01", "xa10", "xa11", "xr0", "xr1", "st0", "st1", "bcAS", "bcG", "w", "cb", "c0", "bada", "wa0", "wa1",
                        "sf", "sfl", "mod", "sgm", "gh", "wst0", "wst1", "w2", "w3", "w4",
                        "k0", "k1", "v0", "v1", "v2", "v3", "p0", "p1", "p2", "p3"]
        for n in stream_names:
            P.stream(n)
        P.frozen = True

        def clr(eng):
            r = None
            for s in P.allsems:
                r = eng.sem_clear(s)
            return None
        P.raw_sp(clr)
        P.flush()

        cs = gstack.enter_context
        identb = cs(SB("identb", [128, 128], BF16))
        identf = cs(SB("identf", [128, 128], F32))
        Tf = cs(SB("Tf", [128, 128], BF16))
        Tb = cs(SB("Tb", [128, 128], BF16))
        Uf = cs(SB("Uf", [128, 128], BF16))
        Ub = cs(SB("Ub", [128, 128], BF16))
        Mf = cs(SB("Mf", [128, 4, 128], BF16))
        Mb = cs(SB("Mb", [128, 4, 128], BF16))
        onesb = cs(SB("onesb", [128, 128], BF16))
        mk = cs(SB("mk", [128, 2 * NT], F32))
        mhalf = cs(SB("mhalf", [128, 8], F32))

        with contextlib.ExitStack() as st:
            al = st.enter_context
            cg = al(SB("cg", [NG, D], F32))
            scg = al(SB("scg", [NG, D], F32))
            scT = al(SB("scT", [128, 8, 8], BF16))
            Wa = [al(SB(f"Wa{i}", [128, 8, 2048], BF16)) for i in range(2)]
            modrow = al(SB("modrow", [NG, 6 * D], F32))
            bada = al(SB("bada", [NG, 6 * D], F32))
            ng6 = al(SB("ng6", [NG, 4, D], F32))
            modo = al(SB("modo", [NG, 6, D], F32))
            psC = al(PS("psC", [128, 8, 8], F32))
            psM = [al(PS(f"psM{i}", [128, 512], F32)) for i in range(2)]

            P.dma("sp", "c0", [(identf[:], cst_in[:, 0:128]), (mk[:], mk_in[:, :]),
                                 (cg[:], cg_in[:, :])],
                  writes=["identf", "mk", "cg"])
            P.dma("pool", "cb", [(identb[:], cst_in[:, 0:128]), (Tf[:], cst_in[:, 128:256]),
                                (Tb[:], cst_in[:, 256:384]), (Uf[:], cst_in[:, 384:512]),
                                (Ub[:], cst_in[:, 512:640]), (onesb[:], cst_in[:, 640:768]),
                                (Mf[:], cst_in[:, 768:1280].rearrange("p (h c) -> p h c", h=4)),
                                (Mb[:], cst_in[:, 1280:1792].rearrange("p (h c) -> p h c", h=4))],
                  writes=["cb", "Mf", "Mb"])
            P.memset("pool", mhalf[:], -0.5, ["mhalf"])
            P.act(scg[:], cg[:], AF.Silu, ["cg"], ["scg"])
            P.tr([(psC[:, kc, 0:NG], scg[0:NG, kc * 128:(kc + 1) * 128], identf[0:NG, 0:NG]) for kc in range(8)],
                 ["scg", "identf"], ["psC"])
            P.copy("act", scT[:, :, 0:NG], psC[:, :, 0:NG], ["psC"], ["scT"])
            pi = 0
            for i in range(DEPTH_RUN):
                P.dma("sp", "bada", [(bada[:], b_ada[i:i + 1, :].partition_broadcast(NG)),
                                     (ng6[:], norm_g[i:i + 1, :, :].partition_broadcast(NG))],
                      writes=["bada", "ng6"])
                for q in range(3):
                    wa = Wa[(i * 3 + q) % 2]
                    wk = f"Wa{(i * 3 + q) % 2}"
                    P.dma("pool", f"wa{(i * 3 + q) % 2}",
                          [(wa[:, :, :], w_ada[i, :, q * 2048:(q + 1) * 2048].rearrange("(kc p) n -> p kc n", p=128))], writes=[wk])
                    for n in range(4):
                        pm = psM[pi % 2]
                        pk = f"psM{pi % 2}"
                        pi += 1
                        P.mm(pm[0:NG, :], [(scT[:, kc, 0:NG], wa[:, kc, n * 512:(n + 1) * 512]) for kc in range(8)],
                             ["scT", wk], [pk])
                        c0 = q * 2048 + n * 512
                        P.tt("dve", modrow[:, c0:c0 + 512], pm[0:NG, :], bada[:, c0:c0 + 512], ALU.add,
                             [pk, "bada"], ["modrow"])
                P.stt(modo[:, 0, :], modrow[:, 1024:2048], 1.0, ng6[:, 0, :], ALU.add, ALU.mult,
                      ["modrow", "ng6"], ["modo"])
                P.copy("dve", modo[:, 1, :], modrow[:, 0:1024], ["modrow"], ["modo"])
                P.tt("dve", modo[:, 2, :], modrow[:, 2048:3072], ng6[:, 1, :], ALU.mult, ["modrow", "ng6"], ["modo"])
                P.stt(modo[:, 3, :], modrow[:, 4096:5120], 1.0, ng6[:, 2, :], ALU.add, ALU.mult,
                      ["modrow", "ng6"], ["modo"])
                P.copy("dve", modo[:, 4, :], modrow[:, 3072:4096], ["modrow"], ["modo"])
                P.tt("dve", modo[:, 5, :], modrow[:, 5120:6144], ng6[:, 3, :], ALU.mult, ["modrow", "ng6"], ["modo"])
                P.dma("sp", "mod", [(modscr[i].rearrange("k g d -> g k d"), modo[:])], reads=["modo"],
                      writes=[("modscr", i)])
            P.flush()

        def rows(t, j=None):
            if j is None:
                return slice(t * TT, (t + 1) * TT)
            return slice(t * TT + j * 128, t * TT + (j + 1) * 128)

        class Ctx:
            pass

        def rsqrt(out, acc, n, scale, keys_in, key_out):
            P.ts("dve", acc, acc, scale, EPS, ALU.mult, ALU.add, keys_in, keys_in)
            P.tt("pool", out, acc, mhalf[:, 0:n], ALU.pow, keys_in + ["mhalf"], [key_out], est=1700.0)

        def alloc_common(al, C, nY=2, inplace=True, nxa=1):
            C.inplace = inplace
            C.xas = [al(SB(f"xa{i}", [128, 2, D], F32)) for i in range(nxa)]
            C.xai = 0
            C.xa = C.xas[0]
            C.xak = "xa0"
            C.xr = [al(SB(f"xr{j}", [128, D], F32)) for j in range(2)]
            C.bcA = al(SB("bcA", [128, D], F32))
            C.bcS = al(SB("bcS", [128, D], F32))
            C.bcG = al(SB("bcG", [128, D], F32))
            C.junk = al(SB("junk", [128, D], BF16))
            C.t1 = [al(SB(f"t1_{j}", [128, D], F32)) for j in range(2)]
            C.h = [al(SB(f"h{j}", [128, D], BF16)) for j in range(2)]
            C.hTs = [al(SB(f"hT{i}", [128, 8, TT], BF16)) for i in range(2)]
            C.hTi = 0
            C.hT = C.hTs[0]
            C.hTk = "hT0"
            C.ssq = al(SB("ssq", [128, 2], F32))
            C.rstd = al(SB("rstd", [128, 2], F32))
            C.ssqy = al(SB("ssqy", [128, 2], F32))
            C.rstdy = al(SB("rstdy", [128, 2], F32))
            C.ssq2 = [al(SB(f"ssq2_{j}", [128, 2], F32)) for j in range(2)]
            C.psT = al(PS("psT", [128, 8, 128], BF16))
            C.psY = [al(PS(f"psY{j}", [128, D], F32)) for j in range(nY)]

        def load_bcAS(C, layer, sub, g):
            k0 = 3 * sub
            P.dma("sp", "bcAS", [(C.bcA[:], modscr[layer, k0, g:g + 1, :].partition_broadcast(128)),
                                 (C.bcS[:], modscr[layer, k0 + 1, g:g + 1, :].partition_broadcast(128))],
                  reads=[("modscr", layer)], writes=["bcA", "bcS"])

        def load_bcG(C, layer, sub, g):
            k0 = 3 * sub
            P.dma("sp", "bcG", [(C.bcG[:], modscr[layer, k0 + 2, g:g + 1, :].partition_broadcast(128))],
                  reads=[("modscr", layer)], writes=["bcG"])

        def front_a(C, t, xsrc):
            C.xai += 1
            bi = C.xai % len(C.xas)
            C.xa = C.xas[bi]
            for j in range(2):
                xk = f"xa{bi}_{j}"
                P.dma("sp", f"xa{bi}{j}", [(C.xa[:, j, :], xsrc[rows(t, j), :])], reads=[("xd", t, j)], writes=[xk])
                P.act(C.junk[:], C.xa[:, j, :], AF.Square, [xk], [f"ssq{j}"], accum=C.ssq[:, j:j + 1])
                rsqrt(C.rstd[:, j:j + 1], C.ssq[:, j:j + 1], 1, 1.0 / D, [f"ssq{j}"], f"rstd{j}")
                if C.inplace:
                    P.stt(C.xa[:, j, :], C.xa[:, j, :], C.rstd[:, j:j + 1], C.bcA[:], ALU.mult, ALU.mult,
                          [xk, f"rstd{j}", "bcA"], [xk])
                    P.tt("pool", C.h[j][:], C.xa[:, j, :], C.bcS[:], ALU.add, [xk, "bcS"], [f"h{j}"])
                else:
                    P.stt(C.t1[j][:], C.xa[:, j, :], C.rstd[:, j:j + 1], C.bcA[:], ALU.mult, ALU.mult,
                          [xk, f"rstd{j}", "bcA"], [f"t1_{j}"])
                    P.tt("pool", C.h[j][:], C.t1[j][:], C.bcS[:], ALU.add, [f"t1_{j}", "bcS"], [f"h{j}"])

        def front_b(C, t):
            C.hTi += 1
            C.hT = C.hTs[C.hTi % 2]
            C.hTk = f"hT{C.hTi % 2}"
            for j in range(2):
                P.tr([(C.psT[:, kc, :], C.h[j][:, kc * 128:(kc + 1) * 128], identb[:]) for kc in range(8)],
                     [f"h{j}", "cb"], ["psT"])
                P.copy("act", C.hT[:, :, j * 128:(j + 1) * 128], C.psT[:], ["psT"], [C.hTk])

        def load_xr(C, t, xsrc):
            for j in range(2):
                P.dma("sp", f"xr{j}", [(C.xr[j][:], xsrc[rows(t, j), :])], reads=[("xd", t, j)], writes=[f"xr{j}"])

        def post(C, t, j, xdst, pieces=None):
            if pieces is None:
                pieces = [(C.psY[j % len(C.psY)][:], f"psY{j % len(C.psY)}", 0, D)]
            if len(pieces) == 1:
                ap, pk, c0, ncol = pieces[0]
                P.act(C.junk[:], ap, AF.Square, [pk], ["ssqy"], accum=C.ssqy[:, j:j + 1])
            else:
                for i, (ap, pk, c0, ncol) in enumerate(pieces):
                    P.act(C.junk[:, 0:ncol], ap, AF.Square, [pk], [f"ssq2_{j}"], accum=C.ssq2[j][:, i:i + 1])
                P.tt("dve", C.ssqy[:, j:j + 1], C.ssq2[j][:, 0:1], C.ssq2[j][:, 1:2], ALU.add, [f"ssq2_{j}"], ["ssqy"])
            for ap, pk, c0, ncol in pieces:
                P.tt("dve", ap, ap, C.bcG[:, c0:c0 + ncol], ALU.mult, [pk, "bcG"], [pk])
            rsqrt(C.rstdy[:, j:j + 1], C.ssqy[:, j:j + 1], 1, 1.0 / D, ["ssqy"], "rstdy")
            for ap, pk, c0, ncol in pieces:
                P.stt(C.xr[j][:, c0:c0 + ncol], ap, C.rstdy[:, j:j + 1], C.xr[j][:, c0:c0 + ncol], ALU.mult, ALU.add,
                      [pk, "rstdy", f"xr{j}"], [f"xr{j}"])
            P.dma("sp", f"st{j}", [(xdst[rows(t, j), :], C.xr[j][:])], reads=[f"xr{j}"], writes=[("xd", t, j)])

        wkeys = {}

        def load_w(dst, src2d, nk, ncols, key, stream="w", ranges=None):
            if ranges is None:
                ranges = [(0, ncols)]
            pairs = []
            for (r0, r1) in ranges:
                c0 = r0
                while c0 < r1:
                    c1 = min(r1, c0 + 2048)
                    for k0 in range(0, nk, 8):
                        k1 = min(nk, k0 + 8)
                        pairs.append((dst[:, k0:k1, c0:c1],
                                      src2d[k0 * 128:k1 * 128, c0:c1].rearrange("(kc p) n -> p kc n", p=128)))
                    c0 = c1
            for i in range(0, len(pairs), 4):
                P.dma("pool", stream, pairs[i:i + 4], writes=[key])
            wkeys.setdefault(stream, []).append(key)

        def wfence():
            for stream, keys in wkeys.items():
                P.fence(stream, keys)
            wkeys.clear()

        phase_no = [0]

        def phase_enabled():
            phase_no[0] += 1
            return phase_no[0] <= PH_LIMIT

        def run_tiles(C, layer, sub, order, xsrc, xdst, mainA, mainB, with_post=True):
            g0 = order[0] // TPG
            load_bcAS(C, layer, sub, g0)
            if with_post:
                load_bcG(C, layer, sub, g0)
            front_a(C, order[0], xsrc)
            front_b(C, order[0])
            for idx, t in enumerate(order):
                nxt = order[idx + 1] if idx + 1 < len(order) else None
                chg = nxt is not None and nxt // TPG != t // TPG
                if with_post:
                    load_xr(C, t, xsrc)
                if nxt is not None:
                    if chg:
                        load_bcAS(C, layer, sub, nxt // TPG)
                    front_a(C, nxt, xsrc)
                hT_cur, hTk_cur = C.hT, C.hTk
                if nxt is not None:
                    front_b(C, nxt)
                    hT_nxt, hTk_nxt = C.hT, C.hTk
                    C.hT, C.hTk = hT_cur, hTk_cur
                mainA(t)
                if nxt is not None:
                    C.hT, C.hTk = hT_nxt, hTk_nxt
                mainB(t)
                if chg and with_post:
                    load_bcG(C, layer, sub, nxt // TPG)

        def ffn_phase(layer, xsrc, xdst):
            with contextlib.ExitStack() as st:
                al = st.enter_context
                C = Ctx()
                alloc_common(al, C, nY=2, inplace=True, nxa=1)
                W1 = al(SB("W1", [128, 8, 2 * FH], BF16))
                W2 = al(SB("W2", [128, 22, D], BF16))
                gT = al(SB("gT", [128, 22, TT], BF16))
                sg = [al(SB(f"sg{i}", [128, TT], F32)) for i in range(2)]
                pab = Banks([(f"pab{i}", al(PS(f"pab{i}", [128, 2, TT], F32))) for i in range(3)])
                CG = (0, 2, 11, 22)
                for g, stream in enumerate(("w", "w2", "w3")):
                    a0, a1 = CG[g] * 128, CG[g + 1] * 128
                    load_w(W1, ffn_w_in[layer], 8, 2 * FH, f"W1g{g}", stream, [(a0, a1), (FH + a0, FH + a1)])
                load_w(W2, ffn_w_out[layer], 22, D, "W2", "w4")
                wfence()

                def mainA(t):
                    for c in range(22):
                        pk, pb = pab.next()
                        P.mmg([(pb[:, 0, :], [(W1[:, kc, c * 128:(c + 1) * 128], C.hT[:, kc, :]) for kc in range(8)]),
                               (pb[:, 1, :], [(W1[:, kc, FH + c * 128:FH + (c + 1) * 128], C.hT[:, kc, :]) for kc in range(8)])],
                              ["W1g0" if c < 2 else ("W1g1" if c < 11 else "W1g2"), C.hTk], [pk])
                        s = sg[c % 2]
                        P.act(s[:], pb[:, 0, :], AF.Silu, [pk], [f"sg{c % 2}"])
                        P.tt("dve", gT[:, c, :], s[:], pb[:, 1, :], ALU.mult, [pk, f"sg{c % 2}"], ["gT"])

                def mainB(t):
                    for j in range(2):
                        P.mmg([(C.psY[j][:, f * 512:(f + 1) * 512],
                                [(gT[:, c, j * 128:(j + 1) * 128], W2[:, c, f * 512:(f + 1) * 512]) for c in range(22)])
                               for f in range(2)], ["gT", "W2"], [f"psY{j}"])
                        post(C, t, j, xdst)

                run_tiles(C, layer, 1, list(range(NT)), xsrc, xdst, mainA, mainB)
                P.flush()

        def sgu_phase(layer, l2, xsrc, xdst):
            with contextlib.ExitStack() as st:
                al = st.enter_context
                C = Ctx()
                alloc_common(al, C, nY=1, inplace=True, nxa=2)
                Wi = al(SB("Wi", [128, 8, 2048], BF16))
                Wo = al(SB("Wo", [128, 8, D], BF16))
                binb = al(SB("binb", [1, 2048], BF16))
                wsn = al(SB("wsn", [128, 4, 128], F32))
                WsT = al(SB("WsT", [128, 4, 128], BF16))
                bs = al(SB("bs", [128, 4], F32))
                lng = al(SB("lng", [128, D], F32))
                lnb = al(SB("lnb", [128, D], F32))
                u = [al(SB(f"u{j}", [128, D], F32)) for j in range(2)]
                vr = [al(SB(f"vr{j}", [128, D], F32)) for j in range(2)]
                vn = [al(SB(f"vn{j}", [128, D], F32)) for j in range(2)]
                vb = [al(SB(f"vb{j}", [128, D], BF16)) for j in range(2)]
                m = [al(SB(f"m{j}", [128, D], BF16)) for j in range(2)]
                mT = al(SB("mT", [128, 8, 128], BF16))
                bst = al(SB("bst", [128, 12], F32))
                mv = al(SB("mv", [128, 2], F32))
                lrs = al(SB("lrs", [128, 1], F32))
                lnm = al(SB("lnm", [128, 1], F32))
                pz = Banks([(f"pz{i}", al(PS(f"pz{i}", [128, 512], F32))) for i in range(3)])
                psV = al(PS("psV", [128, D], F32))

                P.dma("pool", "w", [(binb[:], sg_b_in[l2:l2 + 1, :])], writes=["binb"])
                wkeys.setdefault("w", []).append("binb")
                load_w(Wi, sg_w_in[l2], 8, 2048, "Wi", "w")
                load_w(Wo, sg_w_out[l2], 8, D, "Wo", "w2")
                wfence()
                P.dma("sp", "sgm", [(wsn[:], sg_w_s[l2].rearrange("g t s -> t g s")),
                                    (lng[:], sg_ln_g[l2:l2 + 1, :].partition_broadcast(128)),
                                    (lnb[:], sg_ln_b[l2:l2 + 1, :].partition_broadcast(128))]
                      + [(bs[:, g:g + 1], sg_b_s[l2, g, :].rearrange("(p o) -> p o", o=1)) for g in range(4)],
                      writes=["wsn", "lng", "lnb", "bs"])
                pk, pb = pz.next()
                P.tr([(pb[:, g * 128:(g + 1) * 128], wsn[:, g, :], identf[:]) for g in range(4)],
                     ["wsn", "identf"], [pk])
                P.copy("act", WsT[:].rearrange("p g t -> p (g t)"), pb[:], [pk], ["WsT"])

                def mainA(t):
                    for j in range(2):
                        for q in range(4):
                            pk, pb = pz.next()
                            prs = [(C.hT[:, kc, j * 128:(j + 1) * 128], Wi[:, kc, q * 512:(q + 1) * 512]) for kc in range(8)]
                            prs.append((onesb[0:1, :], binb[0:1, q * 512:(q + 1) * 512]))
                            P.mm(pb[:], prs, [C.hTk, "Wi", "binb", "cb"], [pk])
                            if q < 2:
                                P.act(u[j][:, q * 512:(q + 1) * 512], pb[:], AF.Gelu_apprx_tanh, [pk], [f"u{j}"])
                            else:
                                P.act(vr[j][:, (q - 2) * 512:(q - 1) * 512], pb[:], AF.Gelu_apprx_tanh, [pk], [f"vr{j}"])

                def mainB(t):
                    for j in range(2):
                        def bn(eng, j=j):
                            eng.bn_stats(out=bst[:, 0:6], in_=vr[j][:, 0:512])
                            return eng.bn_stats(out=bst[:, 6:12], in_=vr[j][:, 512:1024])
                        P.op("dve", bn, [f"vr{j}"], ["bst"])

                        def bna(eng):
                            return eng.bn_aggr(out=mv[:], in_=bst[:])
                        P.op("dve", bna, ["bst"], ["mv"])
                        rsqrt(lrs[:], mv[:, 1:2], 1, 1.0, ["mv"], "lrs")
                        P.stt(lnm[:], mv[:, 0:1], -1.0, lrs[:], ALU.mult, ALU.mult, ["mv", "lrs"], ["lnm"])
                        P.act(vn[j][:], vr[j][:], AF.Identity, [f"vr{j}", "lrs", "lnm"], [f"vn{j}"], scale=lrs[:, 0:1], bias=lnm[:, 0:1])
                        P.tt("dve", vn[j][:], vn[j][:], lng[:], ALU.mult, [f"vn{j}", "lng"], [f"vn{j}"])
                        P.tt("pool", vb[j][:], vn[j][:], lnb[:], ALU.add, [f"vn{j}", "lnb"], [f"vb{j}"])
                        P.mmg([(psV[:, g * 256:(g + 1) * 256], [(WsT[:, g, :], vb[j][:, g * 256:(g + 1) * 256])]) for g in range(4)],
                              ["WsT", f"vb{j}"], ["psV"])
                        for g in range(4):
                            P.stt(m[j][:, g * 256:(g + 1) * 256], psV[:, g * 256:(g + 1) * 256], bs[:, g:g + 1],
                                  u[j][:, g * 256:(g + 1) * 256], ALU.add, ALU.mult, ["psV", "bs", f"u{j}"], [f"m{j}"])
                        P.tr([(C.psT[:, kc, :], m[j][:, kc * 128:(kc + 1) * 128], identb[:]) for kc in range(8)],
                             [f"m{j}", "cb"], ["psT"])
                        P.copy("act", mT[:], C.psT[:], ["psT"], ["mT"])
                        P.mmg([(C.psY[0][:, f * 512:(f + 1) * 512],
                                [(mT[:, kc, :], Wo[:, kc, f * 512:(f + 1) * 512]) for kc in range(8)])
                               for f in range(2)], ["mT", "Wo"], ["psY0"])
                        post(C, t, j, xdst)

                run_tiles(C, layer, 0, list(range(NT)), xsrc, xdst, mainA, mainB)
                P.flush()

        def gla_phase(layer, l2, xsrc, xdst):
            with contextlib.ExitStack() as st:
                al = st.enter_context
                C = Ctx()
                alloc_common(al, C, nY=0, inplace=(os.environ.get("MK_GI", "0") == "1"), nxa=1)
                Wi = al(SB("Wi", [128, 8, 3072], BF16))
                Wo = al(SB("Wo", [128, 8, D], BF16))
                wstage = C.t1
                gh = al(SB("gh", [128, 2], F32))
                Wg1 = al(SB("Wg1", [128, 8, 32], BF16))
                Wg2 = al(SB("Wg2", [33, D], BF16))
                g1Ts = [al(SB(f"g1T{i}", [33, TT], BF16)) for i in range(2)]
                etmp = [al(SB(f"etmp{i}", [128, 512], F32)) for i in range(2)]
                Pm = [al(SB(f"Pm{j}", [128, D], BF16)) for j in range(4)]
                qTs = [al(SB(f"qT{i}", [128, 4, TT], F32)) for i in range(2)]
                kTs = [al(SB(f"kT{i}", [128, 4, TT], F32)) for i in range(2)]
                ktok = [al(SB(f"ktok{j}", [128, 512], F32)) for j in range(2)]
                vtok = [al(SB(f"vtok{j}", [128, D], BF16)) for j in range(4)]
                sr = [al(SB(f"sr{j}", [128, D], BF16)) for j in range(4)]
                decs = al(SB("decs", [128, 2, 2, 4], F32))
                qd = [[al(SB(f"qd{d}{j}", [128, 4, 128], BF16)) for j in range(2)] for d in range(2)]
                kd = [al(SB(f"kd{d}", [128, 4, 128], BF16)) for d in range(2)]
                kend = [[al(SB(f"kend{d}{j}", [128, 512], BF16)) for j in range(2)] for d in range(2)]
                scm = [[al(SB(f"scm{d}{j}", [128, 4, 128], BF16)) for j in range(2)] for d in range(2)]
                Sf = al(SB("Sf", [128, D], F32))
                Sb = al(SB("Sb", [128, D], F32))
                Sfb = [al(SB(f"Sfb{j}", [128, D], BF16)) for j in range(2)]
                Sbb = [al(SB(f"Sbb{j}", [128, D], BF16)) for j in range(2)]
                dec = al(SB("dec", [128, 4], F32))
                ssqo = al(SB("ssqo", [128, 4], F32))
                rstdo = al(SB("rstdo", [128, 4], F32))
                mm_ = al(SB("mm_", [128, D], BF16))
                mT = al(SB("mT", [128, 8, 128], BF16))
                pgbanks = [(f"pg{i}", al(PS(f"pg{i}", [128, 512], F32))) for i in range(7)]
                pools = {1: (Banks(pgbanks[:5]), Banks(pgbanks[5:])), 2: (Banks(pgbanks[:2]), Banks(pgbanks[2:]))}
                pgA, pgB = pools[1]
                pgsel = [pgA]

                class _PG:
                    def next(self):
                        return pgsel[0].next()
                pg = _PG()

                P.dma("pool", "w", [(Wg1[:, :, e * 16:(e + 1) * 16], gla_w_gk1[l2, e].rearrange("(kc p) r -> p kc r", p=128))
                                    for e in range(2)], writes=["Wg1"])
                P.memset("pool", Wg2[:], 0.0, ["Wg2"])
                P.dma("pool", "w", [(Wg2[0:16, 0:512], gla_w_gk2[l2, 0]), (Wg2[16:32, 512:1024], gla_w_gk2[l2, 1]),
                                    (Wg2[32:33, :], gla_b_gk[l2:l2 + 1].rearrange("o e k -> o (e k)"))],
                      reads=["Wg2"], writes=["Wg2"])
                wkeys.setdefault("w", []).extend(["Wg1", "Wg2"])
                load_w(Wi, gla_w_in[l2], 8, 3072, "Wia", "w2", [(0, 2048)])
                load_w(Wi, gla_w_in[l2], 8, 3072, "Wib", "w3", [(2048, 3072)])
                wfence()
                P.dma("sp", "gh", [(gh[:, b:b + 1], gla_g_head[l2, b * 128:(b + 1) * 128].rearrange("(p o) -> p o", o=1))
                                   for b in range(2)], writes=["gh"])
                for kc in range(8):
                    ws = wstage[kc % 2]
                    P.dma("sp", f"wst{kc % 2}", [(ws[:], gla_w_out[l2, kc * 128:(kc + 1) * 128, :])], writes=[f"t1_{kc % 2}"])
                    P.act(Wo[:, kc, :], ws[:], AF.Copy, [f"t1_{kc % 2}", "gh"], ["Wo"], scale=gh[:, (kc % 2):(kc % 2) + 1])
                for i in range(2):
                    P.memset("pool", g1Ts[i][:], 1.0, [f"g1T{i}"])

                def VJ(t, j):
                    return j + 2 * (t % 2)

                def gates(t, ndir):
                    pk, pb = pg.next()
                    g1T = g1Ts[t % 2]
                    gk = f"g1T{t % 2}"
                    P.mm(pb[0:32, 0:TT], [(Wg1[:, kc, :], C.hT[:, kc, :]) for kc in range(8)], ["Wg1", C.hTk], [pk])
                    P.copy("act", g1T[0:32, :], pb[0:32, 0:TT], [pk], [gk])
                    for j in range(2):
                        for d in range(ndir):
                            pk, pb = pg.next()
                            P.mm(pb[:], [(g1T[0:33, j * 128:(j + 1) * 128], Wg2[0:33, d * 512:(d + 1) * 512])],
                                 [gk, "Wg2"], [pk])
                            P.act(pb[:], pb[:], AF.Exp, [pk], [pk], scale=-1.0)
                            P.act(Pm[VJ(t, j)][:, d * 512:(d + 1) * 512], pb[:], AF.Ln, [pk], [f"Pm{VJ(t, j)}"], bias=1.0)

                eti = [0]

                def proj_tok(t, j, c0, ncol, dst, dkey, func=AF.Copy, scale=1.0):
                    for n in range(ncol // 512):
                        pk, pb = pg.next()
                        P.mm(pb[:], [(C.hT[:, kc, j * 128:(j + 1) * 128], Wi[:, kc, c0 + n * 512:c0 + (n + 1) * 512]) for kc in range(8)],
                             [C.hTk, "Wia" if c0 < 2048 else "Wib"], [pk])
                        if func == AF.Silu:
                            eti[0] += 1
                            et = etmp[eti[0] % 2]
                            ek = f"etmp{eti[0] % 2}"
                            if os.environ.get("MK_SILU", "act") == "act":
                                P.act(et[:], pb[:], AF.Exp, [pk], [ek], scale=-1.0)
                                P.act(et[:], et[:], AF.Ln, [ek], [ek], bias=1.0)
                                P.act(et[:], et[:], AF.Exp, [ek], [ek], scale=-1.0)
                            else:
                                P.act(et[:], pb[:], AF.Exp, [pk], [ek], scale=-1.0)
                                P.ts("dve", et[:], et[:], 1.0, None, ALU.add, ALU.bypass, [ek], [ek])
                                P.recip(et[:], et[:], [ek], [ek])
                            P.tt("dve", dst[:, n * 512:(n + 1) * 512], pb[:], et[:], ALU.mult, [pk, ek], [dkey])
                        else:
                            P.act(dst[:, n * 512:(n + 1) * 512], pb[:], func, [pk], [dkey], scale=scale)

                def kend_dir(d, j, U, t):
                    pk, pb = pg.next()
                    P.mm(pb[:], [(U[:], Pm[VJ(t, j)][:, d * 512:(d + 1) * 512])], ["cb", f"Pm{VJ(t, j)}"], [pk])
                    P.act(pb[:], pb[:], AF.Exp, [pk], [pk], scale=-1.0 / 16)
                    P.tt("dve", kend[d][j][:], ktok[j][:], pb[:], ALU.mult, [f"ktok{j}", pk], [f"kend{d}{j}"])

                def state_update(S, skey, kd_, kkey, j, decap, deckeys, out_bf=None, okey=None):
                    for hh in range(2):
                        pk, pb = pg.next()
                        P.mmg([(pb[:, i * 256:(i + 1) * 256],
                                [(kd_[:, (2 * hh + i) * 128:(2 * hh + i + 1) * 128], vtok[j][:, (2 * hh + i) * 256:(2 * hh + i + 1) * 256])])
                               for i in range(2)], [kkey, f"vtok{j}"], [pk])
                        for i in range(2):
                            hd = 2 * hh + i
                            dst = S if out_bf is None else out_bf
                            dk = skey if out_bf is None else okey
                            P.stt(dst[:, hd * 256:(hd + 1) * 256], S[:, hd * 256:(hd + 1) * 256], decap(hd),
                                  pb[:, i * 256:(i + 1) * 256], ALU.mult, ALU.add, [skey, pk] + deckeys, [dk])

                if phase_enabled():
                    P.memset("dve", Sf[:], 0.0, ["Sf"])

                    def p1A(t):
                        pgsel[0] = pools[1][0]
                        if t % TPG == 0:
                            P.ts("dve", Sf[:], Sf[:], mk[:, t:t + 1], None, ALU.mult, ALU.bypass, ["Sf", "mk"], ["Sf"])
                        P.dma("sp", "sf", [(sfscr[t], Sf[:])], reads=["Sf"], writes=[("sfs", t)])
                        gates(t, 2)
                        for j in range(2):
                            vj = VJ(t, j)
                            proj_tok(t, j, 512, 512, ktok[j], f"ktok{j}")
                            proj_tok(t, j, 1024, 1024, vtok[vj], f"vtok{vj}")
                            P.dma("sp", f"k{j}", [(kscr[t, j], ktok[j][:])], reads=[f"ktok{j}"], writes=[("ks", t, j)])
                            P.dma("sp", f"v{vj}", [(vscr[t, j], vtok[vj][:])], reads=[f"vtok{vj}"], writes=[("vs", t, j)])
                            P.dma("sp", f"p{vj}", [(pscr[t, j], Pm[vj][:])], reads=[f"Pm{vj}"], writes=[("pss", t, j)])

                    def p1B(t):
                        pgsel[0] = pools[1][1]
                        for j in range(2):
                            kend_dir(0, j, Uf, t)
                            pk, pb = pg.next()
                            P.mmg([(pb[:, 2 * hd:2 * hd + 2], [(Pm[VJ(t, j)][:, hd * 128:(hd + 1) * 128], onesb[:, 0:2])]) for hd in range(4)],
                                  [f"Pm{VJ(t, j)}", "cb"], [pk])
                            P.act(dec[:], pb[:, 0:8].rearrange("p (h two) -> p h two", two=2)[:, :, 0], AF.Exp, [pk], ["dec"], scale=-1.0 / 16)
                            state_update(Sf, "Sf", kend[0][j], f"kend0{j}", VJ(t, j), lambda hd: dec[:, hd:hd + 1], ["dec"])

                    run_tiles(C, layer, 0, list(range(NT)), xsrc, xdst, p1A, p1B, with_post=False)
                    P.flush()

                if phase_enabled():
                    P.memset("dve", Sb[:], 0.0, ["Sb"])
                    P.memset("pool", Sbb[0][:], 0.0, ["Sbb0"])
                    sbi = [0]

                    def p2A(t):
                        pgsel[0] = pools[2][0]
                        for j in range(2):
                            vj = VJ(t, j)
                            P.dma("sp", f"v{vj}", [(vtok[vj][:], vscr[t, j])], reads=[("vs", t, j)], writes=[f"vtok{vj}"])
                            P.dma("sp", f"p{vj}", [(Pm[vj][:], pscr[t, j])], reads=[("pss", t, j)], writes=[f"Pm{vj}"])
                            P.dma("sp", f"k{j}", [(ktok[j][:], kscr[t, j])], reads=[("ks", t, j)], writes=[f"ktok{j}"])
                        for j in range(2):
                            proj_tok(t, j, 2048, 1024, sr[VJ(t, j)], f"sr{VJ(t, j)}", func=AF.Silu)
                        for dst, dkey, c0, sc in ((qTs[t % 2], f"qT{t % 2}", 0, 128.0 ** -0.5), (kTs[t % 2], f"kT{t % 2}", 512, 1.0)):
                            for hh in range(2):
                                pk, pb = pg.next()
                                P.mmg([(pb[:, i * TT:(i + 1) * TT],
                                        [(Wi[:, kc, c0 + (2 * hh + i) * 128:c0 + (2 * hh + i + 1) * 128], C.hT[:, kc, :]) for kc in range(8)])
                                       for i in range(2)], ["Wia", C.hTk], [pk])
                                P.act(dst[:, 2 * hh:2 * hh + 2, :], pb[:].rearrange("p (i c) -> p i c", i=2), AF.Copy, [pk], [dkey], scale=sc)

                    def p2B(t):
                        pgsel[0] = pools[2][1]
                        P.dma("sp", "sfl", [(Sf[:], sfscr[t])], reads=[("sfs", t)], writes=["Sf"])
                        if t % TPG == TPG - 1:
                            P.ts("dve", Sb[:], Sb[:], mk[:, NT + t:NT + t + 1], None, ALU.mult, ALU.bypass, ["Sb", "mk"], ["Sb"])
                            sbi[0] += 1
                            P.copy("act", Sbb[sbi[0] % 2][:], Sb[:], ["Sb"], [f"Sbb{sbi[0] % 2}"])
                        P.copy("act", Sfb[0][:], Sf[:], ["Sf"], ["Sfb0"])
                        for j in range(2):
                            for d, (Tm, Um, Mm) in enumerate(((Tf, Uf, Mf), (Tb, Ub, Mb))):
                                pk, pb = pg.next()
                                P.mmg([(pb[:, hd * 128:(hd + 1) * 128], [(Pm[VJ(t, j)][:, d * 512 + hd * 128:d * 512 + (hd + 1) * 128], Tm[:])])
                                       for hd in range(4)], [f"Pm{VJ(t, j)}", "cb"], [pk])
                                pbv = pb[:].rearrange("p (h c) -> p h c", h=4)
                                pk2, pb2 = pg.next()
                                pbv2 = pb2[:].rearrange("p (h c) -> p h c", h=4)
                                P.act(pbv2, pbv, AF.Exp, [pk], [pk2], scale=1.0 / 16)
                                P.act(pbv, pbv, AF.Exp, [pk], [pk], scale=-1.0 / 16)
                                col = 127 if d == 0 else 0
                                P.act(decs[:, d, j, :], pbv[:, :, col], AF.Copy, [pk], [f"decs{d}{j}"])
                                P.tt("dve", qd[d][j][:], qTs[t % 2][:, :, j * 128:(j + 1) * 128], pbv, ALU.mult, [f"qT{t % 2}", pk], [f"qd{d}{j}"])
                                P.tt("dve", kd[d][:], kTs[t % 2][:, :, j * 128:(j + 1) * 128], pbv2, ALU.mult, [f"kT{t % 2}", pk2], [f"kd{d}"])
                                kend_dir(d, j, Um, t)
                                pk, pb = pg.next()
                                P.mmg([(pb[:, hd * 128:(hd + 1) * 128], [(kd[d][:, hd, :], qd[d][j][:, hd, :])]) for hd in range(4)],
                                      [f"kd{d}", f"qd{d}{j}"], [pk])
                                P.tt("dve", scm[d][j][:], pb[:].rearrange("p (h c) -> p h c", h=4), Mm[:], ALU.mult, [pk, "Mf", "Mb"], [f"scm{d}{j}"])
                        state_update(Sf, "Sf", kend[0][0], "kend00", VJ(t, 0), lambda hd: decs[:, 0, 0, hd:hd + 1], ["decs00"],
                                     out_bf=Sfb[1], okey="Sfb1")
                        for j in (1, 0):
                            cur = Sbb[sbi[0] % 2]
                            ck = f"Sbb{sbi[0] % 2}"
                            vv = vtok[VJ(t, j)]
                            obanks = []
                            for hh in range(2):
                                pk, pb = pg.next()
                                obanks.append((pk, pb))
                                P.mmg([(pb[:, i * 256:(i + 1) * 256],
                                        [(scm[0][j][:, 2 * hh + i, :], vv[:, (2 * hh + i) * 256:(2 * hh + i + 1) * 256]),
                                         (scm[1][j][:, 2 * hh + i, :], vv[:, (2 * hh + i) * 256:(2 * hh + i + 1) * 256]),
                                         (qd[0][j][:, 2 * hh + i, :], Sfb[j][:, (2 * hh + i) * 256:(2 * hh + i + 1) * 256]),
                                         (qd[1][j][:, 2 * hh + i, :], cur[:, (2 * hh + i) * 256:(2 * hh + i + 1) * 256])]) for i in range(2)],
                                      [f"scm0{j}", f"scm1{j}", f"vtok{VJ(t, j)}", f"qd0{j}", f"qd1{j}", f"Sfb{j}", ck], [pk])
                            state_update(Sb, "Sb", kend[1][j], f"kend1{j}", VJ(t, j), lambda hd, j=j: decs[:, 1, j, hd:hd + 1], [f"decs1{j}"])
                            sbi[0] += 1
                            P.copy("act", Sbb[sbi[0] % 2][:], Sb[:], ["Sb"], [f"Sbb{sbi[0] % 2}"])
                            for hd in range(4):
                                pk, pb = obanks[hd // 2]
                                P.act(C.junk[:, 0:256], pb[:, (hd % 2) * 256:(hd % 2 + 1) * 256], AF.Square, [pk], ["ssqo"], accum=ssqo[:, hd:hd + 1])
                            rsqrt(rstdo[:], ssqo[:], 4, 1.0 / 256, ["ssqo"], "rstdo")
                            for hd in range(4):
                                pk, pb = obanks[hd // 2]
                                P.stt(mm_[:, hd * 256:(hd + 1) * 256], pb[:, (hd % 2) * 256:(hd % 2 + 1) * 256], rstdo[:, hd:hd + 1],
                                      sr[VJ(t, j)][:, hd * 256:(hd + 1) * 256], ALU.mult, ALU.mult, [pk, "rstdo", f"sr{VJ(t, j)}"], ["mm_"])
                            P.tr([(C.psT[:, kc, :], mm_[:, kc * 128:(kc + 1) * 128], identb[:]) for kc in range(8)],
                                 ["mm_", "cb"], ["psT"])
                            P.copy("act", mT[:], C.psT[:], ["psT"], ["mT"])
                            pieces = []
                            for f in range(2):
                                pk, pb = pg.next()
                                P.mm(pb[:], [(mT[:, kc, :], Wo[:, kc, f * 512:(f + 1) * 512]) for kc in range(8)], ["mT", "Wo"], [pk])
                                pieces.append((pb[:], pk, f * 512, 512))
                            post(C, t, j, xdst, pieces)

                    run_tiles(C, layer, 0, list(range(NT - 1, -1, -1)), xsrc, xdst, p2A, p2B)
                    P.flush()

        cur = x_in
        for layer in range(DEPTH_RUN):
            last = layer == DEPTH_RUN - 1
            if layer % 2 == 0:
                gla_phase(layer, layer // 2, cur, xscr)
            else:
                if phase_enabled():
                    sgu_phase(layer, layer // 2, cur, xscr)
            cur = xscr
            if phase_enabled():
                ffn_phase(layer, cur, y_out if last else xscr)
    return nc


_NC_CACHE = {}


def _consts():
    s = np.arange(128)[:, None]
    c = np.arange(128)[None, :]
    ident = np.eye(128, dtype=np.float32)
    tf = (s <= c).astype(np.float32)
    tb = (s >= c).astype(np.float32)
    uf = (s > c).astype(np.float32)
    ub = (s < c).astype(np.float32)
    ones = np.ones((128, 128), np.float32)
    mf = np.tile(tf, (1, 4))
    mb = np.tile(uf, (1, 4))
    return np.ascontiguousarray(np.concatenate([ident, tf, tb, uf, ub, ones, mf, mb], axis=1))


def kernel(x_prompt, x_sample, c_prompt, c_sample, norm_g, w_ada, b_ada,
           gla_w_in, gla_w_gk1, gla_w_gk2, gla_b_gk, gla_g_head, gla_w_out,
           sg_w_in, sg_b_in, sg_ln_g, sg_ln_b, sg_w_s, sg_b_s, sg_w_out,
           ffn_w_in, ffn_w_out):
    f = lambda a: np.ascontiguousarray(np.asarray(a, dtype=np.float32))
    x_prompt, x_sample, c_prompt, c_sample = f(x_prompt), f(x_sample), f(c_prompt), f(c_sample)
    shared = dict(norm_g=f(norm_g), w_ada=f(w_ada), b_ada=f(b_ada), gla_w_in=f(gla_w_in),
                  gla_w_gk1=f(gla_w_gk1), gla_w_gk2=f(gla_w_gk2), gla_b_gk=f(gla_b_gk),
                  gla_g_head=f(gla_g_head), gla_w_out=f(gla_w_out), sg_w_in=f(sg_w_in),
                  sg_b_in=f(sg_b_in), sg_ln_g=f(sg_ln_g), sg_ln_b=f(sg_ln_b), sg_w_s=f(sg_w_s),
                  sg_b_s=f(sg_b_s), sg_w_out=f(sg_w_out), ffn_w_in=f(ffn_w_in), ffn_w_out=f(ffn_w_out),
                  cst=_consts())
    plan = []
    for r in range(4):
        plan.append([("s", r), ("p", 2 * r), ("p", 2 * r + 1)])
    for r in range(4, 8):
        plan.append([("p", 8 + (r - 4) * 6 + k) for k in range(6)])
    in_maps = []
    for r in range(NCORES):
        xs, cgs = [], []
        mkf = np.ones(NT, np.float32)
        mkb = np.ones(NT, np.float32)
        tpos = 0
        for kind, idx in plan[r]:
            if kind == "s":
                xs.append(x_sample[idx]); ng = 4; cv = c_sample[idx]
            else:
                xs.append(x_prompt[idx]); ng = 1; cv = c_prompt[idx]
            for _ in range(ng):
                cgs.append(cv)
            mkf[tpos] = 0.0
            tpos += ng * TPG
            mkb[tpos - 1] = 0.0
        xc = np.ascontiguousarray(np.concatenate(xs, axis=0))
        cg = np.ascontiguousarray(np.stack(cgs, axis=0))
        mk = np.ascontiguousarray(np.tile(np.concatenate([mkf, mkb])[None, :], (128, 1)).astype(np.float32))
        m = dict(shared)
        m.update(x=xc, cg=cg, mk=mk)
        in_maps.append(m)
    if "nc" not in _NC_CACHE:
        _NC_CACHE["nc"] = build_program()
    nc = _NC_CACHE["nc"]
    res = run_bass_kernel_spmd(nc, in_maps, core_ids=list(range(NCORES)))
    y_prompt = np.empty_like(x_prompt)
    y_sample = np.empty_like(x_sample)
    for r in range(NCORES):
        y = np.asarray(res.results[r]["y"], dtype=np.float32)
        pos = 0
        for kind, idx in plan[r]:
            if kind == "s":
                y_sample[idx] = y[pos:pos + 8192]; pos += 8192
            else:
                y_prompt[idx] = y[pos:pos + 2048]; pos += 2048
    return (y_prompt, y_sample)
```

```python
import os
import contextlib
import numpy as np
import concourse.bass as bass
import concourse.mybir as mybir
from concourse.bass_utils import run_bass_kernel_spmd

F32 = mybir.dt.float32
BF16 = mybir.dt.bfloat16
AF = mybir.ActivationFunctionType
ALU = mybir.AluOpType

D = 1024
NCORES = 8
TOK = 12288
TT = 256
NT = TOK // TT
TPG = 8
NG = NT // TPG
DEPTH = 4
FH = 2816
EPS = 1e-6
DEPTH_RUN = int(os.environ.get("MK_DEPTH", "4"))
PH_LIMIT = int(os.environ.get("MK_PHASES", "99"))


def _fsz(ap):
    n = 1
    for d in list(ap.shape)[1:]:
        n *= int(d)
    return n


SCHED_WINDOW = int(os.environ.get("MK_WINDOW", "96"))
VERB = bool(os.environ.get("MK_VERBOSE"))
STALL = {}


class Prog:
    ENGS = ("pe", "act", "dve", "pool", "sp")

    def __init__(self, nc, stack):
        self.nc = nc
        self.stack = stack
        self.esem = {e: stack.enter_context(nc.semaphore(f"es_{e}"))
                     for e in ("pe", "act", "dve", "pool")}
        self.ecnt = {e: 0 for e in self.esem}
        self.dsem = {}
        self.allsems = list(self.esem.values())
        self.waited = {e: {} for e in self.ENGS}
        self.semobj = {}
        for e, s in self.esem.items():
            self.semobj[id(s)] = s
        self.frozen = False
        self._reset()

    def _reset(self):
        self.ops = []
        self.lastw = {}
        self.readers = {}
        self.last_stream_op = {}

    def stream(self, name):
        if name not in self.dsem:
            assert not self.frozen, name
            s = self.stack.enter_context(self.nc.semaphore(f"ds_{name}"))
            self.dsem[name] = [s, 0]
            self.allsems.append(s)
            self.semobj[id(s)] = s
        return self.dsem[name]

    def _record(self, e, fn, reads, writes, kind, est, stream=None, ndma=0, comp=None):
        idx = len(self.ops)
        deps = {}
        for k in reads:
            w = self.lastw.get(k)
            if w is not None:
                deps[w] = True
            if isinstance(k, str) and k.startswith(("pg", "pab", "pz", "ps")):
                for r in self.readers.get(k, ()):
                    if self.ops[r]["e"] != e:
                        deps.setdefault(r, False)
        for k in writes:
            w = self.lastw.get(k)
            if w is not None:
                deps.setdefault(w, False)
            for r in self.readers.get(k, ()):
                deps.setdefault(r, False)
        order = []
        if stream is not None:
            p = self.last_stream_op.get(stream)
            if p is not None:
                order.append(p)
            self.last_stream_op[stream] = idx
        self.ops.append(dict(e=e, fn=fn, deps=deps, order=order, kind=kind, est=est,
                             stream=stream, ndma=ndma, comp=comp if comp is not None else est,
                             tag="%s>%s" % (",".join(str(k) for k in reads)[:40], ",".join(str(k) for k in writes)[:30])))
        for k in reads:
            self.readers.setdefault(k, set()).add(idx)
        for k in writes:
            self.lastw[k] = idx
            self.readers[k] = set()
        return idx

    def op(self, e, fn, reads=(), writes=(), est=300.0):
        return self._record(e, fn, reads, writes, "c", est)

    def dma(self, qe, stream, pairs, reads=(), writes=(), **kw):
        self.stream(stream)
        nbytes = 0
        for o, i in pairs:
            try:
                nbytes += int(o.nbytes)
            except Exception:
                nbytes += 4 * _fsz(o) * int(o.shape[0])

        def fn(eng, pairs=pairs, kw=kw):
            return [eng.dma_start(out=o, in_=i, **kw) for (o, i) in pairs]
        return self._record(qe, fn, reads, writes, "d", 60.0 * len(pairs), stream=stream, ndma=len(pairs),
                            comp=2500.0 + nbytes / 120.0)

    def fence(self, stream, keys):
        p = self.last_stream_op[stream]
        for k in keys:
            self.lastw[k] = p
            self.readers[k] = set()

    def _schedule(self):
        ops = self.ops
        n = len(ops)
        succ = [[] for _ in range(n)]
        nun = [0] * n
        ready = [0.0] * n
        start = [0.0] * n
        finish = [0.0] * n
        for i, o in enumerate(ops):
            ds = set(o["deps"].keys()) | set(o["order"])
            nun[i] = len(ds)
            for d in ds:
                succ[d].append(i)
        pend = {e: [i for i in range(n) if ops[i]["e"] == e] for e in self.ENGS}
        free = {e: 0.0 for e in self.ENGS}
        sched = {e: [] for e in self.ENGS}
        left = n
        W = SCHED_WINDOW
        while left:
            best = None
            for e in self.ENGS:
                pl = pend[e]
                fe = free[e]
                for pos in range(min(W, len(pl))):
                    i = pl[pos]
                    if nun[i]:
                        continue
                    st = ready[i] if ready[i] > fe else fe
                    if best is None or st < best[0] or (st == best[0] and i < best[1]):
                        best = (st, i, e, pos)
                    if st <= fe:
                        break
            assert best is not None, "scheduler deadlock"
            st, i, e, pos = best
            pend[e].pop(pos)
            o = ops[i]
            if VERB and st > free[e] + 1.0 and "blk" in o:
                key = (e, "", ops[o["blk"]]["e"], ops[o["blk"]].get("tag", "?"))
                STALL[key] = STALL.get(key, 0.0) + (st - free[e])
            start[i] = st
            free[e] = st + o["est"]
            finish[i] = st + o["comp"]
            sched[e].append(i)
            left -= 1
            for s in succ[i]:
                nun[s] -= 1
                so = ops[s]
                raw = so["deps"].get(i)
                if raw is None:
                    t = start[i]
                elif o["kind"] == "c" and so["kind"] == "c" and so["e"] == e and (e == "pe" or not raw):
                    t = free[e]
                else:
                    t = finish[i] + 60.0
                if t > ready[s]:
                    ready[s] = t
                    so["blk"] = i
        if VERB:
            top = sorted([kv for kv in STALL.items() if kv[0][0] == "pe"], key=lambda kv: -kv[1])[:12]
            for k, v in top:
                print("   stall %.0f us: %s" % (v / 1e3, k))
            STALL.clear()
            busy = {e: sum(ops[i]["est"] for i in sched[e]) / 1e3 for e in self.ENGS}
            print("block: n=%d makespan=%.1f us busy(us)=%s" % (n, max(finish) / 1e3 if n else 0.0,
                  {e: round(v) for e, v in busy.items()}), flush=True)
        return sched

    def flush(self):
        nc = self.nc
        ops = self.ops
        sched = self._schedule()
        tok = {}
        for e in self.ENGS:
            for i in sched[e]:
                o = ops[i]
                if o["kind"] == "x":
                    o["inc"] = None
                elif o["kind"] == "c":
                    self.ecnt[e] += 1
                    tok[i] = (id(self.esem[e]), self.ecnt[e])
                    o["inc"] = (self.esem[e], 1)
                else:
                    st = self.dsem[o["stream"]]
                    st[1] += 16 * o["ndma"]
                    tok[i] = (id(st[0]), st[1])
                    o["inc"] = (st[0], 16)
        qs = {e: [] for e in self.ENGS}
        for e in self.ENGS:
            w = self.waited[e]
            for i in sched[e]:
                o = ops[i]
                need = {}
                for d, raw in o["deps"].items():
                    od = ops[d]
                    if o["kind"] == "c" and od["kind"] == "c" and od["e"] == e and (e == "pe" or not raw):
                        continue
                    sid, val = tok[d]
                    if need.get(sid, 0) < val:
                        need[sid] = val
                waits = []
                for sid, val in need.items():
                    if w.get(sid, 0) < val:
                        w[sid] = val
                        waits.append((self.semobj[sid], val))
                qs[e].append((waits, o["fn"], o["inc"]))
        fin = []
        for name, (s, cnt) in self.dsem.items():
            if cnt > 0 and self.waited["sp"].get(id(s), 0) < cnt:
                self.waited["sp"][id(s)] = cnt
                fin.append((s, cnt))
        if fin:
            qs["sp"].append((fin, (lambda eng: None), None))
        with nc.Block() as block:
            names = {"pe": "tensor", "act": "scalar", "dve": "vector", "pool": "gpsimd", "sp": "sync"}
            for e in self.ENGS:
                if not qs[e]:
                    continue

                def body(eng, lst=qs[e]):
                    for waits, fn, inc in lst:
                        for s, v in waits:
                            eng.wait_ge(s, v)
                        r = fn(eng)
                        if inc is not None and r is not None:
                            if isinstance(r, list):
                                for ins in r:
                                    ins.then_inc(inc[0], inc[1])
                            else:
                                r.then_inc(inc[0], inc[1])
                getattr(block, names[e])(body)
        self._reset()
        for e in self.ENGS:
            w = self.waited[e]
            for ee, s in self.esem.items():
                w[id(s)] = self.ecnt[ee]
            for name, (s, cnt) in self.dsem.items():
                w[id(s)] = cnt

    def raw_sp(self, fn):
        self.ops.append(dict(e="sp", fn=fn, deps={}, order=[], kind="x", est=50.0, stream=None, ndma=0, comp=50.0))

    PE_NS = 0.513

    def mm(self, out, pairs, reads, writes):
        return self.mmg([(out, pairs)], reads, writes)

    def mmg(self, groups, reads, writes):
        est = 0.0
        for out, pairs in groups:
            for l, rh in pairs:
                est += max(_fsz(rh), 64) * self.PE_NS + 2.0

        def fn(eng, groups=groups):
            r = None
            for out, pairs in groups:
                n = len(pairs)
                for i, (l, rh) in enumerate(pairs):
                    r = eng.matmul(out, lhsT=l, rhs=rh, start=(i == 0), stop=(i == n - 1))
            return r
        return self.op("pe", fn, reads, writes, est=est)

    def tr(self, groups, reads, writes):
        def fn(eng, groups=groups):
            r = None
            for out, in_, ident in groups:
                r = eng.transpose(out, in_, ident)
            return r
        return self.op("pe", fn, reads, writes, est=70.0 * len(groups))

    def act(self, out, in_, func, reads, writes, scale=1.0, bias=0.0, accum=None):
        def fn(eng):
            if accum is not None:
                return eng.activation(out=out, in_=in_, func=func, bias=bias, scale=scale, accum_out=accum)
            return eng.activation(out=out, in_=in_, func=func, bias=bias, scale=scale)
        est = 190.0 + _fsz(in_) * 0.84 + (120.0 if accum is not None else 0.0)
        return self.op("act", fn, reads, writes, est=est)

    def tt(self, e, out, in0, in1, op, reads, writes, est=None):
        def fn(eng):
            return eng.tensor_tensor(out=out, in0=in0, in1=in1, op=op)
        n = _fsz(out)
        if est is None:
            if e == "dve":
                ps = (str(in0.space) == "PSUM") or (str(in1.space) == "PSUM")
                est = 80.0 + n * (1.05 if ps else 2.1)
            else:
                est = 300.0 + n * 1.8
        return self.op(e, fn, reads, writes, est=est)

    def stt(self, out, in0, scalar, in1, op0, op1, reads, writes):
        def fn(eng):
            return eng.scalar_tensor_tensor(out=out, in0=in0, scalar=scalar, in1=in1, op0=op0, op1=op1)
        ps = (str(in0.space) == "PSUM") or (str(in1.space) == "PSUM")
        return self.op("dve", fn, reads, writes, est=80.0 + _fsz(out) * (1.05 if ps else 2.1))

    def ts(self, e, out, in0, s1, s2, op0, op1, reads, writes):
        def fn(eng):
            return eng.tensor_scalar(out=out, in0=in0, scalar1=s1, scalar2=s2, op0=op0, op1=op1)
        return self.op(e, fn, reads, writes, est=80.0 + _fsz(out) * 1.05)

    def recip(self, out, in_, reads, writes):
        def fn(eng):
            return eng.reciprocal(out=out, in_=in_)
        return self.op("dve", fn, reads, writes, est=80.0 + _fsz(out) * 6.3)

    def copy(self, e, out, in_, reads, writes):
        if e == "act":
            return self.act(out, in_, AF.Copy, reads, writes)

        def fn(eng):
            return eng.tensor_copy(out=out, in_=in_)
        return self.op(e, fn, reads, writes, est=(80.0 + _fsz(out) * 1.05) if e == "dve" else (300.0 + _fsz(out) * 1.8))

    def memset(self, e, ap, val, writes):
        def fn(eng):
            return eng.memset(ap, val)
        return self.op(e, fn, (), writes, est=100.0 + _fsz(ap) * 1.0)


class Banks:
    def __init__(self, items):
        self.items = items
        self.i = 0

    def next(self):
        it = self.items[self.i % len(self.items)]
        self.i += 1
        return it


def build_program():
    nc = bass.Bass("TRN2", target_bir_lowering=False)

    uid = [0]

    def SB(name, shape, dt):
        uid[0] += 1
        return nc.sbuf_tensor(f"{name}_s{uid[0]}", shape, dt)

    def PS(name, shape, dt):
        uid[0] += 1
        return nc.psum_tensor(f"{name}_p{uid[0]}", shape, dt)

    def din(name, shape):
        return nc.dram_tensor(name, list(shape), F32, kind="ExternalInput").ap()

    x_in = din("x", [TOK, D])
    cg_in = din("cg", [NG, D])
    mk_in = din("mk", [128, 2 * NT])
    cst_in = din("cst", [128, 128 * 6 + 1024])
    norm_g = din("norm_g", [DEPTH, 4, D])
    w_ada = din("w_ada", [DEPTH, D, 6 * D])
    b_ada = din("b_ada", [DEPTH, 6 * D])
    gla_w_in = din("gla_w_in", [2, D, 3072])
    gla_w_gk1 = din("gla_w_gk1", [2, 2, D, 16])
    gla_w_gk2 = din("gla_w_gk2", [2, 2, 16, 512])
    gla_b_gk = din("gla_b_gk", [2, 2, 512])
    gla_g_head = din("gla_g_head", [2, 256])
    gla_w_out = din("gla_w_out", [2, D, D])
    sg_w_in = din("sg_w_in", [2, D, 2048])
    sg_b_in = din("sg_b_in", [2, 2048])
    sg_ln_g = din("sg_ln_g", [2, D])
    sg_ln_b = din("sg_ln_b", [2, D])
    sg_w_s = din("sg_w_s", [2, 4, 128, 128])
    sg_b_s = din("sg_b_s", [2, 4, 128])
    sg_w_out = din("sg_w_out", [2, D, D])
    ffn_w_in = din("ffn_w_in", [DEPTH, D, 2 * FH])
    ffn_w_out = din("ffn_w_out", [DEPTH, FH, D])
    y_out = nc.dram_tensor("y", [TOK, D], F32, kind="ExternalOutput").ap()
    xscr = nc.dram_tensor("xscr", [TOK, D], F32, kind="Internal").ap()
    modscr = nc.dram_tensor("modscr", [DEPTH, 6, NG, D], F32, kind="Internal").ap()
    sfscr = nc.dram_tensor("sfscr", [NT, 128, D], F32, kind="Internal").ap()
    kscr = nc.dram_tensor("kscr", [NT, 2, 128, 512], F32, kind="Internal").ap()
    vscr = nc.dram_tensor("vscr", [NT, 2, 128, D], BF16, kind="Internal").ap()
    pscr = nc.dram_tensor("pscr", [NT, 2, 128, D], BF16, kind="Internal").ap()

    with contextlib.ExitStack() as gstack:
        P = Prog(nc, gstack)

        stream_names = ["xa00", "xa01", "xa10", "xa11", "xr0", "xr1", "st0", "st1", "bcAS", "bcG", "w", "cb", "c0", "bada", "wa0", "wa1",
                        "sf", "sfl", "mod", "sgm", "gh", "wst0", "wst1", "w2", "w3", "w4",
                        "k0", "k1", "v0", "v1", "v2", "v3", "p0", "p1", "p2", "p3"]
        for n in stream_names:
            P.stream(n)
        P.frozen = True

        def clr(eng):
            r = None
            for s in P.allsems:
                r = eng.sem_clear(s)
            return None
        P.raw_sp(clr)
        P.flush()

        cs = gstack.enter_context
        identb = cs(SB("identb", [128, 128], BF16))
        identf = cs(SB("identf", [128, 128], F32))
        Tf = cs(SB("Tf", [128, 128], BF16))
        Tb = cs(SB("Tb", [128, 128], BF16))
        Uf = cs(SB("Uf", [128, 128], BF16))
        Ub = cs(SB("Ub", [128, 128], BF16))
        Mf = cs(SB("Mf", [128, 4, 128], BF16))
        Mb = cs(SB("Mb", [128, 4, 128], BF16))
        onesb = cs(SB("onesb", [128, 128], BF16))
        mk = cs(SB("mk", [128, 2 * NT], F32))
        mhalf = cs(SB("mhalf", [128, 8], F32))

        with contextlib.ExitStack() as st:
            al = st.enter_context
            cg = al(SB("cg", [NG, D], F32))
            scg = al(SB("scg", [NG, D], F32))
            scT = al(SB("scT", [128, 8, 8], BF16))
            Wa = [al(SB(f"Wa{i}", [128, 8, 2048], BF16)) for i in range(2)]
            modrow = al(SB("modrow", [NG, 6 * D], F32))
            bada = al(SB("bada", [NG, 6 * D], F32))
            ng6 = al(SB("ng6", [NG, 4, D], F32))
            modo = al(SB("modo", [NG, 6, D], F32))
            psC = al(PS("psC", [128, 8, 8], F32))
            psM = [al(PS(f"psM{i}", [128, 512], F32)) for i in range(2)]

            P.dma("sp", "c0", [(identf[:], cst_in[:, 0:128]), (mk[:], mk_in[:, :]),
                                 (cg[:], cg_in[:, :])],
                  writes=["identf", "mk", "cg"])
            P.dma("pool", "cb", [(identb[:], cst_in[:, 0:128]), (Tf[:], cst_in[:, 128:256]),
                                (Tb[:], cst_in[:, 256:384]), (Uf[:], cst_in[:, 384:512]),
                                (Ub[:], cst_in[:, 512:640]), (onesb[:], cst_in[:, 640:768]),
                                (Mf[:], cst_in[:, 768:1280].rearrange("p (h c) -> p h c", h=4)),
                                (Mb[:], cst_in[:, 1280:1792].rearrange("p (h c) -> p h c", h=4))],
                  writes=["cb", "Mf", "Mb"])
            P.memset("pool", mhalf[:], -0.5, ["mhalf"])
            P.act(scg[:], cg[:], AF.Silu, ["cg"], ["scg"])
            P.tr([(psC[:, kc, 0:NG], scg[0:NG, kc * 128:(kc + 1) * 128], identf[0:NG, 0:NG]) for kc in range(8)],
                 ["scg", "identf"], ["psC"])
            P.copy("act", scT[:, :, 0:NG], psC[:, :, 0:NG], ["psC"], ["scT"])
            pi = 0
            for i in range(DEPTH_RUN):
                P.dma("sp", "bada", [(bada[:], b_ada[i:i + 1, :].partition_broadcast(NG)),
                                     (ng6[:], norm_g[i:i + 1, :, :].partition_broadcast(NG))],
                      writes=["bada", "ng6"])
                for q in range(3):
                    wa = Wa[(i * 3 + q) % 2]
                    wk = f"Wa{(i * 3 + q) % 2}"
                    P.dma("pool", f"wa{(i * 3 + q) % 2}",
                          [(wa[:, :, :], w_ada[i, :, q * 2048:(q + 1) * 2048].rearrange("(kc p) n -> p kc n", p=128))], writes=[wk])
                    for n in range(4):
                        pm = psM[pi % 2]
                        pk = f"psM{pi % 2}"
                        pi += 1
                        P.mm(pm[0:NG, :], [(scT[:, kc, 0:NG], wa[:, kc, n * 512:(n + 1) * 512]) for kc in range(8)],
                             ["scT", wk], [pk])
                        c0 = q * 2048 + n * 512
                        P.tt("dve", modrow[:, c0:c0 + 512], pm[0:NG, :], bada[:, c0:c0 + 512], ALU.add,
                             [pk, "bada"], ["modrow"])
                P.stt(modo[:, 0, :], modrow[:, 1024:2048], 1.0, ng6[:, 0, :], ALU.add, ALU.mult,
                      ["modrow", "ng6"], ["modo"])
                P.copy("dve", modo[:, 1, :], modrow[:, 0:1024], ["modrow"], ["modo"])
                P.tt("dve", modo[:, 2, :], modrow[:, 2048:3072], ng6[:, 1, :], ALU.mult, ["modrow", "ng6"], ["modo"])
                P.stt(modo[:, 3, :], modrow[:, 4096:5120], 1.0, ng6[:, 2, :], ALU.add, ALU.mult,
                      ["modrow", "ng6"], ["modo"])
                P.copy("dve", modo[:, 4, :], modrow[:, 3072:4096], ["modrow"], ["modo"])
                P.tt("dve", modo[:, 5, :], modrow[:, 5120:6144], ng6[:, 3, :], ALU.mult, ["modrow", "ng6"], ["modo"])
                P.dma("sp", "mod", [(modscr[i].rearrange("k g d -> g k d"), modo[:])], reads=["modo"],
                      writes=[("modscr", i)])
            P.flush()

        def rows(t, j=None):
            if j is None:
                return slice(t * TT, (t + 1) * TT)
            return slice(t * TT + j * 128, t * TT + (j + 1) * 128)

        class Ctx:
            pass

        def rsqrt(out, acc, n, scale, keys_in, key_out):
            P.ts("dve", acc, acc, scale, EPS, ALU.mult, ALU.add, keys_in, keys_in)
            P.tt("pool", out, acc, mhalf[:, 0:n], ALU.pow, keys_in + ["mhalf"], [key_out], est=1700.0)

        def alloc_common(al, C, nY=2, inplace=True, nxa=1):
            C.inplace = inplace
            C.xas = [al(SB(f"xa{i}", [128, 2, D], F32)) for i in range(nxa)]
            C.xai = 0
            C.xa = C.xas[0]
            C.xak = "xa0"
            C.xr = [al(SB(f"xr{j}", [128, D], F32)) for j in range(2)]
            C.bcA = al(SB("bcA", [128, D], F32))
            C.bcS = al(SB("bcS", [128, D], F32))
            C.bcG = al(SB("bcG", [128, D], F32))
            C.junk = al(SB("junk", [128, D], BF16))
            C.t1 = [al(SB(f"t1_{j}", [128, D], F32)) for j in range(2)]
            C.h = [al(SB(f"h{j}", [128, D], BF16)) for j in range(2)]
            C.hTs = [al(SB(f"hT{i}", [128, 8, TT], BF16)) for i in range(2)]
            C.hTi = 0
            C.hT = C.hTs[0]
            C.hTk = "hT0"
            C.ssq = al(SB("ssq", [128, 2], F32))
            C.rstd = al(SB("rstd", [128, 2], F32))
            C.ssqy = al(SB("ssqy", [128, 2], F32))
            C.rstdy = al(SB("rstdy", [128, 2], F32))
            C.ssq2 = [al(SB(f"ssq2_{j}", [128, 2], F32)) for j in range(2)]
            C.psT = al(PS("psT", [128, 8, 128], BF16))
            C.psY = [al(PS(f"psY{j}", [128, D], F32)) for j in range(nY)]

        def load_bcAS(C, layer, sub, g):
            k0 = 3 * sub
            P.dma("sp", "bcAS", [(C.bcA[:], modscr[layer, k0, g:g + 1, :].partition_broadcast(128)),
                                 (C.bcS[:], modscr[layer, k0 + 1, g:g + 1, :].partition_broadcast(128))],
                  reads=[("modscr", layer)], writes=["bcA", "bcS"])

        def load_bcG(C, layer, sub, g):
            k0 = 3 * sub
            P.dma("sp", "bcG", [(C.bcG[:], modscr[layer, k0 + 2, g:g + 1, :].partition_broadcast(128))],
                  reads=[("modscr", layer)], writes=["bcG"])

        def front_a(C, t, xsrc):
            C.xai += 1
            bi = C.xai % len(C.xas)
            C.xa = C.xas[bi]
            for j in range(2):
                xk = f"xa{bi}_{j}"
                P.dma("sp", f"xa{bi}{j}", [(C.xa[:, j, :], xsrc[rows(t, j), :])], reads=[("xd", t, j)], writes=[xk])
                P.act(C.junk[:], C.xa[:, j, :], AF.Square, [xk], [f"ssq{j}"], accum=C.ssq[:, j:j + 1])
                rsqrt(C.rstd[:, j:j + 1], C.ssq[:, j:j + 1], 1, 1.0 / D, [f"ssq{j}"], f"rstd{j}")
                if C.inplace:
                    P.stt(C.xa[:, j, :], C.xa[:, j, :], C.rstd[:, j:j + 1], C.bcA[:], ALU.mult, ALU.mult,
                          [xk, f"rstd{j}", "bcA"], [xk])
                    P.tt("pool", C.h[j][:], C.xa[:, j, :], C.bcS[:], ALU.add, [xk, "bcS"], [f"h{j}"])
                else:
                    P.stt(C.t1[j][:], C.xa[:, j, :], C.rstd[:, j:j + 1], C.bcA[:], ALU.mult, ALU.mult,
                          [xk, f"rstd{j}", "bcA"], [f"t1_{j}"])
                    P.tt("pool", C.h[j][:], C.t1[j][:], C.bcS[:], ALU.add, [f"t1_{j}", "bcS"], [f"h{j}"])

        def front_b(C, t):
            C.hTi += 1
            C.hT = C.hTs[C.hTi % 2]
            C.hTk = f"hT{C.hTi % 2}"
            for j in range(2):
                P.tr([(C.psT[:, kc, :], C.h[j][:, kc * 128:(kc + 1) * 128], identb[:]) for kc in range(8)],
                     [f"h{j}", "cb"], ["psT"])
                P.copy("act", C.hT[:, :, j * 128:(j + 1) * 128], C.psT[:], ["psT"], [C.hTk])

        def load_xr(C, t, xsrc):
            for j in range(2):
                P.dma("sp", f"xr{j}", [(C.xr[j][:], xsrc[rows(t, j), :])], reads=[("xd", t, j)], writes=[f"xr{j}"])

        def post(C, t, j, xdst, pieces=None):
            if pieces is None:
                pieces = [(C.psY[j % len(C.psY)][:], f"psY{j % len(C.psY)}", 0, D)]
            if len(pieces) == 1:
                ap, pk, c0, ncol = pieces[0]
                P.act(C.junk[:], ap, AF.Square, [pk], ["ssqy"], accum=C.ssqy[:, j:j + 1])
            else:
                for i, (ap, pk, c0, ncol) in enumerate(pieces):
                    P.act(C.junk[:, 0:ncol], ap, AF.Square, [pk], [f"ssq2_{j}"], accum=C.ssq2[j][:, i:i + 1])
                P.tt("dve", C.ssqy[:, j:j + 1], C.ssq2[j][:, 0:1], C.ssq2[j][:, 1:2], ALU.add, [f"ssq2_{j}"], ["ssqy"])
            for ap, pk, c0, ncol in pieces:
                P.tt("dve", ap, ap, C.bcG[:, c0:c0 + ncol], ALU.mult, [pk, "bcG"], [pk])
            rsqrt(C.rstdy[:, j:j + 1], C.ssqy[:, j:j + 1], 1, 1.0 / D, ["ssqy"], "rstdy")
            for ap, pk, c0, ncol in pieces:
                P.stt(C.xr[j][:, c0:c0 + ncol], ap, C.rstdy[:, j:j + 1], C.xr[j][:, c0:c0 + ncol], ALU.mult, ALU.add,
                      [pk, "rstdy", f"xr{j}"], [f"xr{j}"])
            P.dma("sp", f"st{j}", [(xdst[rows(t, j), :], C.xr[j][:])], reads=[f"xr{j}"], writes=[("xd", t, j)])

        wkeys = {}

        def load_w(dst, src2d, nk, ncols, key, stream="w", ranges=None):
            if ranges is None:
                ranges = [(0, ncols)]
            pairs = []
            for (r0, r1) in ranges:
                c0 = r0
                while c0 < r1:
                    c1 = min(r1, c0 + 2048)
                    for k0 in range(0, nk, 8):
                        k1 = min(nk, k0 + 8)
                        pairs.append((dst[:, k0:k1, c0:c1],
                                      src2d[k0 * 128:k1 * 128, c0:c1].rearrange("(kc p) n -> p kc n", p=128)))
                    c0 = c1
            for i in range(0, len(pairs), 4):
                P.dma("pool", stream, pairs[i:i + 4], writes=[key])
            wkeys.setdefault(stream, []).append(key)

        def wfence():
            for stream, keys in wkeys.items():
                P.fence(stream, keys)
            wkeys.clear()

        phase_no = [0]

        def phase_enabled():
            phase_no[0] += 1
            return phase_no[0] <= PH_LIMIT

        def run_tiles(C, layer, sub, order, xsrc, xdst, mainA, mainB, with_post=True):
            g0 = order[0] // TPG
            load_bcAS(C, layer, sub, g0)
            if with_post:
                load_bcG(C, layer, sub, g0)
            front_a(C, order[0], xsrc)
            front_b(C, order[0])
            for idx, t in enumerate(order):
                nxt = order[idx + 1] if idx + 1 < len(order) else None
                chg = nxt is not None and nxt // TPG != t // TPG
                if with_post:
                    load_xr(C, t, xsrc)
                if nxt is not None:
                    if chg:
                        load_bcAS(C, layer, sub, nxt // TPG)
                    front_a(C, nxt, xsrc)
                hT_cur, hTk_cur = C.hT, C.hTk
                if nxt is not None:
                    front_b(C, nxt)
                    hT_nxt, hTk_nxt = C.hT, C.hTk
                    C.hT, C.hTk = hT_cur, hTk_cur
                mainA(t)
                if nxt is not None:
                    C.hT, C.hTk = hT_nxt, hTk_nxt
                mainB(t)
                if chg and with_post:
                    load_bcG(C, layer, sub, nxt // TPG)

        def ffn_phase(layer, xsrc, xdst):
            with contextlib.ExitStack() as st:
                al = st.enter_context
                C = Ctx()
                alloc_common(al, C, nY=2, inplace=True, nxa=1)
                W1 = al(SB("W1", [128, 8, 2 * FH], BF16))
                W2 = al(SB("W2", [128, 22, D], BF16))
                gT = al(SB("gT", [128, 22, TT], BF16))
                sg = [al(SB(f"sg{i}", [128, TT], F32)) for i in range(2)]
                pab = Banks([(f"pab{i}", al(PS(f"pab{i}", [128, 2, TT], F32))) for i in range(3)])
                CG = (0, 2, 11, 22)
                for g, stream in enumerate(("w", "w2", "w3")):
                    a0, a1 = CG[g] * 128, CG[g + 1] * 128
                    load_w(W1, ffn_w_in[layer], 8, 2 * FH, f"W1g{g}", stream, [(a0, a1), (FH + a0, FH + a1)])
                load_w(W2, ffn_w_out[layer], 22, D, "W2", "w4")
                wfence()

                def mainA(t):
                    for c in range(22):
                        pk, pb = pab.next()
                        P.mmg([(pb[:, 0, :], [(W1[:, kc, c * 128:(c + 1) * 128], C.hT[:, kc, :]) for kc in range(8)]),
                               (pb[:, 1, :], [(W1[:, kc, FH + c * 128:FH + (c + 1) * 128], C.hT[:, kc, :]) for kc in range(8)])],
                              ["W1g0" if c < 2 else ("W1g1" if c < 11 else "W1g2"), C.hTk], [pk])
                        s = sg[c % 2]
                        P.act(s[:], pb[:, 0, :], AF.Silu, [pk], [f"sg{c % 2}"])
                        P.tt("dve", gT[:, c, :], s[:], pb[:, 1, :], ALU.mult, [pk, f"sg{c % 2}"], ["gT"])

                def mainB(t):
                    for j in range(2):
                        P.mmg([(C.psY[j][:, f * 512:(f + 1) * 512],
                                [(gT[:, c, j * 128:(j + 1) * 128], W2[:, c, f * 512:(f + 1) * 512]) for c in range(22)])
                               for f in range(2)], ["gT", "W2"], [f"psY{j}"])
                        post(C, t, j, xdst)

                run_tiles(C, layer, 1, list(range(NT)), xsrc, xdst, mainA, mainB)
                P.flush()

        def sgu_phase(layer, l2, xsrc, xdst):
            with contextlib.ExitStack() as st:
                al = st.enter_context
                C = Ctx()
                alloc_common(al, C, nY=1, inplace=True, nxa=2)
                Wi = al(SB("Wi", [128, 8, 2048], BF16))
                Wo = al(SB("Wo", [128, 8, D], BF16))
                binb = al(SB("binb", [1, 2048], BF16))
                wsn = al(SB("wsn", [128, 4, 128], F32))
                WsT = al(SB("WsT", [128, 4, 128], BF16))
                bs = al(SB("bs", [128, 4], F32))
                lng = al(SB("lng", [128, D], F32))
                lnb = al(SB("lnb", [128, D], F32))
                u = [al(SB(f"u{j}", [128, D], F32)) for j in range(2)]
                vr = [al(SB(f"vr{j}", [128, D], F32)) for j in range(2)]
                vn = [al(SB(f"vn{j}", [128, D], F32)) for j in range(2)]
                vb = [al(SB(f"vb{j}", [128, D], BF16)) for j in range(2)]
                m = [al(SB(f"m{j}", [128, D], BF16)) for j in range(2)]
                mT = al(SB("mT", [128, 8, 128], BF16))
                bst = al(SB("bst", [128, 12], F32))
                mv = al(SB("mv", [128, 2], F32))
                lrs = al(SB("lrs", [128, 1], F32))
                lnm = al(SB("lnm", [128, 1], F32))
                pz = Banks([(f"pz{i}", al(PS(f"pz{i}", [128, 512], F32))) for i in range(3)])
                psV = al(PS("psV", [128, D], F32))

                P.dma("pool", "w", [(binb[:], sg_b_in[l2:l2 + 1, :])], writes=["binb"])
                wkeys.setdefault("w", []).append("binb")
                load_w(Wi, sg_w_in[l2], 8, 2048, "Wi", "w")
                load_w(Wo, sg_w_out[l2], 8, D, "Wo", "w2")
                wfence()
                P.dma("sp", "sgm", [(wsn[:], sg_w_s[l2].rearrange("g t s -> t g s")),
                                    (lng[:], sg_ln_g[l2:l2 + 1, :].partition_broadcast(128)),
                                    (lnb[:], sg_ln_b[l2:l2 + 1, :].partition_broadcast(128))]
                      + [(bs[:, g:g + 1], sg_b_s[l2, g, :].rearrange("(p o) -> p o", o=1)) for g in range(4)],
                      writes=["wsn", "lng", "lnb", "bs"])
                pk, pb = pz.next()
                P.tr([(pb[:, g * 128:(g + 1) * 128], wsn[:, g, :], identf[:]) for g in range(4)],
                     ["wsn", "identf"], [pk])
                P.copy("act", WsT[:].rearrange("p g t -> p (g t)"), pb[:], [pk], ["WsT"])

                def mainA(t):
                    for j in range(2):
                        for q in range(4):
                            pk, pb = pz.next()
                            prs = [(C.hT[:, kc, j * 128:(j + 1) * 128], Wi[:, kc, q * 512:(q + 1) * 512]) for kc in range(8)]
                            prs.append((onesb[0:1, :], binb[0:1, q * 512:(q + 1) * 512]))
                            P.mm(pb[:], prs, [C.hTk, "Wi", "binb", "cb"], [pk])
                            if q < 2:
                                P.act(u[j][:, q * 512:(q + 1) * 512], pb[:], AF.Gelu_apprx_tanh, [pk], [f"u{j}"])
                            else:
                                P.act(vr[j][:, (q - 2) * 512:(q - 1) * 512], pb[:], AF.Gelu_apprx_tanh, [pk], [f"vr{j}"])

                def mainB(t):
                    for j in range(2):
                        def bn(eng, j=j):
                            eng.bn_stats(out=bst[:, 0:6], in_=vr[j][:, 0:512])
                            return eng.bn_stats(out=bst[:, 6:12], in_=vr[j][:, 512:1024])
                        P.op("dve", bn, [f"vr{j}"], ["bst"])

                        def bna(eng):
                            return eng.bn_aggr(out=mv[:], in_=bst[:])
                        P.op("dve", bna, ["bst"], ["mv"])
                        rsqrt(lrs[:], mv[:, 1:2], 1, 1.0, ["mv"], "lrs")
                        P.stt(lnm[:], mv[:, 0:1], -1.0, lrs[:], ALU.mult, ALU.mult, ["mv", "lrs"], ["lnm"])
                        P.act(vn[j][:], vr[j][:], AF.Identity, [f"vr{j}", "lrs", "lnm"], [f"vn{j}"], scale=lrs[:, 0:1], bias=lnm[:, 0:1])
                        P.tt("dve", vn[j][:], vn[j][:], lng[:], ALU.mult, [f"vn{j}", "lng"], [f"vn{j}"])
                        P.tt("pool", vb[j][:], vn[j][:], lnb[:], ALU.add, [f"vn{j}", "lnb"], [f"vb{j}"])
                        P.mmg([(psV[:, g * 256:(g + 1) * 256], [(WsT[:, g, :], vb[j][:, g * 256:(g + 1) * 256])]) for g in range(4)],
                              ["WsT", f"vb{j}"], ["psV"])
                        for g in range(4):
                            P.stt(m[j][:, g * 256:(g + 1) * 256], psV[:, g * 256:(g + 1) * 256], bs[:, g:g + 1],
                                  u[j][:, g * 256:(g + 1) * 256], ALU.add, ALU.mult, ["psV", "bs", f"u{j}"], [f"m{j}"])
                        P.tr([(C.psT[:, kc, :], m[j][:, kc * 128:(kc + 1) * 128], identb[:]) for kc in range(8)],
                             [f"m{j}", "cb"], ["psT"])
                        P.copy("act", mT[:], C.psT[:], ["psT"], ["mT"])
                        P.mmg([(C.psY[0][:, f * 512:(f + 1) * 512],
                                [(mT[:, kc, :], Wo[:, kc, f * 512:(f + 1) * 512]) for kc in range(8)])
                               for f in range(2)], ["mT", "Wo"], ["psY0"])
                        post(C, t, j, xdst)

                run_tiles(C, layer, 0, list(range(NT)), xsrc, xdst, mainA, mainB)
                P.flush()

        def gla_phase(layer, l2, xsrc, xdst):
            with contextlib.ExitStack() as st:
                al = st.enter_context
                C = Ctx()
                alloc_common(al, C, nY=0, inplace=(os.environ.get("MK_GI", "0") == "1"), nxa=1)
                Wi = al(SB("Wi", [128, 8, 3072], BF16))
                Wo = al(SB("Wo", [128, 8, D], BF16))
                wstage = C.t1
                gh = al(SB("gh", [128, 2], F32))
                Wg1 = al(SB("Wg1", [128, 8, 32], BF16))
                Wg2 = al(SB("Wg2", [33, D], BF16))
                g1Ts = [al(SB(f"g1T{i}", [33, TT], BF16)) for i in range(2)]
                etmp = [al(SB(f"etmp{i}", [128, 512], F32)) for i in range(2)]
                Pm = [al(SB(f"Pm{j}", [128, D], BF16)) for j in range(4)]
                qTs = [al(SB(f"qT{i}", [128, 4, TT], F32)) for i in range(2)]
                kTs = [al(SB(f"kT{i}", [128, 4, TT], F32)) for i in range(2)]
                ktok = [al(SB(f"ktok{j}", [128, 512], F32)) for j in range(2)]
                vtok = [al(SB(f"vtok{j}", [128, D], BF16)) for j in range(4)]
                sr = [al(SB(f"sr{j}", [128, D], BF16)) for j in range(4)]
                decs = al(SB("decs", [128, 2, 2, 4], F32))
                qd = [[al(SB(f"qd{d}{j}", [128, 4, 128], BF16)) for j in range(2)] for d in range(2)]
                kd = [al(SB(f"kd{d}", [128, 4, 128], BF16)) for d in range(2)]
                kend = [[al(SB(f"kend{d}{j}", [128, 512], BF16)) for j in range(2)] for d in range(2)]
                scm = [[al(SB(f"scm{d}{j}", [128, 4, 128], BF16)) for j in range(2)] for d in range(2)]
                Sf = al(SB("Sf", [128, D], F32))
                Sb = al(SB("Sb", [128, D], F32))
                Sfb = [al(SB(f"Sfb{j}", [128, D], BF16)) for j in range(2)]
                Sbb = [al(SB(f"Sbb{j}", [128, D], BF16)) for j in range(2)]
                dec = al(SB("dec", [128, 4], F32))
                ssqo = al(SB("ssqo", [128, 4], F32))
                rstdo = al(SB("rstdo", [128, 4], F32))
                mm_ = al(SB("mm_", [128, D], BF16))
                mT = al(SB("mT", [128, 8, 128], BF16))
                pgbanks = [(f"pg{i}", al(PS(f"pg{i}", [128, 512], F32))) for i in range(7)]
                pools = {1: (Banks(pgbanks[:5]), Banks(pgbanks[5:])), 2: (Banks(pgbanks[:2]), Banks(pgbanks[2:]))}
                pgA, pgB = pools[1]
                pgsel = [pgA]

                class _PG:
                    def next(self):
                        return pgsel[0].next()
                pg = _PG()

                SILU_MODE = os.environ.get("MK_SILU", "tanh")
                P.dma("pool", "w", [(Wg1[:, :, e * 16:(e + 1) * 16], gla_w_gk1[l2, e].rearrange("(kc p) r -> p kc r", p=128))
                                    for e in range(2)], writes=["Wg1"])
                P.memset("pool", Wg2[:], 0.0, ["Wg2"])
                P.dma("pool", "w", [(Wg2[0:16, 0:512], gla_w_gk2[l2, 0]), (Wg2[16:32, 512:1024], gla_w_gk2[l2, 1]),
                                    (Wg2[32:33, :], gla_b_gk[l2:l2 + 1].rearrange("o e k -> o (e k)"))],
                      reads=["Wg2"], writes=["Wg2"])
                wkeys.setdefault("w", []).extend(["Wg1", "Wg2"])
                load_w(Wi, gla_w_in[l2], 8, 3072, "Wia", "w2", [(0, 2048)])
                load_w(Wi, gla_w_in[l2], 8, 3072, "Wib", "w3", [(2048, 3072)])
                wfence()
                P.dma("sp", "gh", [(gh[:, b:b + 1], gla_g_head[l2, b * 128:(b + 1) * 128].rearrange("(p o) -> p o", o=1))
                                   for b in range(2)], writes=["gh"])
                if SILU_MODE == "tanh":
                    P.ts("dve", gh[:], gh[:], 0.5, None, ALU.mult, ALU.bypass, ["gh"], ["gh"])
                for kc in range(8):
                    ws = wstage[kc % 2]
                    P.dma("sp", f"wst{kc % 2}", [(ws[:], gla_w_out[l2, kc * 128:(kc + 1) * 128, :])], writes=[f"t1_{kc % 2}"])
                    P.act(Wo[:, kc, :], ws[:], AF.Copy, [f"t1_{kc % 2}", "gh"], ["Wo"], scale=gh[:, (kc % 2):(kc % 2) + 1])
                for i in range(2):
                    P.memset("pool", g1Ts[i][:], 1.0, [f"g1T{i}"])

                def VJ(t, j):
                    return j + 2 * (t % 2)

                def gates(t, ndir):
                    pk, pb = pg.next()
                    g1T = g1Ts[t % 2]
                    gk = f"g1T{t % 2}"
                    P.mm(pb[0:32, 0:TT], [(Wg1[:, kc, :], C.hT[:, kc, :]) for kc in range(8)], ["Wg1", C.hTk], [pk])
                    P.copy("act", g1T[0:32, :], pb[0:32, 0:TT], [pk], [gk])
                    for j in range(2):
                        for d in range(ndir):
                            pk, pb = pg.next()
                            P.mm(pb[:], [(g1T[0:33, j * 128:(j + 1) * 128], Wg2[0:33, d * 512:(d + 1) * 512])],
                                 [gk, "Wg2"], [pk])
                            P.act(pb[:], pb[:], AF.Exp, [pk], [pk], scale=-1.0)
                            P.act(Pm[VJ(t, j)][:, d * 512:(d + 1) * 512], pb[:], AF.Ln, [pk], [f"Pm{VJ(t, j)}"], bias=1.0)

                eti = [0]

                def proj_tok(t, j, c0, ncol, dst, dkey, func=AF.Copy, scale=1.0):
                    for n in range(ncol // 512):
                        pk, pb = pg.next()
                        P.mm(pb[:], [(C.hT[:, kc, j * 128:(j + 1) * 128], Wi[:, kc, c0 + n * 512:c0 + (n + 1) * 512]) for kc in range(8)],
                             [C.hTk, "Wia" if c0 < 2048 else "Wib"], [pk])
                        if func == AF.Silu:
                            eti[0] += 1
                            et = etmp[eti[0] % 2]
                            ek = f"etmp{eti[0] % 2}"
                            if SILU_MODE == "tanh":
                                P.act(et[:], pb[:], AF.Tanh, [pk], [ek], scale=0.5)
                                P.stt(dst[:, n * 512:(n + 1) * 512], et[:], 1.0, pb[:], ALU.add, ALU.mult, [pk, ek], [dkey])
                                continue
                            if SILU_MODE == "act":
                                P.act(et[:], pb[:], AF.Exp, [pk], [ek], scale=-1.0)
                                P.act(et[:], et[:], AF.Ln, [ek], [ek], bias=1.0)
                                P.act(et[:], et[:], AF.Exp, [ek], [ek], scale=-1.0)
                            else:
                                P.act(et[:], pb[:], AF.Exp, [pk], [ek], scale=-1.0)
                                P.ts("dve", et[:], et[:], 1.0, None, ALU.add, ALU.bypass, [ek], [ek])
                                P.recip(et[:], et[:], [ek], [ek])
                            P.tt("dve", dst[:, n * 512:(n + 1) * 512], pb[:], et[:], ALU.mult, [pk, ek], [dkey])
                        else:
                            P.act(dst[:, n * 512:(n + 1) * 512], pb[:], func, [pk], [dkey], scale=scale)

                def kend_dir(d, j, U, t):
                    pk, pb = pg.next()
                    P.mm(pb[:], [(U[:], Pm[VJ(t, j)][:, d * 512:(d + 1) * 512])], ["cb", f"Pm{VJ(t, j)}"], [pk])
                    P.act(pb[:], pb[:], AF.Exp, [pk], [pk], scale=-1.0 / 16)
                    P.tt("dve", kend[d][j][:], ktok[j][:], pb[:], ALU.mult, [f"ktok{j}", pk], [f"kend{d}{j}"])

                def state_update(S, skey, kd_, kkey, j, decap, deckeys, out_bf=None, okey=None):
                    for hh in range(2):
                        pk, pb = pg.next()
                        P.mmg([(pb[:, i * 256:(i + 1) * 256],
                                [(kd_[:, (2 * hh + i) * 128:(2 * hh + i + 1) * 128], vtok[j][:, (2 * hh + i) * 256:(2 * hh + i + 1) * 256])])
                               for i in range(2)], [kkey, f"vtok{j}"], [pk])
                        for i in range(2):
                            hd = 2 * hh + i
                            dst = S if out_bf is None else out_bf
                            dk = skey if out_bf is None else okey
                            P.stt(dst[:, hd * 256:(hd + 1) * 256], S[:, hd * 256:(hd + 1) * 256], decap(hd),
                                  pb[:, i * 256:(i + 1) * 256], ALU.mult, ALU.add, [skey, pk] + deckeys, [dk])

                if phase_enabled():
                    P.memset("dve", Sf[:], 0.0, ["Sf"])

                    def p1A(t):
                        pgsel[0] = pools[1][0]
                        if t % TPG == 0:
                            P.ts("dve", Sf[:], Sf[:], mk[:, t:t + 1], None, ALU.mult, ALU.bypass, ["Sf", "mk"], ["Sf"])
                        P.dma("sp", "sf", [(sfscr[t], Sf[:])], reads=["Sf"], writes=[("sfs", t)])
                        gates(t, 2)
                        for j in range(2):
                            vj = VJ(t, j)
                            proj_tok(t, j, 512, 512, ktok[j], f"ktok{j}")
                            proj_tok(t, j, 1024, 1024, vtok[vj], f"vtok{vj}")
                            P.dma("sp", f"k{j}", [(kscr[t, j], ktok[j][:])], reads=[f"ktok{j}"], writes=[("ks", t, j)])
                            P.dma("sp", f"v{vj}", [(vscr[t, j], vtok[vj][:])], reads=[f"vtok{vj}"], writes=[("vs", t, j)])
                            P.dma("sp", f"p{vj}", [(pscr[t, j], Pm[vj][:])], reads=[f"Pm{vj}"], writes=[("pss", t, j)])

                    def p1B(t):
                        pgsel[0] = pools[1][1]
                        for j in range(2):
                            kend_dir(0, j, Uf, t)
                            pk, pb = pg.next()
                            P.mmg([(pb[:, 2 * hd:2 * hd + 2], [(Pm[VJ(t, j)][:, hd * 128:(hd + 1) * 128], onesb[:, 0:2])]) for hd in range(4)],
                                  [f"Pm{VJ(t, j)}", "cb"], [pk])
                            P.act(dec[:], pb[:, 0:8].rearrange("p (h two) -> p h two", two=2)[:, :, 0], AF.Exp, [pk], ["dec"], scale=-1.0 / 16)
                            state_update(Sf, "Sf", kend[0][j], f"kend0{j}", VJ(t, j), lambda hd: dec[:, hd:hd + 1], ["dec"])

                    run_tiles(C, layer, 0, list(range(NT)), xsrc, xdst, p1A, p1B, with_post=False)
                    P.flush()

                if phase_enabled():
                    P.memset("dve", Sb[:], 0.0, ["Sb"])
                    P.memset("pool", Sbb[0][:], 0.0, ["Sbb0"])
                    sbi = [0]

                    def p2A(t):
                        pgsel[0] = pools[2][0]
                        for j in range(2):
                            vj = VJ(t, j)
                            P.dma("sp", f"v{vj}", [(vtok[vj][:], vscr[t, j])], reads=[("vs", t, j)], writes=[f"vtok{vj}"])
                            P.dma("sp", f"p{vj}", [(Pm[vj][:], pscr[t, j])], reads=[("pss", t, j)], writes=[f"Pm{vj}"])
                            P.dma("sp", f"k{j}", [(ktok[j][:], kscr[t, j])], reads=[("ks", t, j)], writes=[f"ktok{j}"])
                        for j in range(2):
                            proj_tok(t, j, 2048, 1024, sr[VJ(t, j)], f"sr{VJ(t, j)}", func=AF.Silu)
                        for dst, dkey, c0, sc in ((qTs[t % 2], f"qT{t % 2}", 0, 128.0 ** -0.5), (kTs[t % 2], f"kT{t % 2}", 512, 1.0)):
                            for hh in range(2):
                                pk, pb = pg.next()
                                P.mmg([(pb[:, i * TT:(i + 1) * TT],
                                        [(Wi[:, kc, c0 + (2 * hh + i) * 128:c0 + (2 * hh + i + 1) * 128], C.hT[:, kc, :]) for kc in range(8)])
                                       for i in range(2)], ["Wia", C.hTk], [pk])
                                P.act(dst[:, 2 * hh:2 * hh + 2, :], pb[:].rearrange("p (i c) -> p i c", i=2), AF.Copy, [pk], [dkey], scale=sc)

                    def p2B(t):
                        pgsel[0] = pools[2][1]
                        P.dma("sp", "sfl", [(Sf[:], sfscr[t])], reads=[("sfs", t)], writes=["Sf"])
                        if t % TPG == TPG - 1:
                            P.ts("dve", Sb[:], Sb[:], mk[:, NT + t:NT + t + 1], None, ALU.mult, ALU.bypass, ["Sb", "mk"], ["Sb"])
                            sbi[0] += 1
                            P.copy(os.environ.get("MK_SBB", "act"), Sbb[sbi[0] % 2][:], Sb[:], ["Sb"], [f"Sbb{sbi[0] % 2}"])
                        P.copy("act", Sfb[0][:], Sf[:], ["Sf"], ["Sfb0"])
                        for j in range(2):
                            for d, (Tm, Um, Mm) in enumerate(((Tf, Uf, Mf), (Tb, Ub, Mb))):
                                pk, pb = pg.next()
                                P.mmg([(pb[:, hd * 128:(hd + 1) * 128], [(Pm[VJ(t, j)][:, d * 512 + hd * 128:d * 512 + (hd + 1) * 128], Tm[:])])
                                       for hd in range(4)], [f"Pm{VJ(t, j)}", "cb"], [pk])
                                pbv = pb[:].rearrange("p (h c) -> p h c", h=4)
                                pk2, pb2 = pg.next()
                                pbv2 = pb2[:].rearrange("p (h c) -> p h c", h=4)
                                P.act(pbv2, pbv, AF.Exp, [pk], [pk2], scale=1.0 / 16)
                                P.act(pbv, pbv, AF.Exp, [pk], [pk], scale=-1.0 / 16)
                                col = 127 if d == 0 else 0
                                P.act(decs[:, d, j, :], pbv[:, :, col], AF.Copy, [pk], [f"decs{d}{j}"])
                                P.tt("dve", qd[d][j][:], qTs[t % 2][:, :, j * 128:(j + 1) * 128], pbv, ALU.mult, [f"qT{t % 2}", pk], [f"qd{d}{j}"])
                                P.tt("dve", kd[d][:], kTs[t % 2][:, :, j * 128:(j + 1) * 128], pbv2, ALU.mult, [f"kT{t % 2}", pk2], [f"kd{d}"])
                                kend_dir(d, j, Um, t)
                                pk, pb = pg.next()
                                P.mmg([(pb[:, hd * 128:(hd + 1) * 128], [(kd[d][:, hd, :], qd[d][j][:, hd, :])]) for hd in range(4)],
                                      [f"kd{d}", f"qd{d}{j}"], [pk])
                                P.tt("dve", scm[d][j][:], pb[:].rearrange("p (h c) -> p h c", h=4), Mm[:], ALU.mult, [pk, "Mf", "Mb"], [f"scm{d}{j}"])
                        state_update(Sf, "Sf", kend[0][0], "kend00", VJ(t, 0), lambda hd: decs[:, 0, 0, hd:hd + 1], ["decs00"],
                                     out_bf=Sfb[1], okey="Sfb1")
                        for j in (1, 0):
                            cur = Sbb[sbi[0] % 2]
                            ck = f"Sbb{sbi[0] % 2}"
                            vv = vtok[VJ(t, j)]
                            obanks = []
                            for hh in range(2):
                                pk, pb = pg.next()
                                obanks.append((pk, pb))
                                P.mmg([(pb[:, i * 256:(i + 1) * 256],
                                        [(scm[0][j][:, 2 * hh + i, :], vv[:, (2 * hh + i) * 256:(2 * hh + i + 1) * 256]),
                                         (scm[1][j][:, 2 * hh + i, :], vv[:, (2 * hh + i) * 256:(2 * hh + i + 1) * 256]),
                                         (qd[0][j][:, 2 * hh + i, :], Sfb[j][:, (2 * hh + i) * 256:(2 * hh + i + 1) * 256]),
                                         (qd[1][j][:, 2 * hh + i, :], cur[:, (2 * hh + i) * 256:(2 * hh + i + 1) * 256])]) for i in range(2)],
                                      [f"scm0{j}", f"scm1{j}", f"vtok{VJ(t, j)}", f"qd0{j}", f"qd1{j}", f"Sfb{j}", ck], [pk])
                            state_update(Sb, "Sb", kend[1][j], f"kend1{j}", VJ(t, j), lambda hd, j=j: decs[:, 1, j, hd:hd + 1], [f"decs1{j}"])
                            sbi[0] += 1
                            P.copy(os.environ.get("MK_SBB", "act"), Sbb[sbi[0] % 2][:], Sb[:], ["Sb"], [f"Sbb{sbi[0] % 2}"])
                            for hd in range(4):
                                pk, pb = obanks[hd // 2]
                                P.act(C.junk[:, 0:256], pb[:, (hd % 2) * 256:(hd % 2 + 1) * 256], AF.Square, [pk], ["ssqo"], accum=ssqo[:, hd:hd + 1])
                            rsqrt(rstdo[:], ssqo[:], 4, 1.0 / 256, ["ssqo"], "rstdo")
                            for hd in range(4):
                                pk, pb = obanks[hd // 2]
                                P.stt(mm_[:, hd * 256:(hd + 1) * 256], pb[:, (hd % 2) * 256:(hd % 2 + 1) * 256], rstdo[:, hd:hd + 1],
                                      sr[VJ(t, j)][:, hd * 256:(hd + 1) * 256], ALU.mult, ALU.mult, [pk, "rstdo", f"sr{VJ(t, j)}"], ["mm_"])
                            P.tr([(C.psT[:, kc, :], mm_[:, kc * 128:(kc + 1) * 128], identb[:]) for kc in range(8)],
                                 ["mm_", "cb"], ["psT"])
                            P.copy("act", mT[:], C.psT[:], ["psT"], ["mT"])
                            pieces = []
                            for f in range(2):
                                pk, pb = pg.next()
                                P.mm(pb[:], [(mT[:, kc, :], Wo[:, kc, f * 512:(f + 1) * 512]) for kc in range(8)], ["mT", "Wo"], [pk])
                                pieces.append((pb[:], pk, f * 512, 512))
                            post(C, t, j, xdst, pieces)

                    run_tiles(C, layer, 0, list(range(NT - 1, -1, -1)), xsrc, xdst, p2A, p2B)
                    P.flush()

        cur = x_in
        for layer in range(DEPTH_RUN):
            last = layer == DEPTH_RUN - 1
            if layer % 2 == 0:
                gla_phase(layer, layer // 2, cur, xscr)
            else:
                if phase_enabled():
                    sgu_phase(layer, layer // 2, cur, xscr)
            cur = xscr
            if phase_enabled():
                ffn_phase(layer, cur, y_out if last else xscr)
    return nc


_NC_CACHE = {}


def _consts():
    s = np.arange(128)[:, None]
    c = np.arange(128)[None, :]
    ident = np.eye(128, dtype=np.float32)
    tf = (s <= c).astype(np.float32)
    tb = (s >= c).astype(np.float32)
    uf = (s > c).astype(np.float32)
    ub = (s < c).astype(np.float32)
    ones = np.ones((128, 128), np.float32)
    mf = np.tile(tf, (1, 4))
    mb = np.tile(uf, (1, 4))
    return np.ascontiguousarray(np.concatenate([ident, tf, tb, uf, ub, ones, mf, mb], axis=1))


def kernel(x_prompt, x_sample, c_prompt, c_sample, norm_g, w_ada, b_ada,
           gla_w_in, gla_w_gk1, gla_w_gk2, gla_b_gk, gla_g_head, gla_w_out,
           sg_w_in, sg_b_in, sg_ln_g, sg_ln_b, sg_w_s, sg_b_s, sg_w_out,
           ffn_w_in, ffn_w_out):
    f = lambda a: np.ascontiguousarray(np.asarray(a, dtype=np.float32))
    x_prompt, x_sample, c_prompt, c_sample = f(x_prompt), f(x_sample), f(c_prompt), f(c_sample)
    shared = dict(norm_g=f(norm_g), w_ada=f(w_ada), b_ada=f(b_ada), gla_w_in=f(gla_w_in),
                  gla_w_gk1=f(gla_w_gk1), gla_w_gk2=f(gla_w_gk2), gla_b_gk=f(gla_b_gk),
                  gla_g_head=f(gla_g_head), gla_w_out=f(gla_w_out), sg_w_in=f(sg_w_in),
                  sg_b_in=f(sg_b_in), sg_ln_g=f(sg_ln_g), sg_ln_b=f(sg_ln_b), sg_w_s=f(sg_w_s),
                  sg_b_s=f(sg_b_s), sg_w_out=f(sg_w_out), ffn_w_in=f(ffn_w_in), ffn_w_out=f(ffn_w_out),
                  cst=_consts())
    plan = []
    for r in range(4):
        plan.append([("s", r), ("p", 2 * r), ("p", 2 * r + 1)])
    for r in range(4, 8):
        plan.append([("p", 8 + (r - 4) * 6 + k) for k in range(6)])
    in_maps = []
    for r in range(NCORES):
        xs, cgs = [], []
        mkf = np.ones(NT, np.float32)
        mkb = np.ones(NT, np.float32)
        tpos = 0
        for kind, idx in plan[r]:
            if kind == "s":
                xs.append(x_sample[idx]); ng = 4; cv = c_sample[idx]
            else:
                xs.append(x_prompt[idx]); ng = 1; cv = c_prompt[idx]
            for _ in range(ng):
                cgs.append(cv)
            mkf[tpos] = 0.0
            tpos += ng * TPG
            mkb[tpos - 1] = 0.0
        xc = np.ascontiguousarray(np.concatenate(xs, axis=0))
        cg = np.ascontiguousarray(np.stack(cgs, axis=0))
        mk = np.ascontiguousarray(np.tile(np.concatenate([mkf, mkb])[None, :], (128, 1)).astype(np.float32))
        m = dict(shared)
        m.update(x=xc, cg=cg, mk=mk)
        in_maps.append(m)
    if "nc" not in _NC_CACHE:
        _NC_CACHE["nc"] = build_program()
    nc = _NC_CACHE["nc"]
    res = run_bass_kernel_spmd(nc, in_maps, core_ids=list(range(NCORES)))
    y_prompt = np.empty_like(x_prompt)
    y_sample = np.empty_like(x_sample)
    for r in range(NCORES):
        y = np.asarray(res.results[r]["y"], dtype=np.float32)
        pos = 0
        for kind, idx in plan[r]:
            if kind == "s":
                y_sample[idx] = y[pos:pos + 8192]; pos += 8192
            else:
                y_prompt[idx] = y[pos:pos + 2048]; pos += 2048
    return (y_prompt, y_sample)
```
